# Optimizing a Trainium2 kernel written in Bass

```python
import jax, jax.numpy as jnp
from jax import lax
import numpy as np

D_MODEL = 1024
BATCH = 8
SEQ = 2048
DEPTH = 2
DEC_BATCH = 128
DEC_SEQ = 8
PAST_LEN = 16384
PAGE_SIZE = 128

N_META = 16
D_CONV = D_MODEL
D_POOL = D_MODEL
CONV_WIDTH = 31
POOL_WINDOWS = (2, 4, 8, 16)
N_POOL_GROUPS = len(POOL_WINDOWS)
POOL_GROUP = D_POOL // N_POOL_GROUPS
POOL_MAX = max(POOL_WINDOWS)
CONV_BUF = CONV_WIDTH - 1
POOL_BUF = POOL_MAX - 1
RMS_EPS = 1e-6
LN_EPS = 1e-5
SPLITS = (D_CONV, D_CONV, D_CONV, D_POOL, D_POOL, D_MODEL, D_MODEL)
D_IN = sum(SPLITS)

kernel_name = "gated_conformer_conv_multiscale_pool_decoder_step"


def rmsnorm(x, g):
    xf = x.astype(jnp.float32)
    y = xf * lax.rsqrt(jnp.mean(xf * xf, axis=-1, keepdims=True) + RMS_EPS)
    return (y * g.astype(jnp.float32)).astype(x.dtype)


def layernorm(x, g, b):
    xf = x.astype(jnp.float32)
    mu = jnp.mean(xf, axis=-1, keepdims=True)
    var = jnp.mean(jnp.square(xf - mu), axis=-1, keepdims=True)
    y = (xf - mu) * lax.rsqrt(var + LN_EPS)
    return (y * g.astype(jnp.float32) + b.astype(jnp.float32)).astype(x.dtype)


def causal_depthwise_conv(u, buf, w, b):
    ext = jnp.concatenate([buf.astype(u.dtype), u], axis=1)
    y = lax.conv_general_dilated(ext, w[:, None, :].astype(u.dtype), window_strides=(1,), padding='VALID',
                                 dimension_numbers=('NWC', 'WIO', 'NWC'), feature_group_count=u.shape[-1])
    return y + b.astype(u.dtype), ext[:, -CONV_BUF:]


def multiscale_pool(u, buf, start_pos):
    T = u.shape[1]
    ext = jnp.concatenate([buf.astype(u.dtype), u], axis=1)
    cs = jnp.cumsum(ext.astype(jnp.float32), axis=1)
    cs = jnp.pad(cs, ((0, 0), (1, 0), (0, 0)))
    hi = cs[:, POOL_BUF + 1:POOL_BUF + 1 + T]
    pos = start_pos + jnp.arange(T)
    outs = []
    for g, w in enumerate(POOL_WINDOWS):
        sl = slice(g * POOL_GROUP, (g + 1) * POOL_GROUP)
        lo = cs[:, POOL_BUF + 1 - w:POOL_BUF + 1 - w + T, sl]
        cnt = jnp.minimum(w, pos + 1).astype(jnp.float32)[None, :, None]
        outs.append((hi[..., sl] - lo) / cnt)
    mean = jnp.concatenate(outs, axis=-1)
    return (mean - u.astype(jnp.float32)).astype(u.dtype), ext[:, -POOL_BUF:]


def mixer_layer(x, buf_conv, buf_pool, start_pos, norm_g, w_in, conv_w, conv_b, ln_g, ln_b,
                w_conv_out, w_pool_mix, pool_scale, w_pool_out, w_out):
    h = rmsnorm(x, norm_g)
    z = jnp.einsum('btd,de->bte', h, w_in)
    idx = np.cumsum(SPLITS)[:-1].tolist()
    a_val, a_gate, a_silu, p_in, p_silu, g_a, g_b = jnp.split(z, idx, axis=-1)
    u = a_val * jax.nn.sigmoid(a_gate)
    c, new_conv = causal_depthwise_conv(u, buf_conv, conv_w, conv_b)
    c = jax.nn.silu(layernorm(c, ln_g, ln_b)) * jax.nn.silu(a_silu)
    br_a = jnp.einsum('btc,cd->btd', c, w_conv_out)
    pooled, new_pool = multiscale_pool(p_in, buf_pool, start_pos)
    B_, T_ = pooled.shape[:2]
    pg = pooled.reshape(B_, T_, N_POOL_GROUPS, POOL_GROUP)
    q = jnp.einsum('btgc,gce->btge', pg, w_pool_mix).reshape(B_, T_, D_POOL)
    q = q * pool_scale * jax.nn.silu(p_silu)
    br_b = jnp.einsum('btc,cd->btd', q, w_pool_out)
    merged = jax.nn.sigmoid(g_a) * br_a + jax.nn.sigmoid(g_b) * br_b
    return x + jnp.einsum('btd,de->bte', merged, w_out), new_conv, new_pool


def run_trunk(x, bufs_conv, bufs_pool, start_pos, norm_g, w_in, conv_w, conv_b, ln_g, ln_b,
              w_conv_out, w_pool_mix, pool_scale, w_pool_out, w_out, final_g):
    new_c, new_p = [], []
    for l in range(DEPTH):
        x, nc, npool = mixer_layer(x, bufs_conv[l], bufs_pool[l], start_pos, norm_g[l], w_in[l], conv_w[l],
                                   conv_b[l], ln_g[l], ln_b[l], w_conv_out[l], w_pool_mix[l], pool_scale[l],
                                   w_pool_out[l], w_out[l])
        new_c.append(nc)
        new_p.append(npool)
    return rmsnorm(x, final_g), jnp.stack(new_c), jnp.stack(new_p)


def setup_inputs(seed: int = 0) -> dict:
    key = jax.random.key(seed)
    ks = jax.random.split(key, 20)
    f32 = jnp.float32
    nrm = lambda k, s, sc: jax.random.normal(k, s, f32) * sc
    return {
        'x_prompt': nrm(ks[0], (BATCH, SEQ, D_MODEL), 1.0),
        'x_sample': nrm(ks[1], (DEC_BATCH, DEC_SEQ, D_MODEL), 1.0),
        'state_conv': nrm(ks[2], (DEPTH, DEC_BATCH, CONV_BUF, D_CONV), 0.5),
        'state_pool': nrm(ks[3], (DEPTH, DEC_BATCH, POOL_BUF, D_POOL), 1.0),
        'meta_tokens': nrm(ks[4], (N_META, D_MODEL), 1.0),
        'norm_g': 1.0 + nrm(ks[5], (DEPTH, D_MODEL), 0.1),
        'w_in': nrm(ks[6], (DEPTH, D_MODEL, D_IN), D_MODEL ** -0.5),
        'conv_w': nrm(ks[7], (DEPTH, CONV_WIDTH, D_CONV), CONV_WIDTH ** -0.5),
        'conv_b': nrm(ks[8], (DEPTH, D_CONV), 0.01),
        'ln_g': 1.0 + nrm(ks[9], (DEPTH, D_CONV), 0.1),
        'ln_b': nrm(ks[10], (DEPTH, D_CONV), 0.01),
        'w_conv_out': nrm(ks[11], (DEPTH, D_CONV, D_MODEL), D_CONV ** -0.5),
        'w_pool_mix': nrm(ks[12], (DEPTH, N_POOL_GROUPS, POOL_GROUP, POOL_GROUP), POOL_GROUP ** -0.5),
        'pool_scale': 1.0 + nrm(ks[13], (DEPTH, D_POOL), 0.1),
        'w_pool_out': nrm(ks[14], (DEPTH, D_POOL, D_MODEL), D_POOL ** -0.5),
        'w_out': nrm(ks[15], (DEPTH, D_MODEL, D_MODEL), D_MODEL ** -0.5),
        'final_g': 1.0 + nrm(ks[16], (D_MODEL,), 0.1),
    }


def reference(x_prompt, x_sample, state_conv, state_pool, meta_tokens, norm_g, w_in, conv_w, conv_b,
              ln_g, ln_b, w_conv_out, w_pool_mix, pool_scale, w_pool_out, w_out, final_g):
    B = x_prompt.shape[0]
    meta = jnp.broadcast_to(meta_tokens.astype(x_prompt.dtype)[None], (B, N_META, D_MODEL))
    xp = jnp.concatenate([meta, x_prompt], axis=1)
    zc = jnp.zeros((DEPTH, B, CONV_BUF, D_CONV), x_prompt.dtype)
    zp = jnp.zeros((DEPTH, B, POOL_BUF, D_POOL), x_prompt.dtype)
    yp, new_state_conv_prompt, new_state_pool_prompt = run_trunk(
        xp, zc, zp, 0, norm_g, w_in, conv_w, conv_b, ln_g, ln_b, w_conv_out, w_pool_mix,
        pool_scale, w_pool_out, w_out, final_g)
    y_prompt = yp[:, N_META:]
    y_sample, new_state_conv_sample, new_state_pool_sample = run_trunk(
        x_sample, state_conv, state_pool, PAST_LEN, norm_g, w_in, conv_w, conv_b, ln_g, ln_b,
        w_conv_out, w_pool_mix, pool_scale, w_pool_out, w_out, final_g)
    return (y_prompt, y_sample, new_state_conv_prompt, new_state_pool_prompt,
            new_state_conv_sample, new_state_pool_sample)
```

```python
import numpy as np
from contextlib import ExitStack
import concourse.bass as bass
import concourse.mybir as mybir
from concourse.bass_utils import run_bass_kernel_spmd

F32 = mybir.dt.float32
BF16 = mybir.dt.bfloat16
AF = mybir.ActivationFunctionType
ALU = mybir.AluOpType

ENGS = ['pe', 'act', 'dve', 'pool', 'sp']
CWID = 31
CB = 30
PB = 15
WINS = (2, 4, 8, 16)
NMETA = 16
DS = 8
RMS_EPS = 1e-6
LN_EPS = 1e-5
DEPTH = 2


class Cfg:
    def __init__(self, D=1024, SEQ=2048, NS=16, segs=None, ntile=2, BW=256, WR=6, merge_sample=True, wscratch=True):
        self.D = D
        self.KC = D // 128
        self.SEQ = SEQ
        self.PT = SEQ + NMETA
        self.NS = NS
        self.TS = NS * DS
        self.GC = (D // 4) // 128
        assert self.GC >= 1
        self.segs = segs
        self.ntile = ntile
        self.BW = BW
        self.merge_sample = merge_sample
        self.wscratch = wscratch
        self.WR = WR
        self.MB = BW // 128
        self.NB = D // BW
        self.R = 36 * DEPTH + 1


REAL = Cfg(segs=[(0, 736, False), (736, 1472, False), (1472, 2064, True)], WR=8)


class _Stop(Exception):
    pass


_cur = {'si': 0}


def chk(n):
    return


class Prog:
    def __init__(self, sems):
        self.ops = {e: [] for e in ENGS}
        self.cnt = {e: 0 for e in ENGS}
        self.sem = sems
        self.waited = {e: {} for e in ENGS}
        self.dcnt = {}
        self.lastw = {}
        self.readers = {}

    def _waits(self, eng, deps):
        waits = {}
        for d in deps:
            if d is None:
                continue
            sem, val = d
            key = sem.name
            if eng == 'pe' and key == self.sem['pe'].name:
                continue
            if self.waited[eng].get(key, 0) >= val:
                continue
            if key in waits and waits[key][1] >= val:
                continue
            waits[key] = (sem, val)
        for key, (sem, val) in waits.items():
            self.waited[eng][key] = val
        return list(waits.values())

    def _deps(self, reads, writes):
        deps = []
        for r in reads:
            deps.append(self.lastw.get(r))
        for w in writes:
            deps.append(self.lastw.get(w))
            deps.extend(self.readers.get(w, {}).values())
        return deps

    def _mark(self, tok, reads, writes):
        key = tok[0].name
        for r in reads:
            d = self.readers.setdefault(r, {})
            if key not in d or d[key][1] < tok[1]:
                d[key] = tok
        for w in writes:
            self.lastw[w] = tok
            self.readers[w] = {}

    def op(self, eng, fn, reads=(), writes=()):
        return self.group(eng, [fn], reads, writes)

    def group(self, eng, fns, reads=(), writes=()):
        waits = self._waits(eng, self._deps(reads, writes))
        self.cnt[eng] += 1
        tok = (self.sem[eng], self.cnt[eng])
        n = len(fns)
        for i, fn in enumerate(fns):
            self.ops[eng].append((waits if i == 0 else [], fn, self.sem[eng] if i == n - 1 else None, 1))
        self._mark(tok, reads, writes)
        return tok

    def chain(self, eng, items, writes=()):
        self.cnt[eng] += 1
        tok = (self.sem[eng], self.cnt[eng])
        n = len(items)
        allreads = []
        for i, (fn, reads) in enumerate(items):
            deps = [self.lastw.get(r) for r in reads]
            if i == 0:
                for w in writes:
                    deps.append(self.lastw.get(w))
                    deps.extend(self.readers.get(w, {}).values())
            waits = self._waits(eng, deps)
            self.ops[eng].append((waits, fn, self.sem[eng] if i == n - 1 else None, 1))
            allreads.extend(reads)
        self._mark(tok, allreads, writes)
        return tok

    def dma(self, eng, out, in_, sem, reads=(), writes=()):
        waits = self._waits(eng, self._deps(reads, writes))
        key = sem.name
        self.dcnt[key] = self.dcnt.get(key, 0) + 16
        tok = (sem, self.dcnt[key])
        self.ops[eng].append((waits, lambda e: e.dma_start(out=out, in_=in_), sem, 16))
        self._mark(tok, reads, writes)
        return tok

    def wait_all(self, eng, toks):
        waits = self._waits(eng, toks)
        self.ops[eng].append((waits, None, None, 0))

    def replay(self, eng, e):
        for waits, fn, sem, inc in self.ops[eng]:
            for (s, v) in waits:
                e.wait_ge(s, v)
            if fn is not None:
                inst = fn(e)
                if sem is not None:
                    inst.then_inc(sem, inc)


def build_program(cfg):
    D, KC, PT, NS, TS, GC = cfg.D, cfg.KC, cfg.PT, cfg.NS, cfg.TS, cfg.GC
    BW, MB, NB, R = cfg.BW, cfg.MB, cfg.NB, cfg.R
    PG = D // 4
    nc = bass.Bass("TRN2", target_bir_lowering=False)

    def din(name, shape):
        return nc.dram_tensor(name, shape, F32, kind="ExternalInput").ap()

    def dout(name, shape):
        return nc.dram_tensor(name, shape, F32, kind="ExternalOutput").ap()

    xp = din("xp", [cfg.SEQ, D])
    xs = din("xs", [TS, D])
    sc = din("sc", [DEPTH, NS, CB, D])
    spl = din("spl", [DEPTH, NS, PB, D])
    meta = din("meta", [NMETA, D])
    norm_g = din("norm_g", [DEPTH, D])
    w_in = din("w_in", [DEPTH, 7 * D // BW, 128, KC * BW])
    conv_w = din("conv_w", [DEPTH, CWID, D])
    conv_b = din("conv_b", [DEPTH, D])
    ln_g = din("ln_g", [DEPTH, D])
    ln_b = din("ln_b", [DEPTH, D])
    w_co = din("w_co", [DEPTH, D // BW, 128, KC * BW])
    w_mix = din("w_mix", [DEPTH, 4, 128, GC * PG])
    pscale = din("pscale", [DEPTH, D])
    w_po = din("w_po", [DEPTH, D // BW, 128, KC * BW])
    w_o = din("w_o", [DEPTH, D // BW, 128, KC * BW])
    final_g = din("final_g", [1, D])
    yp = dout("yp", [cfg.SEQ, D])
    ys = dout("ys", [TS, D])
    ncp = dout("ncp", [DEPTH, CB, D])
    npp = dout("npp", [DEPTH, PB, D])
    ncs = dout("ncs", [DEPTH, NS, CB, D])
    nps = dout("nps", [DEPTH, NS, PB, D])

    def seg_tiles(p0, p1, hs):
        Tp_ = p1 - p0
        tl = []
        nt_ = cfg.ntile
        base_ = (Tp_ // nt_) // 2 * 2
        c_ = 0
        for i_ in range(nt_):
            c1_ = Tp_ if i_ == nt_ - 1 else c_ + base_
            tl.append((c_, c1_, c1_ - c_))
            c_ = c1_
        if hs:
            la, lb, lnp = tl[-1]
            if (lb - la) + TS <= 512 and cfg.merge_sample:
                tl[-1] = (la, lb + TS, lnp)
            else:
                tl.append((Tp_, Tp_ + TS, 0))
        return tl

    TW = max(b_ - a_ for sg_ in cfg.segs for (a_, b_, _) in seg_tiles(*sg_))
    TW = (TW + 7) // 8 * 8
    TPmax = max(p1 - p0 for (p0, p1, _) in cfg.segs)
    Tmax = max((p1 - p0) + (TS if hs else 0) for (p0, p1, hs) in cfg.segs)
    NMAX = 512
    EU = CB + TPmax
    EP = PB + TPmax
    EPS_ = max(EP, NS * (PB + DS))

    with ExitStack() as st:
        def sbuf(name, shape, dt):
            return st.enter_context(nc.sbuf_tensor(name, shape, dt))

        def sem(name):
            return st.enter_context(nc.semaphore(name))

        X = sbuf("X", [128, KC, Tmax], F32)
        H = sbuf("H", [128, KC, Tmax], BF16)
        CC = sbuf("CC", [128, KC, Tmax], BF16)
        QF = sbuf("QF", [128, KC, Tmax], BF16)
        MG = sbuf("MG", [128, KC, Tmax], BF16)
        U = [sbuf(f"U{i}", [128, EU], BF16) for i in range(2)]
        UX = sbuf("UX", [128, KC, NS, CB + DS], BF16)
        PX = sbuf("PX", [128, KC, NS, PB + DS], F32)
        PIN = [sbuf(f"PIN{i}", [128, EP], F32) for i in range(2)]
        if KC * Tmax // 2 >= 2 * EPS_:
            MGf = MG[:].rearrange("p k t -> p (k t)").bitcast(F32)
            TA = MGf[:, 0:EPS_]
            TB = MGf[:, EPS_:2 * EPS_]
        else:
            TA = sbuf("TA", [128, EPS_], F32)[:]
            TB = sbuf("TB", [128, EPS_], F32)[:]
        WR = cfg.WR
        W = [sbuf(f"W{i}", [128, KC, BW], BF16) for i in range(WR)]
        DG = [sbuf(f"DG{i}", [128, CWID, 128], BF16) for i in range(3)]
        A1 = sbuf("A1", [128, Tmax], F32)
        A2 = sbuf("A2", [128, Tmax], F32)
        S1 = MU = RR = A1
        S2 = RS = A2
        SG = [sbuf(f"SG{i}", [128, TW], F32) for i in range(4)]
        T1 = [sbuf(f"T1_{i}", [128, TW], F32) for i in range(4)]
        T2 = [sbuf(f"T2_{i}", [128, TW], F32) for i in range(4)]
        STG = [sbuf(f"STG{i}", [128, D], F32) for i in range(2)]
        PV = sbuf("PV", [128, KC, R], F32)
        ID32 = sbuf("ID32", [128, 128], F32)
        IDB = sbuf("IDB", [128, 128], BF16)
        ONESB = sbuf("ONESB", [128, 128], BF16)
        ONES32 = sbuf("ONES32", [128, 128], F32)
        EPSR = sbuf("EPSR", [128, 1], F32)
        EPSL = sbuf("EPSL", [128, 1], F32)
        INV = sbuf("INV", [128, PB], F32)
        UL = sbuf("UL", [128, KC, CB], F32)
        PL15 = sbuf("PL15", [128, KC, PB], F32)
        US = sbuf("US", [128, KC, TS], F32)
        PSN = sbuf("PSN", [128, KC, TS], F32)
        UT = [sbuf(f"UT{l}", [128, KC, CB], BF16) for l in range(DEPTH)]
        PTL = [sbuf(f"PTL{l}", [128, KC, PB], F32) for l in range(DEPTH)]
        PS = [st.enter_context(nc.psum_tensor(f"ps{i}", [128, NMAX], F32)) for i in range(8)]

        sems = {e: sem("s_" + e) for e in ENGS}
        wsem = [sem(f"w{i}") for i in range(WR)]
        stg_ld = [sem(f"stl{i}") for i in range(2)]
        stg_st = [sem(f"sts{i}") for i in range(2)]
        misc_sem = sem("misc")
        BST = [(STG[i][:], ("STG", i), stg_ld[i], stg_st[i]) for i in range(2)]
        if CWID * 128 * 2 >= D * 4:
            for i in range(3):
                dgf = DG[i][:].rearrange("p k c -> p (k c)").bitcast(F32)
                BST.append((dgf[:, 0:D], ("DG", i), sem(f"bld{i}"), sem(f"bst{i}")))
        NBST = len(BST)
        d2d_sem = sem("d2d")
        block = st.enter_context(nc.Block())
        P = Prog(sems)

        psi = [0]

        def bank():
            b = psi[0] % 8
            psi[0] += 1
            return b

        rot = {}

        def nxt(name, n):
            v = rot.get(name, 0)
            rot[name] = v + 1
            return v % n

        def act(out, in_, func, reads, writes, bias=None, scale=None):
            kw = {}
            if bias is not None:
                kw['bias'] = bias
            if scale is not None:
                kw['scale'] = scale
            return P.op('act', lambda e: e.activation(out=out, in_=in_, func=func, **kw), reads, writes)

        def tt(eng, out, in0, in1, op, reads, writes):
            return P.op(eng, lambda e: e.tensor_tensor(out=out, in0=in0, in1=in1, op=op), reads, writes)

        def stt(out, in0, scalar, in1, op0, op1, reads, writes):
            return P.op('dve', lambda e: e.scalar_tensor_tensor(out=out, in0=in0, scalar=scalar, in1=in1, op0=op0, op1=op1), reads, writes)

        def cp(eng, out, in_, reads, writes):
            return P.op(eng, lambda e: e.tensor_copy(out=out, in_=in_), reads, writes)

        def mset(eng, ap, val, writes):
            return P.op(eng, lambda e: e.memset(ap, val), (), writes)

        def mm_group(b, n, pairs, reads, c0=0):
            out = PS[b][:, c0:c0 + n]
            fns = []
            last = len(pairs) - 1
            for i, (l, r) in enumerate(pairs):
                fns.append(lambda e, l=l, r=r, i=i: e.matmul(out, lhsT=l, rhs=r, start=(i == 0), stop=(i == last)))
            return P.group('pe', fns, reads, [("ps", b)])

        def mm_chain(b, n, triples, c0=0):
            out = PS[b][:, c0:c0 + n]
            last = len(triples) - 1
            items = []
            for i, (l_, r_, rd_) in enumerate(triples):
                items.append((lambda e, l_=l_, r_=r_, i=i: e.matmul(out, lhsT=l_, rhs=r_, start=(i == 0), stop=(i == last)), rd_))
            return P.chain('pe', items, [("ps", b)])

        def transposes(b, items, reads):
            fns = []
            for (in_ap, off, r, c) in items:
                fns.append(lambda e, in_ap=in_ap, off=off, r=r, c=c: e.transpose(out=PS[b][0:c, off:off + r], in_=in_ap, identity=ID32[0:r, 0:r]))
            return P.group('pe', fns, reads + ["ID32"], [("ps", b)])

        mset('pool', ID32[:], 0.0, ["ID32"])
        P.op('pool', lambda e: e.affine_select(out=ID32[:], in_=ID32[:], pattern=[[-1, 128]], compare_op=ALU.not_equal,
                                               fill=1.0, base=0, channel_multiplier=1), ["ID32"], ["ID32"])
        cp('dve', IDB[:], ID32[:], ["ID32"], ["IDB"])
        mset('pool', ONESB[:], 1.0 / D, ["ONESB"])
        mset('pool', ONES32[:], 1.0 / D, ["ONES32"])
        mset('pool', EPSR[:], RMS_EPS, ["EPSR"])
        mset('pool', EPSL[:], LN_EPS, ["EPSL"])
        for t in range(PB):
            mset('pool', INV[:, t:t + 1], 1.0 / (t + 1), ["INV"])
        d2d = []
        VEC, vreg, vsem, _ = BST[-1]
        vregs = []

        def vload(dst, src):
            reg = ("VECR", len(vregs))
            vregs.append(reg)
            P.dma('sp', dst, src, vsem, (), [reg])
        for l in range(DEPTH):
            b0 = 36 * l
            vload(VEC[b0:b0 + CWID, :], conv_w[l])
            for j, src in enumerate([conv_b, ln_g, ln_b, norm_g, pscale]):
                vload(VEC[b0 + CWID + j:b0 + CWID + j + 1, :], src[l:l + 1, :])
        vload(VEC[R - 1:R, :], final_g)
        per_bank = max(1, NMAX // R)
        m = 0
        while m < KC:
            b = bank()
            ms = list(range(m, min(KC, m + per_bank)))
            transposes(b, [(VEC[0:R, mm * 128:(mm + 1) * 128], i * R, R, 128) for i, mm in enumerate(ms)], vregs + [vreg])
            for i, mm in enumerate(ms):
                cp('dve', PV[:, mm, :], PS[b][:, i * R:(i + 1) * R], [("ps", b)], ["PV"])
            m += per_bank

        def pv(l, row, mchunk):
            r = 36 * l + row
            return PV[:, mchunk, r:r + 1]

        KH = max(1, KC // 2)
        assert MB % GC == 0 and PG <= BW
        MPB = MB // GC

        def wsrc_in(l, c0):
            v = w_in[l, c0 // BW].rearrange("p (k e) -> p k e", k=KC)
            return lambda slot: [(W[slot][:, k0:k0 + KH, 0:BW], v[:, k0:k0 + KH, :]) for k0 in range(0, KC, KH)]

        def wsrc_sq(wt, l, c0):
            v = wt[l, c0 // BW].rearrange("p (k e) -> p k e", k=KC)
            return lambda slot: [(W[slot][:, k0:k0 + KH, 0:BW], v[:, k0:k0 + KH, :]) for k0 in range(0, KC, KH)]

        def wsrc_mixg(l, g):
            v = w_mix[l, g].rearrange("p (kk e) -> p kk e", kk=GC)
            return lambda slot: [(W[slot][:, 0:GC, 0:PG], v)]

        wlist = []
        for (p0, p1, hs) in cfg.segs:
            for l in range(DEPTH):
                for j in range(NB):
                    wlist.append(wsrc_in(l, 0 * D + j * BW))
                    wlist.append(wsrc_in(l, 1 * D + j * BW))
                    wlist.append(wsrc_in(l, 3 * D + j * BW))
                for j in range(NB):
                    wlist.append(wsrc_in(l, 2 * D + j * BW))
                    wlist.append(wsrc_in(l, 5 * D + j * BW))
                for j in range(NB):
                    wlist.append(wsrc_sq(w_co, l, j * BW))
                for j in range(NB):
                    wlist.append(wsrc_in(l, 4 * D + j * BW))
                    for gg in range(MPB):
                        wlist.append(wsrc_mixg(l, j * MPB + gg))
                for j in range(NB):
                    wlist.append(wsrc_in(l, 6 * D + j * BW))
                for j in range(NB):
                    wlist.append(wsrc_sq(w_po, l, j * BW))
                for j in range(NB):
                    wlist.append(wsrc_sq(w_o, l, j * BW))
        wstate = {'issued': 0, 'used': 0}
        nseg_ = len(cfg.segs)
        NBLK = len(wlist) // nseg_
        use_wsc = cfg.wscratch and nseg_ > 1
        if use_wsc:
            wsc = nc.dram_tensor("wsc", [NBLK, 128, KC * BW], BF16, kind="Internal").ap()
            wst = [sem(f"wst{i}") for i in range(WR)]
        def w_issue():
            i = wstate['issued']
            if i >= len(wlist):
                return
            slot = i % WR
            if use_wsc and i >= NBLK:
                P.dma('pool', W[slot][:].rearrange("p k e -> p (k e)"), wsc[i % NBLK], wsem[slot], [("WSC", i % NBLK)], [("W", slot)])
            else:
                for (dst_, src_) in wlist[i](slot):
                    P.dma('pool', dst_, src_, wsem[slot], (), [("W", slot)])
                if use_wsc:
                    P.dma('sp', wsc[i], W[slot][:].rearrange("p k e -> p (k e)"), wst[slot], [("W", slot)], [("WSC", i)])
            wstate['issued'] = i + 1

        def w_take():
            i = wstate['used']
            wstate['used'] = i + 1
            assert i < wstate['issued']
            return i % WR

        def w_release(n=1):
            for _ in range(n):
                w_issue()

        out_toks = []
        nseg = len(cfg.segs)
        NPF = 2
        bst_state = {'i': 0, 'reserved': set(), 'pre': []}

        def bst_next():
            while True:
                i_ = bst_state['i'] % NBST
                bst_state['i'] += 1
                if i_ not in bst_state['reserved']:
                    return i_

        def seg_blocks(p0_, p1_, hs_):
            out_ = []
            pos_ = p0_
            while pos_ < p1_:
                n_ = min(128, p1_ - pos_)
                srcs_ = []
                q_ = pos_
                while q_ < pos_ + n_:
                    if q_ < NMETA:
                        k_ = min(NMETA, pos_ + n_) - q_
                        srcs_.append((meta[q_:q_ + k_, :], k_))
                    else:
                        k_ = pos_ + n_ - q_
                        srcs_.append((xp[q_ - NMETA:q_ - NMETA + k_, :], k_))
                    q_ += k_
                out_.append((srcs_, n_, pos_ - p0_))
                pos_ += n_
            if hs_:
                for r0_ in range(0, TS, 128):
                    n_ = min(128, TS - r0_)
                    out_.append(([(xs[r0_:r0_ + n_, :], n_)], n_, (p1_ - p0_) + r0_))
            return out_

        def issue_row_dmas(rows_src):
            i_ = bst_next()
            st_ap, st_reg, st_ld, _ = BST[i_]
            r_ = 0
            for (ap_, k_) in rows_src:
                P.dma('sp', st_ap[r_:r_ + k_, :], ap_, st_ld, (), [st_reg])
                r_ += k_
            return i_
        try:
            chk(0)
            _emit_all = True
        except _Stop:
            _emit_all = False
        for si, (p0, p1, hs) in enumerate(cfg.segs if _emit_all else []):
          try:
              _cur['si'] = si
              Tp = p1 - p0
              T = Tp + (TS if hs else 0)
              tiles = seg_tiles(p0, p1, hs)
              nt = cfg.ntile
              for (a, b_, kd) in tiles:
                  assert b_ - a <= NMAX
              first_seg = (p0 == 0)
              last_prompt_seg = (p1 == PT)
              if last_prompt_seg:
                  assert tiles[nt - 1][2] >= CB

              def XR(ti):
                  return ("X", ti)

              def HRL(ti):
                  return [("H", k_, ti) for k_ in range(KC)]

              def load_rows(rows_src, n, c0, pre=None):
                  bi = issue_row_dmas(rows_src) if pre is None else pre
                  st_ap, st_reg, st_ld, _ = BST[bi]
                  bst_state['reserved'].discard(bi)
                  for h in range(0, KC, 4):
                      b = bank()
                      mm_ = list(range(h, min(KC, h + 4)))
                      transposes(b, [(st_ap[0:n, q * 128:(q + 1) * 128], i * 128, n, 128) for i, q in enumerate(mm_)], [st_reg])
                      wr = [("X", ti) for ti, (a, b2, kd) in enumerate(tiles) if not (b2 <= c0 or a >= c0 + n)]
                      act(X[:, h:h + len(mm_), c0:c0 + n], PS[b][:, 0:len(mm_) * 128].rearrange("p (a t) -> p a t", t=128)[:, :, 0:n],
                          AF.Copy, [("ps", b)], wr)

              pre_list = bst_state['pre']
              bst_state['pre'] = []
              for bi_, (srcs, n, c0_) in enumerate(seg_blocks(p0, p1, hs)):
                  load_rows(srcs, n, c0_, pre=(pre_list[bi_] if bi_ < len(pre_list) else None))

              chk(1)
              if si == 0:
                  for _ in range(WR):
                      w_issue()
                  for l_ in range(DEPTH):
                      out_toks.append(P.dma('sp', ncs[l_, :, 0:CB - DS, :], sc[l_, :, DS:CB, :], d2d_sem))
                      out_toks.append(P.dma('sp', nps[l_, :, 0:PB - DS, :], spl[l_, :, DS:PB, :], d2d_sem))

              def rms_stats(ti, a, b_):
                  n = b_ - a
                  bk = bank()
                  for mch in range(KC):
                      act(CC[:, mch, a:b_], X[:, mch, a:b_], AF.Square, [XR(ti)], [("CC", mch, ti)])
                  for mch in range(KC):
                      P.op('pe', lambda e, mch=mch, bk=bk, n=n: e.matmul(PS[bk][:, 0:n], lhsT=ONESB[:], rhs=CC[:, mch, a:b_],
                                                                       start=(mch == 0), stop=(mch == KC - 1)),
                           [("CC", mch, ti), "ONESB"], [("ps", bk)])
                  act(RR[:, a:b_], PS[bk][:, 0:n], AF.Sqrt, [("ps", bk), "EPSR"], [("A1", ti)], bias=EPSR[:, 0:1])
                  P.op('dve', lambda e: e.reciprocal(out=RR[:, a:b_], in_=RR[:, a:b_]), [("A1", ti)], [("A1", ti)])

              def state_groups(l_):
                  gs = []
                  SPG = 128 // CB
                  for g0 in range(0, NS, SPG):
                      ng = min(SPG, NS - g0)
                      gs.append(('c', g0, ng, ng * CB, sc[l_, g0:g0 + ng].rearrange("s j d -> (s j) d")))
                  SPG2 = 128 // PB
                  for g0 in range(0, NS, SPG2):
                      ng = min(SPG2, NS - g0)
                      gs.append(('p', g0, ng, ng * PB, spl[l_, g0:g0 + ng].rearrange("s j d -> (s j) d")))
                  return gs

              def state_dma(l_):
                  pre = []
                  for (kind_, g0, ng, nr, src) in state_groups(l_):
                      if len(bst_state['reserved']) >= NBST - 1:
                          break
                      bi = bst_next()
                      st_ap, st_reg, st_ld, _ = BST[bi]
                      P.dma('sp', st_ap[0:nr, :], src, st_ld, (), [st_reg])
                      bst_state['reserved'].add(bi)
                      pre.append(bi)
                  return pre

              def state_consume(l_, pre):
                  for gi, (kind_, g0, ng, nr, src) in enumerate(state_groups(l_)):
                      if gi < len(pre):
                          bi = pre[gi]
                      else:
                          bi = bst_next()
                          P.dma('sp', BST[bi][0][0:nr, :], src, BST[bi][2], (), [BST[bi][1]])
                      st_ap, st_reg = BST[bi][0], BST[bi][1]
                      bst_state['reserved'].discard(bi)
                      J = CB if kind_ == 'c' else PB
                      for h in range(0, KC, 4):
                          b = bank()
                          mm_ = list(range(h, min(KC, h + 4)))
                          transposes(b, [(st_ap[0:nr, q * 128:(q + 1) * 128], i * 128, nr, 128) for i, q in enumerate(mm_)], [st_reg])
                          for i, q in enumerate(mm_):
                              if kind_ == 'c':
                                  dst_, dreg_ = UX[:, q, g0:g0 + ng, 0:CB], ("UX", q)
                              else:
                                  dst_, dreg_ = PX[:, q, g0:g0 + ng, 0:PB], ("PX", q)
                              cp('dve', dst_, PS[b][:, i * 128:i * 128 + nr].rearrange("p (s j) -> p s j", j=J), [("ps", b)], [dreg_])

              pre_state = {}
              if hs:
                  pre_state[0] = state_dma(0)

              for l in range(DEPTH):
                  for ti, (a, b_, kd) in enumerate(tiles):
                      rms_stats(ti, a, b_)
                      for mch in range(KC):
                          stt(H[:, mch, a:b_], X[:, mch, a:b_], pv(l, CWID + 3, mch), RR[:, a:b_], ALU.mult, ALU.mult,
                              [XR(ti), ("A1", ti), "PV"], [("H", mch, ti)])

                  chk(2)
                  if hs:
                      state_consume(l, pre_state.get(l, []))

                  chk(20)
                  mset('dve', S1[:, 0:T], 0.0, [("A1", ti) for ti in range(len(tiles))])
                  mset('dve', S2[:, 0:T], 0.0, [("A2", ti) for ti in range(len(tiles))])
                  uslot_of = {}

                  def glu(mch, wv, wg, mloc):
                      us = mch % 2
                      uslot_of[mch] = us
                      ds_ = mch % 3
                      base_r = 36 * l
                      def dg_build(k0, k1):
                          nk = k1 - k0
                          P.op('dve', lambda e: e.tensor_tensor(out=DG[ds_][:, k0:k1, :], in0=IDB[:].unsqueeze(1).broadcast_to([128, nk, 128]),
                                                                in1=PV[:, mch, base_r + k0:base_r + k1].unsqueeze(2).broadcast_to([128, nk, 128]),
                                                                op=ALU.mult), ["IDB", "PV"], [("DG", ds_)])
                      dg_parts = [(0, 11), (11, 21), (21, CWID)]
                      dg_build(*dg_parts[0])
                      if first_seg:
                          mset('dve', U[us][:, 0:CB], 0.0, [("U", us)])
                      else:
                          cp('dve', U[us][:, 0:CB], UT[l][:, mch, :], [("UT", l, mch)], [("U", us)])
                      for ti, (a, b_, kd) in enumerate(tiles):
                          n = b_ - a
                          bg = bank()
                          mm_chain(bg, n, [(W[wg][:, k, mloc * 128:(mloc + 1) * 128], H[:, k, a:b_], [("W", wg), ("H", k, ti)]) for k in range(KC)])
                          bv = bank()
                          mm_chain(bv, n, [(W[wv][:, k, mloc * 128:(mloc + 1) * 128], H[:, k, a:b_], [("W", wv), ("H", k, ti)]) for k in range(KC)])
                          sg = nxt("sg", 4)
                          act(SG[sg][:, 0:n], PS[bg][:, 0:n], AF.Sigmoid, [("ps", bg)], [("SG", sg)])
                          if kd > 0:
                              tt('dve', U[us][:, CB + a:CB + a + kd], PS[bv][:, 0:kd], SG[sg][:, 0:kd], ALU.mult, [("ps", bv), ("SG", sg)], [("U", us)])
                              if last_prompt_seg and a + kd == Tp:
                                  tt('dve', UL[:, mch, :], PS[bv][:, kd - CB:kd], SG[sg][:, kd - CB:kd], ALU.mult, [("ps", bv), ("SG", sg)], [("UL", mch)])
                          if kd < n:
                              tt('dve', UX[:, mch, :, CB:CB + DS], PS[bv][:, kd:n].rearrange("p (s j) -> p s j", j=DS),
                                 SG[sg][:, kd:n].rearrange("p (s j) -> p s j", j=DS), ALU.mult, [("ps", bv), ("SG", sg)], [("UX", mch)])
                              tt('dve', US[:, mch, :], PS[bv][:, kd:n], SG[sg][:, kd:n], ALU.mult, [("ps", bv), ("SG", sg)], [("US", mch)])
                          if ti + 1 < len(dg_parts):
                              dg_build(*dg_parts[ti + 1])
                      for pi_ in range(len(tiles) + 1, len(dg_parts)):
                          dg_build(*dg_parts[pi_])
                      if not last_prompt_seg:
                          cp('dve', UT[l][:, mch, :], U[us][:, Tp:Tp + CB], [("U", us)], [("UT", l, mch)])

                  def conv(mch):
                      us = uslot_of[mch]
                      ds_ = mch % 3
                      for ti, (a, b_, kd) in enumerate(tiles):
                          n = b_ - a
                          bc = bank()
                          if kd > 0:
                              mm_group(bc, kd, [(DG[ds_][:, k, :], U[us][:, a + k:a + kd + k]) for k in range(CWID)], [("DG", ds_), ("U", us)])
                          if kd < n:
                              mm_group(bc, n - kd, [(DG[ds_][:, k, :], UX[:, mch, :, k:k + DS]) for k in range(CWID)], [("DG", ds_), ("UX", mch)], c0=kd)
                          cb_ap = pv(l, CWID + 0, mch)
                          act(CC[:, mch, a:b_], PS[bc][:, 0:n], AF.Identity, [("ps", bc), "PV"], [("CC", mch, ti)], bias=cb_ap)
                          cs = nxt("t2", 4)
                          act(T2[cs][:, 0:n], PS[bc][:, 0:n], AF.Square, [("ps", bc), "PV"], [("T2", cs)], bias=cb_ap)
                          tt('dve', S1[:, a:b_], S1[:, a:b_], CC[:, mch, a:b_], ALU.add, [("CC", mch, ti)], [("A1", ti)])
                          tt('dve', S2[:, a:b_], S2[:, a:b_], T2[cs][:, 0:n], ALU.add, [("T2", cs)], [("A2", ti)])

                  pool_first = [False]

                  def pin(mch, wp, mloc):
                      pslot = mch % 2
                      wwin = WINS[mch // GC]
                      if first_seg:
                          mset('dve', PIN[pslot][:, 0:PB], 0.0, [("PIN", pslot)])
                      else:
                          cp('dve', PIN[pslot][:, 0:PB], PTL[l][:, mch, :], [("PTL", l, mch)], [("PIN", pslot)])
                      for ti, (a, b_, kd) in enumerate(tiles):
                          n = b_ - a
                          bp = bank()
                          mm_group(bp, n, [(W[wp][:, k, mloc * 128:(mloc + 1) * 128], H[:, k, a:b_]) for k in range(KC)], [("W", wp)] + HRL(ti))
                          if kd > 0:
                              act(PIN[pslot][:, PB + a:PB + a + kd], PS[bp][:, 0:kd], AF.Copy, [("ps", bp)], [("PIN", pslot)])
                          if kd < n:
                              act(PX[:, mch, :, PB:PB + DS], PS[bp][:, kd:n].rearrange("p (s j) -> p s j", j=DS), AF.Copy, [("ps", bp)], [("PX", mch)])
                              act(PSN[:, mch, :], PS[bp][:, kd:n], AF.Copy, [("ps", bp)], [("PSN", mch)])
                      if not last_prompt_seg:
                          cp('dve', PTL[l][:, mch, :], PIN[pslot][:, Tp:Tp + PB], [("PIN", pslot)], [("PTL", l, mch)])
                      else:
                          cp('dve', PL15[:, mch, :], PIN[pslot][:, Tp:Tp + PB], [("PIN", pslot)], [("PL15", mch)])

                      def pool_region(xap_fn, E, rd_reg, outs):
                          cur = xap_fn
                          sh = 1
                          lvl = 0
                          cur_reg = rd_reg
                          while sh < wwin:
                              dst = TA if lvl % 2 == 0 else TB
                              dreg = "MGALL"
                              lo = 2 * sh - 1
                              wr_ = [dreg]
                              if not pool_first[0]:
                                  pool_first[0] = True
                                  wr_ = [dreg] + [("MG", k_, t_) for k_ in range(KC) for t_ in range(len(tiles))]
                              tt('dve', dst[:, lo:E], cur(lo, E), cur(lo - sh, E - sh), ALU.add, [cur_reg], wr_)
                              cur = (lambda d: (lambda lo_, hi_: d[:, lo_:hi_]))(dst)
                              cur_reg = dreg
                              sh *= 2
                              lvl += 1
                          outs(cur, cur_reg)

                      E = PB + Tp

                      def outs_p(cur, cur_reg, pslot=pslot, mch=mch, wwin=wwin):
                          for ti, (a, b_, kd) in enumerate(tiles):
                              if kd == 0:
                                  continue
                              stt(QF[:, mch, a:a + kd], cur(PB + a, PB + a + kd), 1.0 / wwin, PIN[pslot][:, PB + a:PB + a + kd], ALU.mult, ALU.subtract,
                                  [cur_reg, ("PIN", pslot)], [("QF", mch, ti)])
                          if first_seg:
                              nf = wwin - 1
                              tt('dve', T1[0][:, 0:nf], cur(PB, PB + nf), INV[:, 0:nf], ALU.mult, [cur_reg, "INV"], [("T1", 0)])
                              tt('dve', QF[:, mch, 0:nf], T1[0][:, 0:nf], PIN[pslot][:, PB:PB + nf], ALU.subtract, [("T1", 0), ("PIN", pslot)], [("QF", mch, 0)])
                      pool_region(lambda lo, hi, pslot=pslot: PIN[pslot][:, lo:hi], E, ("PIN", pslot), outs_p)
                      if hs:
                          E2 = NS * (PB + DS)
                          flat = PX[:, mch, :, :].rearrange("p s j -> p (s j)")

                          def outs_s(cur, cur_reg, mch=mch, wwin=wwin):
                              ti = len(tiles) - 1
                              a, b_, kd = tiles[ti]
                              cur3 = cur(0, E2).rearrange("p (s j) -> p s j", j=PB + DS)
                              stt(QF[:, mch, a + kd:b_].rearrange("p (s j) -> p s j", j=DS), cur3[:, :, PB:PB + DS], 1.0 / wwin, PX[:, mch, :, PB:PB + DS],
                                  ALU.mult, ALU.subtract, [cur_reg, ("PX", mch)], [("QF", mch, ti)])
                          pool_region(lambda lo, hi, flat=flat: flat[:, lo:hi], E2, ("PX", mch), outs_s)

                  wv = wg = wp = None
                  for mch in range(KC):
                      if mch % MB == 0:
                          wv = w_take()
                          wg = w_take()
                          wp = w_take()
                      glu(mch, wv, wg, mch % MB)
                      if mch >= 1:
                          conv(mch - 1)
                      pin(mch, wp, mch % MB)
                      if mch % MB == MB - 1:
                          w_release(3)
                  conv(KC - 1)

                  chk(3)
                  def out_rows_from_fm(src_fn, nrows, dst_fn, rd):
                      s = nxt("stg", 2)
                      for h in range(0, KC, 4):
                          b = bank()
                          mm_ = list(range(h, min(KC, h + 4)))
                          transposes(b, [(src_fn(q), i * 128, 128, nrows) for i, q in enumerate(mm_)], [r for q in mm_ for r in rd(q)])
                          act(STG[s][0:nrows, h * 128:(h + len(mm_)) * 128], PS[b][0:nrows, 0:len(mm_) * 128], AF.Copy, [("ps", b)], [("STG", s)])
                      return dst_fn(s)

                  if last_prompt_seg:
                      out_toks.append(out_rows_from_fm(lambda q: UL[:, q, :], CB,
                                                       lambda s: P.dma('sp', ncp[l], STG[s][0:CB, :], stg_st[s], [("STG", s)], ()),
                                                       lambda q: [("UL", q)]))
                  if hs:
                      for r0 in range(0, TS, 128):
                          n = min(128, TS - r0)

                          def dst(s, r0=r0, n=n):
                              tk = None
                              for sq_ in range(r0 // DS, (r0 + n) // DS):
                                  tk = P.dma('sp', ncs[l, sq_, CB - DS:CB, :], STG[s][(sq_ * DS - r0):(sq_ * DS - r0) + DS, :], stg_st[s], [("STG", s)], ())
                              return tk
                          out_toks.append(out_rows_from_fm(lambda q, r0=r0, n=n: US[:, q, r0:r0 + n], n, dst, lambda q: [("US", q)]))

                  chk(30)
                  for ti, (a, b_, kd) in enumerate(tiles):
                      n = b_ - a
                      b1 = bank()
                      P.op('pe', lambda e, b1=b1, a=a, b_=b_, n=n: e.matmul(PS[b1][:, 0:n], lhsT=ONES32[:], rhs=S1[:, a:b_], start=True, stop=True),
                           [("A1", ti), "ONES32"], [("ps", b1)])
                      b2 = bank()
                      P.op('pe', lambda e, b2=b2, a=a, b_=b_, n=n: e.matmul(PS[b2][:, 0:n], lhsT=ONES32[:], rhs=S2[:, a:b_], start=True, stop=True),
                           [("A2", ti), "ONES32"], [("ps", b2)])
                      cp('dve', MU[:, a:b_], PS[b1][:, 0:n], [("ps", b1)], [("A1", ti)])
                      tt('dve', RS[:, a:b_], MU[:, a:b_], MU[:, a:b_], ALU.mult, [("A1", ti)], [("A2", ti)])
                      tt('dve', RS[:, a:b_], PS[b2][:, 0:n], RS[:, a:b_], ALU.subtract, [("ps", b2), ("A2", ti)], [("A2", ti)])
                      act(RS[:, a:b_], RS[:, a:b_], AF.Sqrt, [("A2", ti), "EPSL"], [("A2", ti)], bias=EPSL[:, 0:1])
                      P.op('dve', lambda e, a=a, b_=b_: e.reciprocal(out=RS[:, a:b_], in_=RS[:, a:b_]), [("A2", ti)], [("A2", ti)])
                  pend_fin = [None]
                  for blk in range(NB):
                      ws = w_take()
                      wa = w_take()
                      chs = list(range(blk * MB, (blk + 1) * MB))
                      if blk == NB - 1:
                          order = [(m_, t_) for t_ in range(len(tiles)) for m_ in chs]
                      else:
                          order = [(m_, t_) for m_ in chs for t_ in range(len(tiles))]
                      for (mch, ti) in order:
                          mloc = mch % MB
                          a, b_, kd = tiles[ti]
                          n = b_ - a
                          bs = bank()
                          mm_group(bs, n, [(W[ws][:, k, mloc * 128:(mloc + 1) * 128], H[:, k, a:b_]) for k in range(KC)], [("W", ws)] + HRL(ti))
                          bg = bank()
                          mm_group(bg, n, [(W[wa][:, k, mloc * 128:(mloc + 1) * 128], H[:, k, a:b_]) for k in range(KC)], [("W", wa)] + HRL(ti))
                          sg = nxt("sg", 4)
                          act(SG[sg][:, 0:n], PS[bs][:, 0:n], AF.Silu, [("ps", bs)], [("SG", sg)])
                          act(MG[:, mch, a:b_], PS[bg][:, 0:n], AF.Tanh, [("ps", bg)], [("MG", mch, ti), "MGALL"], scale=0.5)
                          t1 = nxt("t1", 4)
                          tt('dve', T1[t1][:, 0:n], CC[:, mch, a:b_], MU[:, a:b_], ALU.subtract, [("CC", mch, ti), ("A1", ti)], [("T1", t1)])
                          tt('dve', T1[t1][:, 0:n], T1[t1][:, 0:n], RS[:, a:b_], ALU.mult, [("T1", t1), ("A2", ti)], [("T1", t1)])
                          t2 = nxt("t2", 4)
                          act(T2[t2][:, 0:n], T1[t1][:, 0:n], AF.Silu, [("T1", t1), "PV"], [("T2", t2)], bias=pv(l, CWID + 2, mch), scale=pv(l, CWID + 1, mch))
                          if pend_fin[0] is not None:
                              pend_fin[0]()

                          def fin_(mch=mch, a=a, b_=b_, n=n, t2=t2, sg=sg, ti=ti):
                              tt('dve', CC[:, mch, a:b_], T2[t2][:, 0:n], SG[sg][:, 0:n], ALU.mult, [("T2", t2), ("SG", sg)], [("CC", mch, ti)])
                          pend_fin[0] = fin_
                      w_release(2)
                  pend_fin[0]()

                  chk(6)
                  if last_prompt_seg:
                      out_toks.append(out_rows_from_fm(lambda q: PL15[:, q, :], PB,
                                                       lambda s: P.dma('sp', npp[l], STG[s][0:PB, :], stg_st[s], [("STG", s)], ()),
                                                       lambda q: [("PL15", q)]))
                  if hs:
                      for r0 in range(0, TS, 128):
                          n = min(128, TS - r0)

                          def dst2(s, r0=r0, n=n):
                              tk = None
                              for sq_ in range(r0 // DS, (r0 + n) // DS):
                                  tk = P.dma('sp', nps[l, sq_, PB - DS:PB, :], STG[s][(sq_ * DS - r0):(sq_ * DS - r0) + DS, :], stg_st[s], [("STG", s)], ())
                              return tk
                          out_toks.append(out_rows_from_fm(lambda q, r0=r0, n=n: PSN[:, q, r0:r0 + n], n, dst2, lambda q: [("PSN", q)]))

                  chk(4)
                  wcs = [w_take() for _ in range(NB)]
                  for ti, (a, b_, kd) in enumerate(tiles):
                      n = b_ - a
                      for mch in range(KC):
                          wc = wcs[mch // MB]
                          mloc = mch % MB
                          ba = bank()
                          mm_group(ba, n, [(W[wc][:, k, mloc * 128:(mloc + 1) * 128], CC[:, k, a:b_]) for k in range(KC)],
                                   [("W", wc)] + [("CC", k, ti) for k in range(KC)])
                          stt(MG[:, mch, a:b_], MG[:, mch, a:b_], 1.0, PS[ba][:, 0:n], ALU.add, ALU.mult, [("ps", ba)], [("MG", mch, ti)])
                          if ti == len(tiles) - 1 and mch % MB == MB - 1:
                              w_release(1)

                  chk(60)
                  wms = None
                  wps = None
                  for mch in range(KC):
                      if mch % MB == 0:
                          wps = w_take()
                          wms = [w_take() for _ in range(MPB)]
                      mloc = mch % MB
                      g = mch // GC
                      wm = wms[mloc // GC]
                      for ti, (a, b_, kd) in enumerate(tiles):
                          n = b_ - a
                          bq = bank()
                          mm_group(bq, n, [(W[wm][:, kk, (mch % GC) * 128:(mch % GC + 1) * 128], QF[:, g * GC + kk, a:b_]) for kk in range(GC)],
                                   [("W", wm)] + [("QF", g * GC + kk, ti) for kk in range(GC)])
                          bs = bank()
                          mm_group(bs, n, [(W[wps][:, k, mloc * 128:(mloc + 1) * 128], H[:, k, a:b_]) for k in range(KC)], [("W", wps)] + HRL(ti))
                          sg = nxt("sg", 4)
                          act(SG[sg][:, 0:n], PS[bs][:, 0:n], AF.Silu, [("ps", bs)], [("SG", sg)])
                          stt(CC[:, mch, a:b_], PS[bq][:, 0:n], pv(l, CWID + 4, mch), SG[sg][:, 0:n], ALU.mult, ALU.mult,
                              [("ps", bq), ("SG", sg), "PV"], [("CC", mch, ti)])
                      if mch % MB == MB - 1:
                          w_release(1 + MPB)

                  chk(7)
                  wo_ = wb = None
                  for mch in range(KC):
                      if mch % MB == 0:
                          wb = w_take()
                      mloc = mch % MB
                      for ti, (a, b_, kd) in enumerate(tiles):
                          n = b_ - a
                          bg = bank()
                          mm_group(bg, n, [(W[wb][:, k, mloc * 128:(mloc + 1) * 128], H[:, k, a:b_]) for k in range(KC)], [("W", wb)] + HRL(ti))
                          act(QF[:, mch, a:b_], PS[bg][:, 0:n], AF.Sigmoid, [("ps", bg)], [("QF", mch, ti)])
                      if mch % MB == MB - 1:
                          w_release(1)
                  for mch in range(KC):
                      if mch % MB == 0:
                          wo_ = w_take()
                      mloc = mch % MB
                      for ti, (a, b_, kd) in enumerate(tiles):
                          n = b_ - a
                          bb = bank()
                          mm_group(bb, n, [(W[wo_][:, k, mloc * 128:(mloc + 1) * 128], CC[:, k, a:b_]) for k in range(KC)],
                                   [("W", wo_)] + [("CC", k, ti) for k in range(KC)])
                          t1 = nxt("t1", 4)
                          tt('dve', T1[t1][:, 0:n], PS[bb][:, 0:n], QF[:, mch, a:b_], ALU.mult, [("ps", bb), ("QF", mch, ti)], [("T1", t1)])
                          stt(MG[:, mch, a:b_], T1[t1][:, 0:n], 2.0, MG[:, mch, a:b_], ALU.mult, ALU.add, [("T1", t1)], [("MG", mch, ti)])
                      if mch % MB == MB - 1:
                          w_release(1)

                  chk(8)
                  wws = [w_take() for _ in range(NB)]
                  if hs and l + 1 < DEPTH:
                      pre_state[l + 1] = state_dma(l + 1)
                  for ti, (a, b_, kd) in enumerate(tiles):
                      n = b_ - a
                      for mch in range(KC):
                          ww = wws[mch // MB]
                          mloc = mch % MB
                          bo = bank()
                          mm_group(bo, n, [(W[ww][:, k, mloc * 128:(mloc + 1) * 128], MG[:, k, a:b_]) for k in range(KC)],
                                   [("W", ww)] + [("MG", k, ti) for k in range(KC)])
                          stt(X[:, mch, a:b_], PS[bo][:, 0:n], 0.5, X[:, mch, a:b_], ALU.mult, ALU.add, [("ps", bo)], [XR(ti)])
                          if ti == len(tiles) - 1 and mch % MB == MB - 1:
                              w_release(1)

              chk(9)
              for ti, (a, b_, kd) in enumerate(tiles):
                  rms_stats(ti, a, b_)
                  for mch in range(KC):
                      stt(X[:, mch, a:b_], X[:, mch, a:b_], PV[:, mch, R - 1:R], RR[:, a:b_], ALU.mult, ALU.mult,
                          [XR(ti), ("A1", ti), "PV"], [XR(ti)])

              def xregs(c0, n):
                  return [("X", ti) for ti, (a, b2, kd) in enumerate(tiles) if not (b2 <= c0 or a >= c0 + n)]

              def store_rows(c0, n, dst_fn):
                  bst = BST[bst_next()]
                  st_ap, st_reg = bst[0], bst[1]
                  for h in range(0, KC, 4):
                      b = bank()
                      mm_ = list(range(h, min(KC, h + 4)))
                      transposes(b, [(X[:, q, c0:c0 + n], i * 128, 128, n) for i, q in enumerate(mm_)], xregs(c0, n))
                      act(st_ap[0:n, h * 128:(h + len(mm_)) * 128], PS[b][0:n, 0:len(mm_) * 128], AF.Copy, [("ps", b)], [st_reg])
                  return dst_fn(bst)

              if si + 1 < nseg and NBST > NPF + 1:
                  for (srcs_n, n_n, c0_n) in seg_blocks(*cfg.segs[si + 1])[:NPF]:
                      bi_n = issue_row_dmas(srcs_n)
                      bst_state['reserved'].add(bi_n)
                      bst_state['pre'].append(bi_n)
              pos = p0
              while pos < p1:
                  n = min(128, p1 - pos)
                  lo = max(pos, NMETA)
                  if lo < pos + n:
                      def dsty(bst, pos=pos, n=n, lo=lo):
                          return P.dma('sp', yp[lo - NMETA:pos + n - NMETA, :], bst[0][lo - pos:n, :], bst[3], [bst[1]], ())
                      out_toks.append(store_rows(pos - p0, n, dsty))
                  pos += n
              chk(10)
              if hs:
                  for r0 in range(0, TS, 128):
                      n = min(128, TS - r0)

                      def dsts(bst, r0=r0, n=n):
                          return P.dma('sp', ys[r0:r0 + n, :], bst[0][0:n, :], bst[3], [bst[1]], ())
                      out_toks.append(store_rows(Tp + r0, n, dsts))
          except _Stop:
            break

        finals = {}
        for tk in out_toks:
            if tk is None:
                continue
            k = tk[0].name
            tot = P.dcnt[k]
            finals[k] = (tk[0], tot)
        P.wait_all('sp', list(finals.values()))

        @block.sync
        def _(e):
            P.replay('sp', e)

        @block.gpsimd
        def _(e):
            P.replay('pool', e)

        @block.tensor
        def _(e):
            P.replay('pe', e)

        @block.scalar
        def _(e):
            P.replay('act', e)

        @block.vector
        def _(e):
            P.replay('dve', e)
    return nc


def make_in_maps(cfg, ncores, x_prompt, x_sample, state_conv, state_pool, meta_tokens, norm_g, w_in, conv_w, conv_b,
                 ln_g, ln_b, w_conv_out, w_pool_mix, pool_scale, w_pool_out, w_out, final_g):
    f = lambda a: np.ascontiguousarray(np.asarray(a, dtype=np.float32))
    NS, D = cfg.NS, cfg.D
    KC, BW, GC = cfg.KC, cfg.BW, cfg.GC
    PG = D // 4

    def pack_blocks(w):
        w = f(w)
        L, _, C = w.shape
        return np.ascontiguousarray(w.reshape(L, KC, 128, C // BW, BW).transpose(0, 3, 2, 1, 4)).reshape(L, C // BW, 128, KC * BW)

    def pack_mix(w):
        w = f(w)
        L = w.shape[0]
        return np.ascontiguousarray(w.reshape(L, 4, GC, 128, PG).transpose(0, 1, 3, 2, 4)).reshape(L, 4, 128, GC * PG)

    shared = dict(meta=f(meta_tokens), norm_g=f(norm_g), w_in=pack_blocks(w_in), conv_w=f(conv_w), conv_b=f(conv_b), ln_g=f(ln_g),
                  ln_b=f(ln_b), w_co=pack_blocks(w_conv_out), w_mix=pack_mix(w_pool_mix), pscale=f(pool_scale),
                  w_po=pack_blocks(w_pool_out), w_o=pack_blocks(w_out), final_g=f(final_g).reshape(1, D))
    x_prompt = np.asarray(x_prompt); x_sample = np.asarray(x_sample)
    state_conv = np.asarray(state_conv); state_pool = np.asarray(state_pool)
    maps = []
    for c in range(ncores):
        m = dict(shared)
        m["xp"] = f(x_prompt[c])
        m["xs"] = f(x_sample[c * NS:(c + 1) * NS]).reshape(NS * DS, D)
        m["sc"] = f(state_conv[:, c * NS:(c + 1) * NS])
        m["spl"] = f(state_pool[:, c * NS:(c + 1) * NS])
        maps.append(m)
    return maps


def gather(cfg, results):
    NS, D = cfg.NS, cfg.D
    y_prompt = np.stack([np.asarray(r["yp"]) for r in results], axis=0).astype(np.float32)
    y_sample = np.concatenate([np.asarray(r["ys"]).reshape(NS, DS, D) for r in results], axis=0).astype(np.float32)
    ncp = np.stack([np.asarray(r["ncp"]) for r in results], axis=1).astype(np.float32)
    npp = np.stack([np.asarray(r["npp"]) for r in results], axis=1).astype(np.float32)
    ncs = np.concatenate([np.asarray(r["ncs"]) for r in results], axis=1).astype(np.float32)
    nps = np.concatenate([np.asarray(r["nps"]) for r in results], axis=1).astype(np.float32)
    return (y_prompt, y_sample, ncp, npp, ncs, nps)


def kernel(**inputs):
    cfg = REAL
    nc = build_program(cfg)
    maps = make_in_maps(cfg, 8, **inputs)
    res = run_bass_kernel_spmd(nc, maps, core_ids=list(range(8)))
    return gather(cfg, res.results)
```

```python
import numpy as np
from contextlib import ExitStack
import concourse.bass as bass
import concourse.mybir as mybir
from concourse.bass_utils import run_bass_kernel_spmd

F32 = mybir.dt.float32
BF16 = mybir.dt.bfloat16
AF = mybir.ActivationFunctionType
ALU = mybir.AluOpType

ENGS = ['pe', 'act', 'dve', 'pool', 'sp']
CWID = 31
CB = 30
PB = 15
WINS = (2, 4, 8, 16)
NMETA = 16
DS = 8
RMS_EPS = 1e-6
LN_EPS = 1e-5
DEPTH = 2


class Cfg:
    def __init__(self, D=1024, SEQ=2048, NS=16, segs=None, ntile=2, BW=256, WR=6, merge_sample=True, wscratch=True):
        self.D = D
        self.KC = D // 128
        self.SEQ = SEQ
        self.PT = SEQ + NMETA
        self.NS = NS
        self.TS = NS * DS
        self.GC = (D // 4) // 128
        assert self.GC >= 1
        self.segs = segs
        self.ntile = ntile
        self.BW = BW
        self.merge_sample = merge_sample
        self.wscratch = wscratch
        self.WR = WR
        self.MB = BW // 128
        self.NB = D // BW
        self.R = 36 * DEPTH + 1


REAL = Cfg(segs=[(0, 736, False), (736, 1472, False), (1472, 2064, True)], WR=8)


class _Stop(Exception):
    pass


_cur = {'si': 0}


def chk(n):
    return


class Prog:
    def __init__(self, sems):
        self.ops = {e: [] for e in ENGS}
        self.cnt = {e: 0 for e in ENGS}
        self.sem = sems
        self.waited = {e: {} for e in ENGS}
        self.dcnt = {}
        self.lastw = {}
        self.readers = {}

    def _waits(self, eng, deps):
        waits = {}
        for d in deps:
            if d is None:
                continue
            sem, val = d
            key = sem.name
            if eng == 'pe' and key == self.sem['pe'].name:
                continue
            if self.waited[eng].get(key, 0) >= val:
                continue
            if key in waits and waits[key][1] >= val:
                continue
            waits[key] = (sem, val)
        for key, (sem, val) in waits.items():
            self.waited[eng][key] = val
        return list(waits.values())

    def _deps(self, reads, writes):
        deps = []
        for r in reads:
            deps.append(self.lastw.get(r))
        for w in writes:
            deps.append(self.lastw.get(w))
            deps.extend(self.readers.get(w, {}).values())
        return deps

    def _mark(self, tok, reads, writes):
        key = tok[0].name
        for r in reads:
            d = self.readers.setdefault(r, {})
            if key not in d or d[key][1] < tok[1]:
                d[key] = tok
        for w in writes:
            self.lastw[w] = tok
            self.readers[w] = {}

    def op(self, eng, fn, reads=(), writes=()):
        return self.group(eng, [fn], reads, writes)

    def group(self, eng, fns, reads=(), writes=()):
        waits = self._waits(eng, self._deps(reads, writes))
        self.cnt[eng] += 1
        tok = (self.sem[eng], self.cnt[eng])
        n = len(fns)
        for i, fn in enumerate(fns):
            self.ops[eng].append((waits if i == 0 else [], fn, self.sem[eng] if i == n - 1 else None, 1))
        self._mark(tok, reads, writes)
        return tok

    def chain(self, eng, items, writes=()):
        self.cnt[eng] += 1
        tok = (self.sem[eng], self.cnt[eng])
        n = len(items)
        allreads = []
        for i, (fn, reads) in enumerate(items):
            deps = [self.lastw.get(r) for r in reads]
            if i == 0:
                for w in writes:
                    deps.append(self.lastw.get(w))
                    deps.extend(self.readers.get(w, {}).values())
            waits = self._waits(eng, deps)
            self.ops[eng].append((waits, fn, self.sem[eng] if i == n - 1 else None, 1))
            allreads.extend(reads)
        self._mark(tok, allreads, writes)
        return tok

    def dma(self, eng, out, in_, sem, reads=(), writes=()):
        waits = self._waits(eng, self._deps(reads, writes))
        key = sem.name
        self.dcnt[key] = self.dcnt.get(key, 0) + 16
        tok = (sem, self.dcnt[key])
        self.ops[eng].append((waits, lambda e: e.dma_start(out=out, in_=in_), sem, 16))
        self._mark(tok, reads, writes)
        return tok

    def wait_all(self, eng, toks):
        waits = self._waits(eng, toks)
        self.ops[eng].append((waits, None, None, 0))

    def replay(self, eng, e):
        for waits, fn, sem, inc in self.ops[eng]:
            for (s, v) in waits:
                e.wait_ge(s, v)
            if fn is not None:
                inst = fn(e)
                if sem is not None:
                    inst.then_inc(sem, inc)


def build_program(cfg):
    D, KC, PT, NS, TS, GC = cfg.D, cfg.KC, cfg.PT, cfg.NS, cfg.TS, cfg.GC
    BW, MB, NB, R = cfg.BW, cfg.MB, cfg.NB, cfg.R
    PG = D // 4
    nc = bass.Bass("TRN2", target_bir_lowering=False)

    def din(name, shape):
        return nc.dram_tensor(name, shape, F32, kind="ExternalInput").ap()

    def dout(name, shape):
        return nc.dram_tensor(name, shape, F32, kind="ExternalOutput").ap()

    xp = din("xp", [cfg.SEQ, D])
    xs = din("xs", [TS, D])
    sc = din("sc", [DEPTH, NS, CB, D])
    spl = din("spl", [DEPTH, NS, PB, D])
    meta = din("meta", [NMETA, D])
    norm_g = din("norm_g", [DEPTH, D])
    w_in = din("w_in", [DEPTH, 7 * D // BW, 128, KC * BW])
    conv_w = din("conv_w", [DEPTH, CWID, D])
    conv_b = din("conv_b", [DEPTH, D])
    ln_g = din("ln_g", [DEPTH, D])
    ln_b = din("ln_b", [DEPTH, D])
    w_co = din("w_co", [DEPTH, D // BW, 128, KC * BW])
    w_mix = din("w_mix", [DEPTH, 4, 128, GC * PG])
    pscale = din("pscale", [DEPTH, D])
    w_po = din("w_po", [DEPTH, D // BW, 128, KC * BW])
    w_o = din("w_o", [DEPTH, D // BW, 128, KC * BW])
    final_g = din("final_g", [1, D])
    yp = dout("yp", [cfg.SEQ, D])
    ys = dout("ys", [TS, D])
    ncp = dout("ncp", [DEPTH, CB, D])
    npp = dout("npp", [DEPTH, PB, D])
    ncs = dout("ncs", [DEPTH, NS, CB, D])
    nps = dout("nps", [DEPTH, NS, PB, D])

    def seg_tiles(p0, p1, hs):
        Tp_ = p1 - p0
        tl = []
        nt_ = cfg.ntile
        base_ = (Tp_ // nt_) // 2 * 2
        c_ = 0
        for i_ in range(nt_):
            c1_ = Tp_ if i_ == nt_ - 1 else c_ + base_
            tl.append((c_, c1_, c1_ - c_))
            c_ = c1_
        if hs:
            la, lb, lnp = tl[-1]
            if (lb - la) + TS <= 512 and cfg.merge_sample:
                tl[-1] = (la, lb + TS, lnp)
            else:
                tl.append((Tp_, Tp_ + TS, 0))
        return tl

    TW = max(b_ - a_ for sg_ in cfg.segs for (a_, b_, _) in seg_tiles(*sg_))
    TW = (TW + 7) // 8 * 8
    TPmax = max(p1 - p0 for (p0, p1, _) in cfg.segs)
    Tmax = max((p1 - p0) + (TS if hs else 0) for (p0, p1, hs) in cfg.segs)
    NMAX = 512
    EU = CB + TPmax
    EP = PB + TPmax
    EPS_ = max(EP, NS * (PB + DS))

    with ExitStack() as st:
        def sbuf(name, shape, dt):
            return st.enter_context(nc.sbuf_tensor(name, shape, dt))

        def sem(name):
            return st.enter_context(nc.semaphore(name))

        X = sbuf("X", [128, KC, Tmax], F32)
        H = sbuf("H", [128, KC, Tmax], BF16)
        CC = sbuf("CC", [128, KC, Tmax], BF16)
        QF = sbuf("QF", [128, KC, Tmax], BF16)
        MG = sbuf("MG", [128, KC, Tmax], BF16)
        U = [sbuf(f"U{i}", [128, EU], BF16) for i in range(2)]
        UX = sbuf("UX", [128, KC, NS, CB + DS], BF16)
        PX = sbuf("PX", [128, KC, NS, PB + DS], F32)
        PIN = [sbuf(f"PIN{i}", [128, EP], F32) for i in range(2)]
        if KC * Tmax // 2 >= 2 * EPS_:
            MGf = MG[:].rearrange("p k t -> p (k t)").bitcast(F32)
            TA = MGf[:, 0:EPS_]
            TB = MGf[:, EPS_:2 * EPS_]
        else:
            TA = sbuf("TA", [128, EPS_], F32)[:]
            TB = sbuf("TB", [128, EPS_], F32)[:]
        WR = cfg.WR
        W = [sbuf(f"W{i}", [128, KC, BW], BF16) for i in range(WR)]
        DG = [sbuf(f"DG{i}", [128, CWID, 128], BF16) for i in range(3)]
        A1 = sbuf("A1", [128, Tmax], F32)
        A2 = sbuf("A2", [128, Tmax], F32)
        S1 = MU = RR = A1
        S2 = RS = A2
        SG = [sbuf(f"SG{i}", [128, TW], F32) for i in range(4)]
        T1 = [sbuf(f"T1_{i}", [128, TW], F32) for i in range(4)]
        T2 = [sbuf(f"T2_{i}", [128, TW], F32) for i in range(4)]
        STG = [sbuf(f"STG{i}", [128, D], F32) for i in range(2)]
        PV = sbuf("PV", [128, KC, R], F32)
        ID32 = sbuf("ID32", [128, 128], F32)
        IDB = sbuf("IDB", [128, 128], BF16)
        ONESB = sbuf("ONESB", [128, 128], BF16)
        ONES32 = sbuf("ONES32", [128, 128], F32)
        EPSR = sbuf("EPSR", [128, 1], F32)
        EPSL = sbuf("EPSL", [128, 1], F32)
        INV = sbuf("INV", [128, PB], F32)
        UL = sbuf("UL", [128, KC, CB], F32)
        PL15 = sbuf("PL15", [128, KC, PB], F32)
        US = sbuf("US", [128, KC, TS], F32)
        PSN = sbuf("PSN", [128, KC, TS], F32)
        UT = [sbuf(f"UT{l}", [128, KC, CB], BF16) for l in range(DEPTH)]
        PTL = [sbuf(f"PTL{l}", [128, KC, PB], F32) for l in range(DEPTH)]
        PS = [st.enter_context(nc.psum_tensor(f"ps{i}", [128, NMAX], F32)) for i in range(8)]

        sems = {e: sem("s_" + e) for e in ENGS}
        wsem = [sem(f"w{i}") for i in range(WR)]
        stg_ld = [sem(f"stl{i}") for i in range(2)]
        stg_st = [sem(f"sts{i}") for i in range(2)]
        misc_sem = sem("misc")
        BST = [(STG[i][:], ("STG", i), stg_ld[i], stg_st[i]) for i in range(2)]
        if CWID * 128 * 2 >= D * 4:
            for i in range(3):
                dgf = DG[i][:].rearrange("p k c -> p (k c)").bitcast(F32)
                BST.append((dgf[:, 0:D], ("DG", i), sem(f"bld{i}"), sem(f"bst{i}")))
        NBST = len(BST)
        d2d_sem = sem("d2d")
        block = st.enter_context(nc.Block())
        P = Prog(sems)

        psi = [0]

        def bank():
            b = psi[0] % 8
            psi[0] += 1
            return b

        rot = {}

        def nxt(name, n):
            v = rot.get(name, 0)
            rot[name] = v + 1
            return v % n

        def act(out, in_, func, reads, writes, bias=None, scale=None):
            kw = {}
            if bias is not None:
                kw['bias'] = bias
            if scale is not None:
                kw['scale'] = scale
            return P.op('act', lambda e: e.activation(out=out, in_=in_, func=func, **kw), reads, writes)

        def tt(eng, out, in0, in1, op, reads, writes):
            return P.op(eng, lambda e: e.tensor_tensor(out=out, in0=in0, in1=in1, op=op), reads, writes)

        def stt(out, in0, scalar, in1, op0, op1, reads, writes):
            return P.op('dve', lambda e: e.scalar_tensor_tensor(out=out, in0=in0, scalar=scalar, in1=in1, op0=op0, op1=op1), reads, writes)

        def cp(eng, out, in_, reads, writes):
            return P.op(eng, lambda e: e.tensor_copy(out=out, in_=in_), reads, writes)

        def mset(eng, ap, val, writes):
            return P.op(eng, lambda e: e.memset(ap, val), (), writes)

        def mm_group(b, n, pairs, reads, c0=0):
            out = PS[b][:, c0:c0 + n]
            fns = []
            last = len(pairs) - 1
            for i, (l, r) in enumerate(pairs):
                fns.append(lambda e, l=l, r=r, i=i: e.matmul(out, lhsT=l, rhs=r, start=(i == 0), stop=(i == last)))
            return P.group('pe', fns, reads, [("ps", b)])

        def mm_chain(b, n, triples, c0=0):
            out = PS[b][:, c0:c0 + n]
            last = len(triples) - 1
            items = []
            for i, (l_, r_, rd_) in enumerate(triples):
                items.append((lambda e, l_=l_, r_=r_, i=i: e.matmul(out, lhsT=l_, rhs=r_, start=(i == 0), stop=(i == last)), rd_))
            return P.chain('pe', items, [("ps", b)])

        def transposes(b, items, reads):
            fns = []
            for (in_ap, off, r, c) in items:
                fns.append(lambda e, in_ap=in_ap, off=off, r=r, c=c: e.transpose(out=PS[b][0:c, off:off + r], in_=in_ap, identity=ID32[0:r, 0:r]))
            return P.group('pe', fns, reads + ["ID32"], [("ps", b)])

        mset('pool', ID32[:], 0.0, ["ID32"])
        P.op('pool', lambda e: e.affine_select(out=ID32[:], in_=ID32[:], pattern=[[-1, 128]], compare_op=ALU.not_equal,
                                               fill=1.0, base=0, channel_multiplier=1), ["ID32"], ["ID32"])
        cp('dve', IDB[:], ID32[:], ["ID32"], ["IDB"])
        mset('pool', ONESB[:], 1.0 / D, ["ONESB"])
        mset('pool', ONES32[:], 1.0 / D, ["ONES32"])
        mset('pool', EPSR[:], RMS_EPS, ["EPSR"])
        mset('pool', EPSL[:], LN_EPS, ["EPSL"])
        for t in range(PB):
            mset('pool', INV[:, t:t + 1], 1.0 / (t + 1), ["INV"])
        d2d = []
        VEC, vreg, vsem, _ = BST[-1]
        vregs = []

        def vload(dst, src):
            reg = ("VECR", len(vregs))
            vregs.append(reg)
            P.dma('sp', dst, src, vsem, (), [reg])
        for l in range(DEPTH):
            b0 = 36 * l
            vload(VEC[b0:b0 + CWID, :], conv_w[l])
            for j, src in enumerate([conv_b, ln_g, ln_b, norm_g, pscale]):
                vload(VEC[b0 + CWID + j:b0 + CWID + j + 1, :], src[l:l + 1, :])
        vload(VEC[R - 1:R, :], final_g)
        per_bank = max(1, NMAX // R)
        m = 0
        while m < KC:
            b = bank()
            ms = list(range(m, min(KC, m + per_bank)))
            transposes(b, [(VEC[0:R, mm * 128:(mm + 1) * 128], i * R, R, 128) for i, mm in enumerate(ms)], vregs + [vreg])
            for i, mm in enumerate(ms):
                cp('dve', PV[:, mm, :], PS[b][:, i * R:(i + 1) * R], [("ps", b)], ["PV"])
            m += per_bank

        def pv(l, row, mchunk):
            r = 36 * l + row
            return PV[:, mchunk, r:r + 1]

        KH = max(1, KC // 2)
        assert MB % GC == 0 and PG <= BW
        MPB = MB // GC

        def wsrc_in(l, c0):
            v = w_in[l, c0 // BW].rearrange("p (k e) -> p k e", k=KC)
            return lambda slot: [(W[slot][:, k0:k0 + KH, 0:BW], v[:, k0:k0 + KH, :]) for k0 in range(0, KC, KH)]

        def wsrc_sq(wt, l, c0):
            v = wt[l, c0 // BW].rearrange("p (k e) -> p k e", k=KC)
            return lambda slot: [(W[slot][:, k0:k0 + KH, 0:BW], v[:, k0:k0 + KH, :]) for k0 in range(0, KC, KH)]

        def wsrc_mixg(l, g):
            v = w_mix[l, g].rearrange("p (kk e) -> p kk e", kk=GC)
            return lambda slot: [(W[slot][:, 0:GC, 0:PG], v)]

        wlist = []
        for (p0, p1, hs) in cfg.segs:
            for l in range(DEPTH):
                for j in range(NB):
                    wlist.append(wsrc_in(l, 0 * D + j * BW))
                    wlist.append(wsrc_in(l, 1 * D + j * BW))
                    wlist.append(wsrc_in(l, 3 * D + j * BW))
                for j in range(NB):
                    wlist.append(wsrc_in(l, 2 * D + j * BW))
                    wlist.append(wsrc_in(l, 5 * D + j * BW))
                for j in range(NB):
                    wlist.append(wsrc_sq(w_co, l, j * BW))
                for j in range(NB):
                    wlist.append(wsrc_in(l, 4 * D + j * BW))
                    for gg in range(MPB):
                        wlist.append(wsrc_mixg(l, j * MPB + gg))
                for j in range(NB):
                    wlist.append(wsrc_in(l, 6 * D + j * BW))
                for j in range(NB):
                    wlist.append(wsrc_sq(w_po, l, j * BW))
                for j in range(NB):
                    wlist.append(wsrc_sq(w_o, l, j * BW))
        wstate = {'issued': 0, 'used': 0}
        nseg_ = len(cfg.segs)
        NBLK = len(wlist) // nseg_
        use_wsc = cfg.wscratch and nseg_ > 1
        if use_wsc:
            wsc = nc.dram_tensor("wsc", [NBLK, 128, KC * BW], BF16, kind="Internal").ap()
            wst = [sem(f"wst{i}") for i in range(WR)]
        def w_issue():
            i = wstate['issued']
            if i >= len(wlist):
                return
            slot = i % WR
            if use_wsc and i >= NBLK:
                P.dma('pool', W[slot][:].rearrange("p k e -> p (k e)"), wsc[i % NBLK], wsem[slot], [("WSC", i % NBLK)], [("W", slot)])
            else:
                for (dst_, src_) in wlist[i](slot):
                    P.dma('pool', dst_, src_, wsem[slot], (), [("W", slot)])
                if use_wsc:
                    P.dma('sp', wsc[i], W[slot][:].rearrange("p k e -> p (k e)"), wst[slot], [("W", slot)], [("WSC", i)])
            wstate['issued'] = i + 1

        def w_take():
            i = wstate['used']
            wstate['used'] = i + 1
            assert i < wstate['issued']
            return i % WR

        def w_release(n=1):
            for _ in range(n):
                w_issue()

        out_toks = []
        nseg = len(cfg.segs)
        NPF = 2
        bst_state = {'i': 0, 'reserved': set(), 'pre': []}

        def bst_next():
            while True:
                i_ = bst_state['i'] % NBST
                bst_state['i'] += 1
                if i_ not in bst_state['reserved']:
                    return i_

        def seg_blocks(p0_, p1_, hs_):
            out_ = []
            pos_ = p0_
            while pos_ < p1_:
                n_ = min(128, p1_ - pos_)
                srcs_ = []
                q_ = pos_
                while q_ < pos_ + n_:
                    if q_ < NMETA:
                        k_ = min(NMETA, pos_ + n_) - q_
                        srcs_.append((meta[q_:q_ + k_, :], k_))
                    else:
                        k_ = pos_ + n_ - q_
                        srcs_.append((xp[q_ - NMETA:q_ - NMETA + k_, :], k_))
                    q_ += k_
                out_.append((srcs_, n_, pos_ - p0_))
                pos_ += n_
            if hs_:
                for r0_ in range(0, TS, 128):
                    n_ = min(128, TS - r0_)
                    out_.append(([(xs[r0_:r0_ + n_, :], n_)], n_, (p1_ - p0_) + r0_))
            return out_

        def issue_row_dmas(rows_src):
            i_ = bst_next()
            st_ap, st_reg, st_ld, _ = BST[i_]
            r_ = 0
            for (ap_, k_) in rows_src:
                P.dma('sp', st_ap[r_:r_ + k_, :], ap_, st_ld, (), [st_reg])
                r_ += k_
            return i_
        try:
            chk(0)
            _emit_all = True
        except _Stop:
            _emit_all = False
        for si, (p0, p1, hs) in enumerate(cfg.segs if _emit_all else []):
          try:
              _cur['si'] = si
              Tp = p1 - p0
              T = Tp + (TS if hs else 0)
              tiles = seg_tiles(p0, p1, hs)
              nt = cfg.ntile
              for (a, b_, kd) in tiles:
                  assert b_ - a <= NMAX
              first_seg = (p0 == 0)
              last_prompt_seg = (p1 == PT)
              if last_prompt_seg:
                  assert tiles[nt - 1][2] >= CB

              def XR(ti):
                  return ("X", ti)

              def HRL(ti):
                  return [("H", k_, ti) for k_ in range(KC)]

              def load_rows(rows_src, n, c0, pre=None):
                  bi = issue_row_dmas(rows_src) if pre is None else pre
                  st_ap, st_reg, st_ld, _ = BST[bi]
                  bst_state['reserved'].discard(bi)
                  for h in range(0, KC, 4):
                      b = bank()
                      mm_ = list(range(h, min(KC, h + 4)))
                      transposes(b, [(st_ap[0:n, q * 128:(q + 1) * 128], i * 128, n, 128) for i, q in enumerate(mm_)], [st_reg])
                      wr = [("X", ti) for ti, (a, b2, kd) in enumerate(tiles) if not (b2 <= c0 or a >= c0 + n)]
                      act(X[:, h:h + len(mm_), c0:c0 + n], PS[b][:, 0:len(mm_) * 128].rearrange("p (a t) -> p a t", t=128)[:, :, 0:n],
                          AF.Copy, [("ps", b)], wr)

              pre_list = bst_state['pre']
              bst_state['pre'] = []
              for bi_, (srcs, n, c0_) in enumerate(seg_blocks(p0, p1, hs)):
                  load_rows(srcs, n, c0_, pre=(pre_list[bi_] if bi_ < len(pre_list) else None))

              chk(1)
              if si == 0:
                  for _ in range(WR):
                      w_issue()
                  for l_ in range(DEPTH):
                      out_toks.append(P.dma('sp', ncs[l_, :, 0:CB - DS, :], sc[l_, :, DS:CB, :], d2d_sem))
                      out_toks.append(P.dma('sp', nps[l_, :, 0:PB - DS, :], spl[l_, :, DS:PB, :], d2d_sem))

              def rms_stats(ti, a, b_):
                  n = b_ - a
                  bk = bank()
                  for mch in range(KC):
                      act(CC[:, mch, a:b_], X[:, mch, a:b_], AF.Square, [XR(ti)], [("CC", mch, ti)])
                  for mch in range(KC):
                      P.op('pe', lambda e, mch=mch, bk=bk, n=n: e.matmul(PS[bk][:, 0:n], lhsT=ONESB[:], rhs=CC[:, mch, a:b_],
                                                                       start=(mch == 0), stop=(mch == KC - 1)),
                           [("CC", mch, ti), "ONESB"], [("ps", bk)])
                  act(RR[:, a:b_], PS[bk][:, 0:n], AF.Sqrt, [("ps", bk), "EPSR"], [("A1", ti)], bias=EPSR[:, 0:1])
                  P.op('dve', lambda e: e.reciprocal(out=RR[:, a:b_], in_=RR[:, a:b_]), [("A1", ti)], [("A1", ti)])

              def state_groups(l_):
                  gs = []
                  SPG = 128 // CB
                  for g0 in range(0, NS, SPG):
                      ng = min(SPG, NS - g0)
                      gs.append(('c', g0, ng, ng * CB, sc[l_, g0:g0 + ng].rearrange("s j d -> (s j) d")))
                  SPG2 = 128 // PB
                  for g0 in range(0, NS, SPG2):
                      ng = min(SPG2, NS - g0)
                      gs.append(('p', g0, ng, ng * PB, spl[l_, g0:g0 + ng].rearrange("s j d -> (s j) d")))
                  return gs

              def state_dma(l_):
                  pre = []
                  for (kind_, g0, ng, nr, src) in state_groups(l_):
                      if len(bst_state['reserved']) >= NBST:
                          break
                      bi = bst_next()
                      st_ap, st_reg, st_ld, _ = BST[bi]
                      P.dma('sp', st_ap[0:nr, :], src, st_ld, (), [st_reg])
                      bst_state['reserved'].add(bi)
                      pre.append(bi)
                  return pre

              def state_consume(l_, pre):
                  groups_ = state_groups(l_)
                  queue_ = list(pre)

                  def issue_next():
                      gj = len(queue_)
                      if gj < len(groups_):
                          bj = bst_next()
                          P.dma('sp', BST[bj][0][0:groups_[gj][3], :], groups_[gj][4], BST[bj][2], (), [BST[bj][1]])
                          bst_state['reserved'].add(bj)
                          queue_.append(bj)
                  for gi, (kind_, g0, ng, nr, src) in enumerate(groups_):
                      if gi >= len(queue_):
                          issue_next()
                      bi = queue_[gi]
                      st_ap, st_reg = BST[bi][0], BST[bi][1]
                      bst_state['reserved'].discard(bi)
                      J = CB if kind_ == 'c' else PB
                      for h in range(0, KC, 4):
                          b = bank()
                          mm_ = list(range(h, min(KC, h + 4)))
                          transposes(b, [(st_ap[0:nr, q * 128:(q + 1) * 128], i * 128, nr, 128) for i, q in enumerate(mm_)], [st_reg])
                          for i, q in enumerate(mm_):
                              if kind_ == 'c':
                                  dst_, dreg_ = UX[:, q, g0:g0 + ng, 0:CB], ("UX", q)
                              else:
                                  dst_, dreg_ = PX[:, q, g0:g0 + ng, 0:PB], ("PX", q)
                              cp('dve', dst_, PS[b][:, i * 128:i * 128 + nr].rearrange("p (s j) -> p s j", j=J), [("ps", b)], [dreg_])
                      issue_next()

              pre_state = {}
              if hs:
                  pre_state[0] = state_dma(0)

              for l in range(DEPTH):
                  for ti, (a, b_, kd) in enumerate(tiles):
                      rms_stats(ti, a, b_)
                      for mch in range(KC):
                          stt(H[:, mch, a:b_], X[:, mch, a:b_], pv(l, CWID + 3, mch), RR[:, a:b_], ALU.mult, ALU.mult,
                              [XR(ti), ("A1", ti), "PV"], [("H", mch, ti)])

                  chk(2)
                  if hs:
                      state_consume(l, pre_state.get(l, []))

                  chk(20)
                  mset('dve', S1[:, 0:T], 0.0, [("A1", ti) for ti in range(len(tiles))])
                  mset('dve', S2[:, 0:T], 0.0, [("A2", ti) for ti in range(len(tiles))])
                  uslot_of = {}

                  def glu(mch, wv, wg, mloc):
                      us = mch % 2
                      uslot_of[mch] = us
                      ds_ = mch % 3
                      base_r = 36 * l
                      def dg_build(k0, k1):
                          nk = k1 - k0
                          P.op('dve', lambda e: e.tensor_tensor(out=DG[ds_][:, k0:k1, :], in0=IDB[:].unsqueeze(1).broadcast_to([128, nk, 128]),
                                                                in1=PV[:, mch, base_r + k0:base_r + k1].unsqueeze(2).broadcast_to([128, nk, 128]),
                                                                op=ALU.mult), ["IDB", "PV"], [("DG", ds_)])
                      dg_parts = [(0, 11), (11, 21), (21, CWID)]
                      dg_build(*dg_parts[0])
                      if first_seg:
                          mset('dve', U[us][:, 0:CB], 0.0, [("U", us)])
                      else:
                          cp('dve', U[us][:, 0:CB], UT[l][:, mch, :], [("UT", l, mch)], [("U", us)])
                      for ti, (a, b_, kd) in enumerate(tiles):
                          n = b_ - a
                          bg = bank()
                          mm_chain(bg, n, [(W[wg][:, k, mloc * 128:(mloc + 1) * 128], H[:, k, a:b_], [("W", wg), ("H", k, ti)]) for k in range(KC)])
                          bv = bank()
                          mm_chain(bv, n, [(W[wv][:, k, mloc * 128:(mloc + 1) * 128], H[:, k, a:b_], [("W", wv), ("H", k, ti)]) for k in range(KC)])
                          sg = nxt("sg", 4)
                          act(SG[sg][:, 0:n], PS[bg][:, 0:n], AF.Sigmoid, [("ps", bg)], [("SG", sg)])
                          if kd > 0:
                              tt('dve', U[us][:, CB + a:CB + a + kd], PS[bv][:, 0:kd], SG[sg][:, 0:kd], ALU.mult, [("ps", bv), ("SG", sg)], [("U", us)])
                              if last_prompt_seg and a + kd == Tp:
                                  tt('dve', UL[:, mch, :], PS[bv][:, kd - CB:kd], SG[sg][:, kd - CB:kd], ALU.mult, [("ps", bv), ("SG", sg)], [("UL", mch)])
                          if kd < n:
                              tt('dve', UX[:, mch, :, CB:CB + DS], PS[bv][:, kd:n].rearrange("p (s j) -> p s j", j=DS),
                                 SG[sg][:, kd:n].rearrange("p (s j) -> p s j", j=DS), ALU.mult, [("ps", bv), ("SG", sg)], [("UX", mch)])
                              tt('dve', US[:, mch, :], PS[bv][:, kd:n], SG[sg][:, kd:n], ALU.mult, [("ps", bv), ("SG", sg)], [("US", mch)])
                          if ti + 1 < len(dg_parts):
                              dg_build(*dg_parts[ti + 1])
                      for pi_ in range(len(tiles) + 1, len(dg_parts)):
                          dg_build(*dg_parts[pi_])
                      if not last_prompt_seg:
                          cp('dve', UT[l][:, mch, :], U[us][:, Tp:Tp + CB], [("U", us)], [("UT", l, mch)])

                  def conv(mch):
                      us = uslot_of[mch]
                      ds_ = mch % 3
                      for ti, (a, b_, kd) in enumerate(tiles):
                          n = b_ - a
                          bc = bank()
                          if kd > 0:
                              mm_group(bc, kd, [(DG[ds_][:, k, :], U[us][:, a + k:a + kd + k]) for k in range(CWID)], [("DG", ds_), ("U", us)])
                          if kd < n:
                              mm_group(bc, n - kd, [(DG[ds_][:, k, :], UX[:, mch, :, k:k + DS]) for k in range(CWID)], [("DG", ds_), ("UX", mch)], c0=kd)
                          cb_ap = pv(l, CWID + 0, mch)
                          act(CC[:, mch, a:b_], PS[bc][:, 0:n], AF.Identity, [("ps", bc), "PV"], [("CC", mch, ti)], bias=cb_ap)
                          cs = nxt("t2", 4)
                          act(T2[cs][:, 0:n], PS[bc][:, 0:n], AF.Square, [("ps", bc), "PV"], [("T2", cs)], bias=cb_ap)
                          tt('dve', S1[:, a:b_], S1[:, a:b_], CC[:, mch, a:b_], ALU.add, [("CC", mch, ti)], [("A1", ti)])
                          tt('dve', S2[:, a:b_], S2[:, a:b_], T2[cs][:, 0:n], ALU.add, [("T2", cs)], [("A2", ti)])

                  pool_first = [False]

                  def pin(mch, wp, mloc):
                      pslot = mch % 2
                      wwin = WINS[mch // GC]
                      if first_seg:
                          mset('dve', PIN[pslot][:, 0:PB], 0.0, [("PIN", pslot)])
                      else:
                          cp('dve', PIN[pslot][:, 0:PB], PTL[l][:, mch, :], [("PTL", l, mch)], [("PIN", pslot)])
                      for ti, (a, b_, kd) in enumerate(tiles):
                          n = b_ - a
                          bp = bank()
                          mm_group(bp, n, [(W[wp][:, k, mloc * 128:(mloc + 1) * 128], H[:, k, a:b_]) for k in range(KC)], [("W", wp)] + HRL(ti))
                          if kd > 0:
                              act(PIN[pslot][:, PB + a:PB + a + kd], PS[bp][:, 0:kd], AF.Copy, [("ps", bp)], [("PIN", pslot)])
                          if kd < n:
                              act(PX[:, mch, :, PB:PB + DS], PS[bp][:, kd:n].rearrange("p (s j) -> p s j", j=DS), AF.Copy, [("ps", bp)], [("PX", mch)])
                              act(PSN[:, mch, :], PS[bp][:, kd:n], AF.Copy, [("ps", bp)], [("PSN", mch)])
                      if not last_prompt_seg:
                          cp('dve', PTL[l][:, mch, :], PIN[pslot][:, Tp:Tp + PB], [("PIN", pslot)], [("PTL", l, mch)])
                      else:
                          cp('dve', PL15[:, mch, :], PIN[pslot][:, Tp:Tp + PB], [("PIN", pslot)], [("PL15", mch)])

                      def pool_region(xap_fn, E, rd_reg, outs):
                          cur = xap_fn
                          sh = 1
                          lvl = 0
                          cur_reg = rd_reg
                          while sh < wwin:
                              dst = TA if lvl % 2 == 0 else TB
                              dreg = "MGALL"
                              lo = 2 * sh - 1
                              wr_ = [dreg]
                              if not pool_first[0]:
                                  pool_first[0] = True
                                  wr_ = [dreg] + [("MG", k_, t_) for k_ in range(KC) for t_ in range(len(tiles))]
                              tt('dve', dst[:, lo:E], cur(lo, E), cur(lo - sh, E - sh), ALU.add, [cur_reg], wr_)
                              cur = (lambda d: (lambda lo_, hi_: d[:, lo_:hi_]))(dst)
                              cur_reg = dreg
                              sh *= 2
                              lvl += 1
                          outs(cur, cur_reg)

                      E = PB + Tp

                      def outs_p(cur, cur_reg, pslot=pslot, mch=mch, wwin=wwin):
                          for ti, (a, b_, kd) in enumerate(tiles):
                              if kd == 0:
                                  continue
                              stt(QF[:, mch, a:a + kd], cur(PB + a, PB + a + kd), 1.0 / wwin, PIN[pslot][:, PB + a:PB + a + kd], ALU.mult, ALU.subtract,
                                  [cur_reg, ("PIN", pslot)], [("QF", mch, ti)])
                          if first_seg:
                              nf = wwin - 1
                              tt('dve', T1[0][:, 0:nf], cur(PB, PB + nf), INV[:, 0:nf], ALU.mult, [cur_reg, "INV"], [("T1", 0)])
                              tt('dve', QF[:, mch, 0:nf], T1[0][:, 0:nf], PIN[pslot][:, PB:PB + nf], ALU.subtract, [("T1", 0), ("PIN", pslot)], [("QF", mch, 0)])
                      pool_region(lambda lo, hi, pslot=pslot: PIN[pslot][:, lo:hi], E, ("PIN", pslot), outs_p)
                      if hs:
                          E2 = NS * (PB + DS)
                          flat = PX[:, mch, :, :].rearrange("p s j -> p (s j)")

                          def outs_s(cur, cur_reg, mch=mch, wwin=wwin):
                              ti = len(tiles) - 1
                              a, b_, kd = tiles[ti]
                              cur3 = cur(0, E2).rearrange("p (s j) -> p s j", j=PB + DS)
                              stt(QF[:, mch, a + kd:b_].rearrange("p (s j) -> p s j", j=DS), cur3[:, :, PB:PB + DS], 1.0 / wwin, PX[:, mch, :, PB:PB + DS],
                                  ALU.mult, ALU.subtract, [cur_reg, ("PX", mch)], [("QF", mch, ti)])
                          pool_region(lambda lo, hi, flat=flat: flat[:, lo:hi], E2, ("PX", mch), outs_s)

                  wv = wg = wp = None
                  for mch in range(KC):
                      if mch % MB == 0:
                          wv = w_take()
                          wg = w_take()
                          wp = w_take()
                      glu(mch, wv, wg, mch % MB)
                      if mch >= 1:
                          conv(mch - 1)
                      pin(mch, wp, mch % MB)
                      if mch % MB == MB - 1:
                          w_release(3)
                  conv(KC - 1)

                  chk(3)
                  def out_rows_from_fm(src_fn, nrows, dst_fn, rd):
                      s = nxt("stg", 2)
                      for h in range(0, KC, 4):
                          b = bank()
                          mm_ = list(range(h, min(KC, h + 4)))
                          transposes(b, [(src_fn(q), i * 128, 128, nrows) for i, q in enumerate(mm_)], [r for q in mm_ for r in rd(q)])
                          act(STG[s][0:nrows, h * 128:(h + len(mm_)) * 128], PS[b][0:nrows, 0:len(mm_) * 128], AF.Copy, [("ps", b)], [("STG", s)])
                      return dst_fn(s)

                  if last_prompt_seg:
                      out_toks.append(out_rows_from_fm(lambda q: UL[:, q, :], CB,
                                                       lambda s: P.dma('sp', ncp[l], STG[s][0:CB, :], stg_st[s], [("STG", s)], ()),
                                                       lambda q: [("UL", q)]))
                  if hs:
                      for r0 in range(0, TS, 128):
                          n = min(128, TS - r0)

                          def dst(s, r0=r0, n=n):
                              tk = None
                              for sq_ in range(r0 // DS, (r0 + n) // DS):
                                  tk = P.dma('sp', ncs[l, sq_, CB - DS:CB, :], STG[s][(sq_ * DS - r0):(sq_ * DS - r0) + DS, :], stg_st[s], [("STG", s)], ())
                              return tk
                          out_toks.append(out_rows_from_fm(lambda q, r0=r0, n=n: US[:, q, r0:r0 + n], n, dst, lambda q: [("US", q)]))

                  chk(30)
                  for ti, (a, b_, kd) in enumerate(tiles):
                      n = b_ - a
                      b1 = bank()
                      P.op('pe', lambda e, b1=b1, a=a, b_=b_, n=n: e.matmul(PS[b1][:, 0:n], lhsT=ONES32[:], rhs=S1[:, a:b_], start=True, stop=True),
                           [("A1", ti), "ONES32"], [("ps", b1)])
                      b2 = bank()
                      P.op('pe', lambda e, b2=b2, a=a, b_=b_, n=n: e.matmul(PS[b2][:, 0:n], lhsT=ONES32[:], rhs=S2[:, a:b_], start=True, stop=True),
                           [("A2", ti), "ONES32"], [("ps", b2)])
                      cp('dve', MU[:, a:b_], PS[b1][:, 0:n], [("ps", b1)], [("A1", ti)])
                      tt('dve', RS[:, a:b_], MU[:, a:b_], MU[:, a:b_], ALU.mult, [("A1", ti)], [("A2", ti)])
                      tt('dve', RS[:, a:b_], PS[b2][:, 0:n], RS[:, a:b_], ALU.subtract, [("ps", b2), ("A2", ti)], [("A2", ti)])
                      act(RS[:, a:b_], RS[:, a:b_], AF.Sqrt, [("A2", ti), "EPSL"], [("A2", ti)], bias=EPSL[:, 0:1])
                      P.op('dve', lambda e, a=a, b_=b_: e.reciprocal(out=RS[:, a:b_], in_=RS[:, a:b_]), [("A2", ti)], [("A2", ti)])
                  pend_fin = [None]
                  for blk in range(NB):
                      ws = w_take()
                      wa = w_take()
                      chs = list(range(blk * MB, (blk + 1) * MB))
                      if blk == NB - 1:
                          order = [(m_, t_) for t_ in range(len(tiles)) for m_ in chs]
                      else:
                          order = [(m_, t_) for m_ in chs for t_ in range(len(tiles))]
                      for (mch, ti) in order:
                          mloc = mch % MB
                          a, b_, kd = tiles[ti]
                          n = b_ - a
                          bs = bank()
                          mm_group(bs, n, [(W[ws][:, k, mloc * 128:(mloc + 1) * 128], H[:, k, a:b_]) for k in range(KC)], [("W", ws)] + HRL(ti))
                          bg = bank()
                          mm_group(bg, n, [(W[wa][:, k, mloc * 128:(mloc + 1) * 128], H[:, k, a:b_]) for k in range(KC)], [("W", wa)] + HRL(ti))
                          sg = nxt("sg", 4)
                          act(SG[sg][:, 0:n], PS[bs][:, 0:n], AF.Silu, [("ps", bs)], [("SG", sg)])
                          act(MG[:, mch, a:b_], PS[bg][:, 0:n], AF.Tanh, [("ps", bg)], [("MG", mch, ti), "MGALL"], scale=0.5)
                          t1 = nxt("t1", 4)
                          tt('dve', T1[t1][:, 0:n], CC[:, mch, a:b_], MU[:, a:b_], ALU.subtract, [("CC", mch, ti), ("A1", ti)], [("T1", t1)])
                          tt('dve', T1[t1][:, 0:n], T1[t1][:, 0:n], RS[:, a:b_], ALU.mult, [("T1", t1), ("A2", ti)], [("T1", t1)])
                          t2 = nxt("t2", 4)
                          act(T2[t2][:, 0:n], T1[t1][:, 0:n], AF.Silu, [("T1", t1), "PV"], [("T2", t2)], bias=pv(l, CWID + 2, mch), scale=pv(l, CWID + 1, mch))
                          if pend_fin[0] is not None:
                              pend_fin[0]()

                          def fin_(mch=mch, a=a, b_=b_, n=n, t2=t2, sg=sg, ti=ti):
                              tt('dve', CC[:, mch, a:b_], T2[t2][:, 0:n], SG[sg][:, 0:n], ALU.mult, [("T2", t2), ("SG", sg)], [("CC", mch, ti)])
                          pend_fin[0] = fin_
                      w_release(2)
                  pend_fin[0]()

                  chk(6)
                  if last_prompt_seg:
                      out_toks.append(out_rows_from_fm(lambda q: PL15[:, q, :], PB,
                                                       lambda s: P.dma('sp', npp[l], STG[s][0:PB, :], stg_st[s], [("STG", s)], ()),
                                                       lambda q: [("PL15", q)]))
                  if hs:
                      for r0 in range(0, TS, 128):
                          n = min(128, TS - r0)

                          def dst2(s, r0=r0, n=n):
                              tk = None
                              for sq_ in range(r0 // DS, (r0 + n) // DS):
                                  tk = P.dma('sp', nps[l, sq_, PB - DS:PB, :], STG[s][(sq_ * DS - r0):(sq_ * DS - r0) + DS, :], stg_st[s], [("STG", s)], ())
                              return tk
                          out_toks.append(out_rows_from_fm(lambda q, r0=r0, n=n: PSN[:, q, r0:r0 + n], n, dst2, lambda q: [("PSN", q)]))

                  chk(4)
                  wcs = [w_take() for _ in range(NB)]
                  for ti, (a, b_, kd) in enumerate(tiles):
                      n = b_ - a
                      for mch in range(KC):
                          wc = wcs[mch // MB]
                          mloc = mch % MB
                          ba = bank()
                          mm_group(ba, n, [(W[wc][:, k, mloc * 128:(mloc + 1) * 128], CC[:, k, a:b_]) for k in range(KC)],
                                   [("W", wc)] + [("CC", k, ti) for k in range(KC)])
                          stt(MG[:, mch, a:b_], MG[:, mch, a:b_], 1.0, PS[ba][:, 0:n], ALU.add, ALU.mult, [("ps", ba)], [("MG", mch, ti)])
                          if ti == len(tiles) - 1 and mch % MB == MB - 1:
                              w_release(1)

                  chk(60)
                  wms = None
                  wps = None
                  for mch in range(KC):
                      if mch % MB == 0:
                          wps = w_take()
                          wms = [w_take() for _ in range(MPB)]
                      mloc = mch % MB
                      g = mch // GC
                      wm = wms[mloc // GC]
                      for ti, (a, b_, kd) in enumerate(tiles):
                          n = b_ - a
                          bq = bank()
                          mm_group(bq, n, [(W[wm][:, kk, (mch % GC) * 128:(mch % GC + 1) * 128], QF[:, g * GC + kk, a:b_]) for kk in range(GC)],
                                   [("W", wm)] + [("QF", g * GC + kk, ti) for kk in range(GC)])
                          bs = bank()
                          mm_group(bs, n, [(W[wps][:, k, mloc * 128:(mloc + 1) * 128], H[:, k, a:b_]) for k in range(KC)], [("W", wps)] + HRL(ti))
                          sg = nxt("sg", 4)
                          act(SG[sg][:, 0:n], PS[bs][:, 0:n], AF.Silu, [("ps", bs)], [("SG", sg)])
                          stt(CC[:, mch, a:b_], PS[bq][:, 0:n], pv(l, CWID + 4, mch), SG[sg][:, 0:n], ALU.mult, ALU.mult,
                              [("ps", bq), ("SG", sg), "PV"], [("CC", mch, ti)])
                      if mch % MB == MB - 1:
                          w_release(1 + MPB)

                  chk(7)
                  wo_ = wb = None
                  for mch in range(KC):
                      if mch % MB == 0:
                          wb = w_take()
                      mloc = mch % MB
                      for ti, (a, b_, kd) in enumerate(tiles):
                          n = b_ - a
                          bg = bank()
                          mm_group(bg, n, [(W[wb][:, k, mloc * 128:(mloc + 1) * 128], H[:, k, a:b_]) for k in range(KC)], [("W", wb)] + HRL(ti))
                          act(QF[:, mch, a:b_], PS[bg][:, 0:n], AF.Sigmoid, [("ps", bg)], [("QF", mch, ti)])
                      if mch % MB == MB - 1:
                          w_release(1)
                  for mch in range(KC):
                      if mch % MB == 0:
                          wo_ = w_take()
                      mloc = mch % MB
                      for ti, (a, b_, kd) in enumerate(tiles):
                          n = b_ - a
                          bb = bank()
                          mm_group(bb, n, [(W[wo_][:, k, mloc * 128:(mloc + 1) * 128], CC[:, k, a:b_]) for k in range(KC)],
                                   [("W", wo_)] + [("CC", k, ti) for k in range(KC)])
                          t1 = nxt("t1", 4)
                          tt('dve', T1[t1][:, 0:n], PS[bb][:, 0:n], QF[:, mch, a:b_], ALU.mult, [("ps", bb), ("QF", mch, ti)], [("T1", t1)])
                          stt(MG[:, mch, a:b_], T1[t1][:, 0:n], 2.0, MG[:, mch, a:b_], ALU.mult, ALU.add, [("T1", t1)], [("MG", mch, ti)])
                      if mch % MB == MB - 1:
                          w_release(1)

                  chk(8)
                  wws = [w_take() for _ in range(NB)]
                  if hs and l + 1 < DEPTH:
                      pre_state[l + 1] = state_dma(l + 1)
                  for ti, (a, b_, kd) in enumerate(tiles):
                      n = b_ - a
                      for mch in range(KC):
                          ww = wws[mch // MB]
                          mloc = mch % MB
                          bo = bank()
                          mm_group(bo, n, [(W[ww][:, k, mloc * 128:(mloc + 1) * 128], MG[:, k, a:b_]) for k in range(KC)],
                                   [("W", ww)] + [("MG", k, ti) for k in range(KC)])
                          stt(X[:, mch, a:b_], PS[bo][:, 0:n], 0.5, X[:, mch, a:b_], ALU.mult, ALU.add, [("ps", bo)], [XR(ti)])
                          if ti == len(tiles) - 1 and mch % MB == MB - 1:
                              w_release(1)

              chk(9)
              for ti, (a, b_, kd) in enumerate(tiles):
                  rms_stats(ti, a, b_)
                  for mch in range(KC):
                      stt(X[:, mch, a:b_], X[:, mch, a:b_], PV[:, mch, R - 1:R], RR[:, a:b_], ALU.mult, ALU.mult,
                          [XR(ti), ("A1", ti), "PV"], [XR(ti)])

              def xregs(c0, n):
                  return [("X", ti) for ti, (a, b2, kd) in enumerate(tiles) if not (b2 <= c0 or a >= c0 + n)]

              def store_rows(c0, n, dst_fn):
                  bst = BST[bst_next()]
                  st_ap, st_reg = bst[0], bst[1]
                  for h in range(0, KC, 4):
                      b = bank()
                      mm_ = list(range(h, min(KC, h + 4)))
                      transposes(b, [(X[:, q, c0:c0 + n], i * 128, 128, n) for i, q in enumerate(mm_)], xregs(c0, n))
                      act(st_ap[0:n, h * 128:(h + len(mm_)) * 128], PS[b][0:n, 0:len(mm_) * 128], AF.Copy, [("ps", b)], [st_reg])
                  return dst_fn(bst)

              if si + 1 < nseg and NBST > NPF + 1:
                  for (srcs_n, n_n, c0_n) in seg_blocks(*cfg.segs[si + 1])[:NPF]:
                      bi_n = issue_row_dmas(srcs_n)
                      bst_state['reserved'].add(bi_n)
                      bst_state['pre'].append(bi_n)
              pos = p0
              while pos < p1:
                  n = min(128, p1 - pos)
                  lo = max(pos, NMETA)
                  if lo < pos + n:
                      def dsty(bst, pos=pos, n=n, lo=lo):
                          return P.dma('sp', yp[lo - NMETA:pos + n - NMETA, :], bst[0][lo - pos:n, :], bst[3], [bst[1]], ())
                      out_toks.append(store_rows(pos - p0, n, dsty))
                  pos += n
              chk(10)
              if hs:
                  for r0 in range(0, TS, 128):
                      n = min(128, TS - r0)

                      def dsts(bst, r0=r0, n=n):
                          return P.dma('sp', ys[r0:r0 + n, :], bst[0][0:n, :], bst[3], [bst[1]], ())
                      out_toks.append(store_rows(Tp + r0, n, dsts))
          except _Stop:
            break

        finals = {}
        for tk in out_toks:
            if tk is None:
                continue
            k = tk[0].name
            tot = P.dcnt[k]
            finals[k] = (tk[0], tot)
        P.wait_all('sp', list(finals.values()))

        @block.sync
        def _(e):
            P.replay('sp', e)

        @block.gpsimd
        def _(e):
            P.replay('pool', e)

        @block.tensor
        def _(e):
            P.replay('pe', e)

        @block.scalar
        def _(e):
            P.replay('act', e)

        @block.vector
        def _(e):
            P.replay('dve', e)
    return nc


def make_in_maps(cfg, ncores, x_prompt, x_sample, state_conv, state_pool, meta_tokens, norm_g, w_in, conv_w, conv_b,
                 ln_g, ln_b, w_conv_out, w_pool_mix, pool_scale, w_pool_out, w_out, final_g):
    f = lambda a: np.ascontiguousarray(np.asarray(a, dtype=np.float32))
    NS, D = cfg.NS, cfg.D
    KC, BW, GC = cfg.KC, cfg.BW, cfg.GC
    PG = D // 4

    def pack_blocks(w):
        w = f(w)
        L, _, C = w.shape
        return np.ascontiguousarray(w.reshape(L, KC, 128, C // BW, BW).transpose(0, 3, 2, 1, 4)).reshape(L, C // BW, 128, KC * BW)

    def pack_mix(w):
        w = f(w)
        L = w.shape[0]
        return np.ascontiguousarray(w.reshape(L, 4, GC, 128, PG).transpose(0, 1, 3, 2, 4)).reshape(L, 4, 128, GC * PG)

    shared = dict(meta=f(meta_tokens), norm_g=f(norm_g), w_in=pack_blocks(w_in), conv_w=f(conv_w), conv_b=f(conv_b), ln_g=f(ln_g),
                  ln_b=f(ln_b), w_co=pack_blocks(w_conv_out), w_mix=pack_mix(w_pool_mix), pscale=f(pool_scale),
                  w_po=pack_blocks(w_pool_out), w_o=pack_blocks(w_out), final_g=f(final_g).reshape(1, D))
    x_prompt = np.asarray(x_prompt); x_sample = np.asarray(x_sample)
    state_conv = np.asarray(state_conv); state_pool = np.asarray(state_pool)
    maps = []
    for c in range(ncores):
        m = dict(shared)
        m["xp"] = f(x_prompt[c])
        m["xs"] = f(x_sample[c * NS:(c + 1) * NS]).reshape(NS * DS, D)
        m["sc"] = f(state_conv[:, c * NS:(c + 1) * NS])
        m["spl"] = f(state_pool[:, c * NS:(c + 1) * NS])
        maps.append(m)
    return maps


def gather(cfg, results):
    NS, D = cfg.NS, cfg.D
    y_prompt = np.stack([np.asarray(r["yp"]) for r in results], axis=0).astype(np.float32)
    y_sample = np.concatenate([np.asarray(r["ys"]).reshape(NS, DS, D) for r in results], axis=0).astype(np.float32)
    ncp = np.stack([np.asarray(r["ncp"]) for r in results], axis=1).astype(np.float32)
    npp = np.stack([np.asarray(r["npp"]) for r in results], axis=1).astype(np.float32)
    ncs = np.concatenate([np.asarray(r["ncs"]) for r in results], axis=1).astype(np.float32)
    nps = np.concatenate([np.asarray(r["nps"]) for r in results], axis=1).astype(np.float32)
    return (y_prompt, y_sample, ncp, npp, ncs, nps)


def kernel(**inputs):
    cfg = REAL
    nc = build_program(cfg)
    maps = make_in_maps(cfg, 8, **inputs)
    res = run_bass_kernel_spmd(nc, maps, core_ids=list(range(8)))
    return gather(cfg, res.results)
```

```python
import numpy as np
from contextlib import ExitStack
import concourse.bass as bass
import concourse.mybir as mybir
from concourse.bass_utils import run_bass_kernel_spmd

F32 = mybir.dt.float32
BF16 = mybir.dt.bfloat16
AF = mybir.ActivationFunctionType
ALU = mybir.AluOpType

ENGS = ['pe', 'act', 'dve', 'pool', 'sp']
CWID = 31
CB = 30
PB = 15
WINS = (2, 4, 8, 16)
NMETA = 16
DS = 8
RMS_EPS = 1e-6
LN_EPS = 1e-5
DEPTH = 2


class Cfg:
    def __init__(self, D=1024, SEQ=2048, NS=16, segs=None, ntile=2, BW=256, WR=6, merge_sample=True, wscratch=True):
        self.D = D
        self.KC = D // 128
        self.SEQ = SEQ
        self.PT = SEQ + NMETA
        self.NS = NS
        self.TS = NS * DS
        self.GC = (D // 4) // 128
        assert self.GC >= 1
        self.segs = segs
        self.ntile = ntile
        self.BW = BW
        self.merge_sample = merge_sample
        self.wscratch = wscratch
        self.WR = WR
        self.MB = BW // 128
        self.NB = D // BW
        self.R = 36 * DEPTH + 1


REAL = Cfg(segs=[(0, 736, False), (736, 1472, False), (1472, 2064, True)], WR=8)


class _Stop(Exception):
    pass


_cur = {'si': 0}


def chk(n):
    return


class Prog:
    def __init__(self, sems):
        self.ops = {e: [] for e in ENGS}
        self.cnt = {e: 0 for e in ENGS}
        self.sem = sems
        self.waited = {e: {} for e in ENGS}
        self.dcnt = {}
        self.lastw = {}
        self.readers = {}

    def _waits(self, eng, deps):
        waits = {}
        for d in deps:
            if d is None:
                continue
            sem, val = d
            key = sem.name
            if eng == 'pe' and key == self.sem['pe'].name:
                continue
            if self.waited[eng].get(key, 0) >= val:
                continue
            if key in waits and waits[key][1] >= val:
                continue
            waits[key] = (sem, val)
        for key, (sem, val) in waits.items():
            self.waited[eng][key] = val
        return list(waits.values())

    def _deps(self, reads, writes):
        deps = []
        for r in reads:
            deps.append(self.lastw.get(r))
        for w in writes:
            deps.append(self.lastw.get(w))
            deps.extend(self.readers.get(w, {}).values())
        return deps

    def _mark(self, tok, reads, writes):
        key = tok[0].name
        for r in reads:
            d = self.readers.setdefault(r, {})
            if key not in d or d[key][1] < tok[1]:
                d[key] = tok
        for w in writes:
            self.lastw[w] = tok
            self.readers[w] = {}

    def op(self, eng, fn, reads=(), writes=()):
        return self.group(eng, [fn], reads, writes)

    def group(self, eng, fns, reads=(), writes=()):
        waits = self._waits(eng, self._deps(reads, writes))
        self.cnt[eng] += 1
        tok = (self.sem[eng], self.cnt[eng])
        n = len(fns)
        for i, fn in enumerate(fns):
            self.ops[eng].append((waits if i == 0 else [], fn, self.sem[eng] if i == n - 1 else None, 1))
        self._mark(tok, reads, writes)
        return tok

    def chain(self, eng, items, writes=()):
        self.cnt[eng] += 1
        tok = (self.sem[eng], self.cnt[eng])
        n = len(items)
        allreads = []
        for i, (fn, reads) in enumerate(items):
            deps = [self.lastw.get(r) for r in reads]
            if i == 0:
                for w in writes:
                    deps.append(self.lastw.get(w))
                    deps.extend(self.readers.get(w, {}).values())
            waits = self._waits(eng, deps)
            self.ops[eng].append((waits, fn, self.sem[eng] if i == n - 1 else None, 1))
            allreads.extend(reads)
        self._mark(tok, allreads, writes)
        return tok

    def dma(self, eng, out, in_, sem, reads=(), writes=()):
        waits = self._waits(eng, self._deps(reads, writes))
        key = sem.name
        self.dcnt[key] = self.dcnt.get(key, 0) + 16
        tok = (sem, self.dcnt[key])
        self.ops[eng].append((waits, lambda e: e.dma_start(out=out, in_=in_), sem, 16))
        self._mark(tok, reads, writes)
        return tok

    def wait_all(self, eng, toks):
        waits = self._waits(eng, toks)
        self.ops[eng].append((waits, None, None, 0))

    def replay(self, eng, e):
        for waits, fn, sem, inc in self.ops[eng]:
            for (s, v) in waits:
                e.wait_ge(s, v)
            if fn is not None:
                inst = fn(e)
                if sem is not None:
                    inst.then_inc(sem, inc)


def build_program(cfg):
    D, KC, PT, NS, TS, GC = cfg.D, cfg.KC, cfg.PT, cfg.NS, cfg.TS, cfg.GC
    BW, MB, NB, R = cfg.BW, cfg.MB, cfg.NB, cfg.R
    PG = D // 4
    nc = bass.Bass("TRN2", target_bir_lowering=False)

    def din(name, shape):
        return nc.dram_tensor(name, shape, F32, kind="ExternalInput").ap()

    def dout(name, shape):
        return nc.dram_tensor(name, shape, F32, kind="ExternalOutput").ap()

    xp = din("xp", [cfg.SEQ, D])
    xs = din("xs", [TS, D])
    sc = din("sc", [DEPTH, NS, CB, D])
    spl = din("spl", [DEPTH, NS, PB, D])
    meta = din("meta", [NMETA, D])
    norm_g = din("norm_g", [DEPTH, D])
    w_in = din("w_in", [DEPTH, 7 * D // BW, 128, KC * BW])
    conv_w = din("conv_w", [DEPTH, CWID, D])
    conv_b = din("conv_b", [DEPTH, D])
    ln_g = din("ln_g", [DEPTH, D])
    ln_b = din("ln_b", [DEPTH, D])
    w_co = din("w_co", [DEPTH, D // BW, 128, KC * BW])
    w_mix = din("w_mix", [DEPTH, 4, 128, GC * PG])
    pscale = din("pscale", [DEPTH, D])
    w_po = din("w_po", [DEPTH, D // BW, 128, KC * BW])
    w_o = din("w_o", [DEPTH, D // BW, 128, KC * BW])
    final_g = din("final_g", [1, D])
    yp = dout("yp", [cfg.SEQ, D])
    ys = dout("ys", [TS, D])
    ncp = dout("ncp", [DEPTH, CB, D])
    npp = dout("npp", [DEPTH, PB, D])
    ncs = dout("ncs", [DEPTH, NS, CB, D])
    nps = dout("nps", [DEPTH, NS, PB, D])

    def seg_tiles(p0, p1, hs):
        Tp_ = p1 - p0
        tl = []
        nt_ = cfg.ntile
        base_ = (Tp_ // nt_) // 2 * 2
        c_ = 0
        for i_ in range(nt_):
            c1_ = Tp_ if i_ == nt_ - 1 else c_ + base_
            tl.append((c_, c1_, c1_ - c_))
            c_ = c1_
        if hs:
            la, lb, lnp = tl[-1]
            if (lb - la) + TS <= 512 and cfg.merge_sample:
                tl[-1] = (la, lb + TS, lnp)
            else:
                tl.append((Tp_, Tp_ + TS, 0))
        return tl

    TW = max(b_ - a_ for sg_ in cfg.segs for (a_, b_, _) in seg_tiles(*sg_))
    TW = (TW + 7) // 8 * 8
    TPmax = max(p1 - p0 for (p0, p1, _) in cfg.segs)
    Tmax = max((p1 - p0) + (TS if hs else 0) for (p0, p1, hs) in cfg.segs)
    NMAX = 512
    EU = CB + TPmax
    EP = PB + TPmax
    EPS_ = max(EP, NS * (PB + DS))

    with ExitStack() as st:
        def sbuf(name, shape, dt):
            return st.enter_context(nc.sbuf_tensor(name, shape, dt))

        def sem(name):
            return st.enter_context(nc.semaphore(name))

        X = sbuf("X", [128, KC, Tmax], F32)
        H = sbuf("H", [128, KC, Tmax], BF16)
        CC = sbuf("CC", [128, KC, Tmax], BF16)
        QF = sbuf("QF", [128, KC, Tmax], BF16)
        MG = sbuf("MG", [128, KC, Tmax], BF16)
        U = [sbuf(f"U{i}", [128, EU], BF16) for i in range(2)]
        UX = sbuf("UX", [128, KC, NS, CB + DS], BF16)
        PX = sbuf("PX", [128, KC, NS, PB + DS], F32)
        PIN = [sbuf(f"PIN{i}", [128, EP], F32) for i in range(2)]
        if KC * Tmax // 2 >= 2 * EPS_:
            MGf = MG[:].rearrange("p k t -> p (k t)").bitcast(F32)
            TA = MGf[:, 0:EPS_]
            TB = MGf[:, EPS_:2 * EPS_]
        else:
            TA = sbuf("TA", [128, EPS_], F32)[:]
            TB = sbuf("TB", [128, EPS_], F32)[:]
        WR = cfg.WR
        W = [sbuf(f"W{i}", [128, KC, BW], BF16) for i in range(WR)]
        DG = [sbuf(f"DG{i}", [128, CWID, 128], BF16) for i in range(3)]
        A1 = sbuf("A1", [128, Tmax], F32)
        A2 = sbuf("A2", [128, Tmax], F32)
        S1 = MU = RR = A1
        S2 = RS = A2
        SG = [sbuf(f"SG{i}", [128, TW], F32) for i in range(4)]
        T1 = [sbuf(f"T1_{i}", [128, TW], F32) for i in range(4)]
        T2 = [sbuf(f"T2_{i}", [128, TW], F32) for i in range(4)]
        STG = [sbuf(f"STG{i}", [128, D], F32) for i in range(2)]
        PV = sbuf("PV", [128, KC, R], F32)
        ID32 = sbuf("ID32", [128, 128], F32)
        IDB = sbuf("IDB", [128, 128], BF16)
        ONESB = sbuf("ONESB", [128, 128], BF16)
        ONES32 = sbuf("ONES32", [128, 128], F32)
        EPSR = sbuf("EPSR", [128, 1], F32)
        EPSL = sbuf("EPSL", [128, 1], F32)
        INV = sbuf("INV", [128, PB], F32)
        UL = sbuf("UL", [128, KC, CB], F32)
        PL15 = sbuf("PL15", [128, KC, PB], F32)
        US = sbuf("US", [128, KC, TS], F32)
        PSN = sbuf("PSN", [128, KC, TS], F32)
        UT = [sbuf(f"UT{l}", [128, KC, CB], BF16) for l in range(DEPTH)]
        PTL = [sbuf(f"PTL{l}", [128, KC, PB], F32) for l in range(DEPTH)]
        PS = [st.enter_context(nc.psum_tensor(f"ps{i}", [128, NMAX], F32)) for i in range(8)]

        sems = {e: sem("s_" + e) for e in ENGS}
        wsem = [sem(f"w{i}") for i in range(WR)]
        stg_ld = [sem(f"stl{i}") for i in range(2)]
        stg_st = [sem(f"sts{i}") for i in range(2)]
        misc_sem = sem("misc")
        BST = [(STG[i][:], ("STG", i), stg_ld[i], stg_st[i]) for i in range(2)]
        if CWID * 128 * 2 >= D * 4:
            for i in range(3):
                dgf = DG[i][:].rearrange("p k c -> p (k c)").bitcast(F32)
                BST.append((dgf[:, 0:D], ("DG", i), sem(f"bld{i}"), sem(f"bst{i}")))
        NBST = len(BST)
        d2d_sem = sem("d2d")
        block = st.enter_context(nc.Block())
        P = Prog(sems)

        psi = [0]

        def bank():
            b = psi[0] % 8
            psi[0] += 1
            return b

        rot = {}

        def nxt(name, n):
            v = rot.get(name, 0)
            rot[name] = v + 1
            return v % n

        def act(out, in_, func, reads, writes, bias=None, scale=None):
            kw = {}
            if bias is not None:
                kw['bias'] = bias
            if scale is not None:
                kw['scale'] = scale
            return P.op('act', lambda e: e.activation(out=out, in_=in_, func=func, **kw), reads, writes)

        def tt(eng, out, in0, in1, op, reads, writes):
            return P.op(eng, lambda e: e.tensor_tensor(out=out, in0=in0, in1=in1, op=op), reads, writes)

        def stt(out, in0, scalar, in1, op0, op1, reads, writes):
            return P.op('dve', lambda e: e.scalar_tensor_tensor(out=out, in0=in0, scalar=scalar, in1=in1, op0=op0, op1=op1), reads, writes)

        def cp(eng, out, in_, reads, writes):
            return P.op(eng, lambda e: e.tensor_copy(out=out, in_=in_), reads, writes)

        def mset(eng, ap, val, writes):
            return P.op(eng, lambda e: e.memset(ap, val), (), writes)

        def mm_group(b, n, pairs, reads, c0=0):
            out = PS[b][:, c0:c0 + n]
            fns = []
            last = len(pairs) - 1
            for i, (l, r) in enumerate(pairs):
                fns.append(lambda e, l=l, r=r, i=i: e.matmul(out, lhsT=l, rhs=r, start=(i == 0), stop=(i == last)))
            return P.group('pe', fns, reads, [("ps", b)])

        def mm_chain(b, n, triples, c0=0):
            out = PS[b][:, c0:c0 + n]
            last = len(triples) - 1
            items = []
            for i, (l_, r_, rd_) in enumerate(triples):
                items.append((lambda e, l_=l_, r_=r_, i=i: e.matmul(out, lhsT=l_, rhs=r_, start=(i == 0), stop=(i == last)), rd_))
            return P.chain('pe', items, [("ps", b)])

        def transposes(b, items, reads):
            fns = []
            for (in_ap, off, r, c) in items:
                fns.append(lambda e, in_ap=in_ap, off=off, r=r, c=c: e.transpose(out=PS[b][0:c, off:off + r], in_=in_ap, identity=ID32[0:r, 0:r]))
            return P.group('pe', fns, reads + ["ID32"], [("ps", b)])

        mset('pool', ID32[:], 0.0, ["ID32"])
        P.op('pool', lambda e: e.affine_select(out=ID32[:], in_=ID32[:], pattern=[[-1, 128]], compare_op=ALU.not_equal,
                                               fill=1.0, base=0, channel_multiplier=1), ["ID32"], ["ID32"])
        cp('dve', IDB[:], ID32[:], ["ID32"], ["IDB"])
        mset('pool', ONESB[:], 1.0 / D, ["ONESB"])
        mset('pool', ONES32[:], 1.0 / D, ["ONES32"])
        mset('pool', EPSR[:], RMS_EPS, ["EPSR"])
        mset('pool', EPSL[:], LN_EPS, ["EPSL"])
        for t in range(PB):
            mset('pool', INV[:, t:t + 1], 1.0 / (t + 1), ["INV"])
        d2d = []
        VEC, vreg, vsem, _ = BST[-1]
        vregs = []

        def vload(dst, src):
            reg = ("VECR", len(vregs))
            vregs.append(reg)
            P.dma('sp', dst, src, vsem, (), [reg])
        for l in range(DEPTH):
            b0 = 36 * l
            vload(VEC[b0:b0 + CWID, :], conv_w[l])
            for j, src in enumerate([conv_b, ln_g, ln_b, norm_g, pscale]):
                vload(VEC[b0 + CWID + j:b0 + CWID + j + 1, :], src[l:l + 1, :])
        vload(VEC[R - 1:R, :], final_g)
        per_bank = max(1, NMAX // R)
        m = 0
        while m < KC:
            b = bank()
            ms = list(range(m, min(KC, m + per_bank)))
            transposes(b, [(VEC[0:R, mm * 128:(mm + 1) * 128], i * R, R, 128) for i, mm in enumerate(ms)], vregs + [vreg])
            for i, mm in enumerate(ms):
                cp('dve', PV[:, mm, :], PS[b][:, i * R:(i + 1) * R], [("ps", b)], ["PV"])
            m += per_bank

        def pv(l, row, mchunk):
            r = 36 * l + row
            return PV[:, mchunk, r:r + 1]

        KH = max(1, KC // 2)
        assert MB % GC == 0 and PG <= BW
        MPB = MB // GC

        def wsrc_in(l, c0):
            v = w_in[l, c0 // BW].rearrange("p (k e) -> p k e", k=KC)
            return lambda slot: [(W[slot][:, k0:k0 + KH, 0:BW], v[:, k0:k0 + KH, :]) for k0 in range(0, KC, KH)]

        def wsrc_sq(wt, l, c0):
            v = wt[l, c0 // BW].rearrange("p (k e) -> p k e", k=KC)
            return lambda slot: [(W[slot][:, k0:k0 + KH, 0:BW], v[:, k0:k0 + KH, :]) for k0 in range(0, KC, KH)]

        def wsrc_mixg(l, g):
            v = w_mix[l, g].rearrange("p (kk e) -> p kk e", kk=GC)
            return lambda slot: [(W[slot][:, 0:GC, 0:PG], v)]

        wlist = []
        for (p0, p1, hs) in cfg.segs:
            for l in range(DEPTH):
                for j in range(NB):
                    wlist.append(wsrc_in(l, 0 * D + j * BW))
                    wlist.append(wsrc_in(l, 1 * D + j * BW))
                    wlist.append(wsrc_in(l, 3 * D + j * BW))
                for j in range(NB):
                    wlist.append(wsrc_in(l, 2 * D + j * BW))
                    wlist.append(wsrc_in(l, 5 * D + j * BW))
                for j in range(NB):
                    wlist.append(wsrc_sq(w_co, l, j * BW))
                for j in range(NB):
                    wlist.append(wsrc_in(l, 4 * D + j * BW))
                    for gg in range(MPB):
                        wlist.append(wsrc_mixg(l, j * MPB + gg))
                for j in range(NB):
                    wlist.append(wsrc_in(l, 6 * D + j * BW))
                for j in range(NB):
                    wlist.append(wsrc_sq(w_po, l, j * BW))
                for j in range(NB):
                    wlist.append(wsrc_sq(w_o, l, j * BW))
        wstate = {'issued': 0, 'used': 0}
        nseg_ = len(cfg.segs)
        NBLK = len(wlist) // nseg_
        use_wsc = cfg.wscratch and nseg_ > 1
        if use_wsc:
            wsc = nc.dram_tensor("wsc", [NBLK, 128, KC * BW], BF16, kind="Internal").ap()
            wst = [sem(f"wst{i}") for i in range(WR)]
        def w_issue():
            i = wstate['issued']
            if i >= len(wlist):
                return
            slot = i % WR
            if use_wsc and i >= NBLK:
                P.dma('pool', W[slot][:].rearrange("p k e -> p (k e)"), wsc[i % NBLK], wsem[slot], [("WSC", i % NBLK)], [("W", slot)])
            else:
                for (dst_, src_) in wlist[i](slot):
                    P.dma('pool', dst_, src_, wsem[slot], (), [("W", slot)])
                if use_wsc:
                    P.dma('sp', wsc[i], W[slot][:].rearrange("p k e -> p (k e)"), wst[slot], [("W", slot)], [("WSC", i)])
            wstate['issued'] = i + 1

        def w_take():
            i = wstate['used']
            wstate['used'] = i + 1
            assert i < wstate['issued']
            return i % WR

        def w_release(n=1):
            for _ in range(n):
                w_issue()

        out_toks = []
        nseg = len(cfg.segs)
        NPF = 2
        bst_state = {'i': 0, 'reserved': set(), 'pre': []}

        def bst_next():
            while True:
                i_ = bst_state['i'] % NBST
                bst_state['i'] += 1
                if i_ not in bst_state['reserved']:
                    return i_

        def seg_blocks(p0_, p1_, hs_):
            out_ = []
            pos_ = p0_
            while pos_ < p1_:
                n_ = min(128, p1_ - pos_)
                srcs_ = []
                q_ = pos_
                while q_ < pos_ + n_:
                    if q_ < NMETA:
                        k_ = min(NMETA, pos_ + n_) - q_
                        srcs_.append((meta[q_:q_ + k_, :], k_))
                    else:
                        k_ = pos_ + n_ - q_
                        srcs_.append((xp[q_ - NMETA:q_ - NMETA + k_, :], k_))
                    q_ += k_
                out_.append((srcs_, n_, pos_ - p0_))
                pos_ += n_
            if hs_:
                for r0_ in range(0, TS, 128):
                    n_ = min(128, TS - r0_)
                    out_.append(([(xs[r0_:r0_ + n_, :], n_)], n_, (p1_ - p0_) + r0_))
            return out_

        def issue_row_dmas(rows_src):
            i_ = bst_next()
            st_ap, st_reg, st_ld, _ = BST[i_]
            r_ = 0
            for (ap_, k_) in rows_src:
                P.dma('sp', st_ap[r_:r_ + k_, :], ap_, st_ld, (), [st_reg])
                r_ += k_
            return i_
        try:
            chk(0)
            _emit_all = True
        except _Stop:
            _emit_all = False
        for si, (p0, p1, hs) in enumerate(cfg.segs if _emit_all else []):
          try:
              _cur['si'] = si
              Tp = p1 - p0
              T = Tp + (TS if hs else 0)
              tiles = seg_tiles(p0, p1, hs)
              nt = cfg.ntile
              for (a, b_, kd) in tiles:
                  assert b_ - a <= NMAX
              first_seg = (p0 == 0)
              last_prompt_seg = (p1 == PT)
              if last_prompt_seg:
                  assert tiles[nt - 1][2] >= CB

              def XR(ti):
                  return ("X", ti)

              def HRL(ti):
                  return [("H", k_, ti) for k_ in range(KC)]

              def load_rows(rows_src, n, c0, pre=None):
                  bi = issue_row_dmas(rows_src) if pre is None else pre
                  st_ap, st_reg, st_ld, _ = BST[bi]
                  bst_state['reserved'].discard(bi)
                  for h in range(0, KC, 4):
                      b = bank()
                      mm_ = list(range(h, min(KC, h + 4)))
                      transposes(b, [(st_ap[0:n, q * 128:(q + 1) * 128], i * 128, n, 128) for i, q in enumerate(mm_)], [st_reg])
                      wr = [("X", ti) for ti, (a, b2, kd) in enumerate(tiles) if not (b2 <= c0 or a >= c0 + n)]
                      act(X[:, h:h + len(mm_), c0:c0 + n], PS[b][:, 0:len(mm_) * 128].rearrange("p (a t) -> p a t", t=128)[:, :, 0:n],
                          AF.Copy, [("ps", b)], wr)

              pre_list = bst_state['pre']
              bst_state['pre'] = []
              for bi_, (srcs, n, c0_) in enumerate(seg_blocks(p0, p1, hs)):
                  load_rows(srcs, n, c0_, pre=(pre_list[bi_] if bi_ < len(pre_list) else None))

              chk(1)
              if si == 0:
                  for _ in range(WR):
                      w_issue()
                  for l_ in range(DEPTH):
                      out_toks.append(P.dma('sp', ncs[l_, :, 0:CB - DS, :], sc[l_, :, DS:CB, :], d2d_sem))
                      out_toks.append(P.dma('sp', nps[l_, :, 0:PB - DS, :], spl[l_, :, DS:PB, :], d2d_sem))

              def rms_stats(ti, a, b_):
                  n = b_ - a
                  bk = bank()
                  for mch in range(KC):
                      act(CC[:, mch, a:b_], X[:, mch, a:b_], AF.Square, [XR(ti)], [("CC", mch, ti)])
                  for mch in range(KC):
                      P.op('pe', lambda e, mch=mch, bk=bk, n=n: e.matmul(PS[bk][:, 0:n], lhsT=ONESB[:], rhs=CC[:, mch, a:b_],
                                                                       start=(mch == 0), stop=(mch == KC - 1)),
                           [("CC", mch, ti), "ONESB"], [("ps", bk)])
                  act(RR[:, a:b_], PS[bk][:, 0:n], AF.Ln, [("ps", bk), "EPSR"], [("A1", ti)], bias=EPSR[:, 0:1])
                  act(RR[:, a:b_], RR[:, a:b_], AF.Exp, [("A1", ti)], [("A1", ti)], scale=-0.5)

              for l in range(DEPTH):
                  for ti, (a, b_, kd) in enumerate(tiles):
                      rms_stats(ti, a, b_)
                      for mch in range(KC):
                          stt(H[:, mch, a:b_], X[:, mch, a:b_], pv(l, CWID + 3, mch), RR[:, a:b_], ALU.mult, ALU.mult,
                              [XR(ti), ("A1", ti), "PV"], [("H", mch, ti)])

                  chk(2)
                  if hs:
                      SPG = 128 // CB
                      for g0 in range(0, NS, SPG):
                          ng = min(SPG, NS - g0)
                          s = nxt("stg", 2)
                          P.dma('sp', STG[s][0:ng * CB, :], sc[l, g0:g0 + ng].rearrange("s j d -> (s j) d"), stg_ld[s], (), [("STG", s)])
                          for h in range(0, KC, 4):
                              b = bank()
                              mm_ = list(range(h, min(KC, h + 4)))
                              transposes(b, [(STG[s][0:ng * CB, q * 128:(q + 1) * 128], i * 128, ng * CB, 128) for i, q in enumerate(mm_)], [("STG", s)])
                              for i, q in enumerate(mm_):
                                  cp('dve', UX[:, q, g0:g0 + ng, 0:CB], PS[b][:, i * 128:i * 128 + ng * CB].rearrange("p (s j) -> p s j", j=CB),
                                     [("ps", b)], [("UX", q)])
                      SPG2 = 128 // PB
                      for g0 in range(0, NS, SPG2):
                          ng = min(SPG2, NS - g0)
                          s = nxt("stg", 2)
                          P.dma('sp', STG[s][0:ng * PB, :], spl[l, g0:g0 + ng].rearrange("s j d -> (s j) d"), stg_ld[s], (), [("STG", s)])
                          for h in range(0, KC, 4):
                              b = bank()
                              mm_ = list(range(h, min(KC, h + 4)))
                              transposes(b, [(STG[s][0:ng * PB, q * 128:(q + 1) * 128], i * 128, ng * PB, 128) for i, q in enumerate(mm_)], [("STG", s)])
                              for i, q in enumerate(mm_):
                                  cp('dve', PX[:, q, g0:g0 + ng, 0:PB], PS[b][:, i * 128:i * 128 + ng * PB].rearrange("p (s j) -> p s j", j=PB),
                                     [("ps", b)], [("PX", q)])

                  chk(20)
                  mset('dve', S1[:, 0:T], 0.0, [("A1", ti) for ti in range(len(tiles))])
                  mset('dve', S2[:, 0:T], 0.0, [("A2", ti) for ti in range(len(tiles))])
                  uslot_of = {}

                  def glu(mch, wv, wg, mloc):
                      us = mch % 2
                      uslot_of[mch] = us
                      ds_ = mch % 3
                      base_r = 36 * l
                      def dg_build(k0, k1):
                          nk = k1 - k0
                          P.op('dve', lambda e: e.tensor_tensor(out=DG[ds_][:, k0:k1, :], in0=IDB[:].unsqueeze(1).broadcast_to([128, nk, 128]),
                                                                in1=PV[:, mch, base_r + k0:base_r + k1].unsqueeze(2).broadcast_to([128, nk, 128]),
                                                                op=ALU.mult), ["IDB", "PV"], [("DG", ds_)])
                      dg_parts = [(0, 11), (11, 21), (21, CWID)]
                      dg_build(*dg_parts[0])
                      if first_seg:
                          mset('dve', U[us][:, 0:CB], 0.0, [("U", us)])
                      else:
                          cp('dve', U[us][:, 0:CB], UT[l][:, mch, :], [("UT", l, mch)], [("U", us)])
                      for ti, (a, b_, kd) in enumerate(tiles):
                          n = b_ - a
                          bg = bank()
                          mm_chain(bg, n, [(W[wg][:, k, mloc * 128:(mloc + 1) * 128], H[:, k, a:b_], [("W", wg), ("H", k, ti)]) for k in range(KC)])
                          bv = bank()
                          mm_chain(bv, n, [(W[wv][:, k, mloc * 128:(mloc + 1) * 128], H[:, k, a:b_], [("W", wv), ("H", k, ti)]) for k in range(KC)])
                          sg = nxt("sg", 4)
                          act(SG[sg][:, 0:n], PS[bg][:, 0:n], AF.Sigmoid, [("ps", bg)], [("SG", sg)])
                          if kd > 0:
                              tt('dve', U[us][:, CB + a:CB + a + kd], PS[bv][:, 0:kd], SG[sg][:, 0:kd], ALU.mult, [("ps", bv), ("SG", sg)], [("U", us)])
                              if last_prompt_seg and a + kd == Tp:
                                  tt('dve', UL[:, mch, :], PS[bv][:, kd - CB:kd], SG[sg][:, kd - CB:kd], ALU.mult, [("ps", bv), ("SG", sg)], [("UL", mch)])
                          if kd < n:
                              tt('dve', UX[:, mch, :, CB:CB + DS], PS[bv][:, kd:n].rearrange("p (s j) -> p s j", j=DS),
                                 SG[sg][:, kd:n].rearrange("p (s j) -> p s j", j=DS), ALU.mult, [("ps", bv), ("SG", sg)], [("UX", mch)])
                              tt('dve', US[:, mch, :], PS[bv][:, kd:n], SG[sg][:, kd:n], ALU.mult, [("ps", bv), ("SG", sg)], [("US", mch)])
                          if ti + 1 < len(dg_parts):
                              dg_build(*dg_parts[ti + 1])
                      for pi_ in range(len(tiles) + 1, len(dg_parts)):
                          dg_build(*dg_parts[pi_])
                      if not last_prompt_seg:
                          cp('dve', UT[l][:, mch, :], U[us][:, Tp:Tp + CB], [("U", us)], [("UT", l, mch)])

                  def conv(mch):
                      us = uslot_of[mch]
                      ds_ = mch % 3
                      for ti, (a, b_, kd) in enumerate(tiles):
                          n = b_ - a
                          bc = bank()
                          if kd > 0:
                              mm_group(bc, kd, [(DG[ds_][:, k, :], U[us][:, a + k:a + kd + k]) for k in range(CWID)], [("DG", ds_), ("U", us)])
                          if kd < n:
                              mm_group(bc, n - kd, [(DG[ds_][:, k, :], UX[:, mch, :, k:k + DS]) for k in range(CWID)], [("DG", ds_), ("UX", mch)], c0=kd)
                          cb_ap = pv(l, CWID + 0, mch)
                          act(CC[:, mch, a:b_], PS[bc][:, 0:n], AF.Identity, [("ps", bc), "PV"], [("CC", mch, ti)], bias=cb_ap)
                          cs = nxt("t2", 4)
                          act(T2[cs][:, 0:n], PS[bc][:, 0:n], AF.Square, [("ps", bc), "PV"], [("T2", cs)], bias=cb_ap)
                          tt('dve', S1[:, a:b_], S1[:, a:b_], CC[:, mch, a:b_], ALU.add, [("CC", mch, ti)], [("A1", ti)])
                          tt('dve', S2[:, a:b_], S2[:, a:b_], T2[cs][:, 0:n], ALU.add, [("T2", cs)], [("A2", ti)])

                  pool_first = [False]

                  def pin(mch, wp, mloc):
                      pslot = mch % 2
                      wwin = WINS[mch // GC]
                      if first_seg:
                          mset('dve', PIN[pslot][:, 0:PB], 0.0, [("PIN", pslot)])
                      else:
                          cp('dve', PIN[pslot][:, 0:PB], PTL[l][:, mch, :], [("PTL", l, mch)], [("PIN", pslot)])
                      for ti, (a, b_, kd) in enumerate(tiles):
                          n = b_ - a
                          bp = bank()
                          mm_group(bp, n, [(W[wp][:, k, mloc * 128:(mloc + 1) * 128], H[:, k, a:b_]) for k in range(KC)], [("W", wp)] + HRL(ti))
                          if kd > 0:
                              act(PIN[pslot][:, PB + a:PB + a + kd], PS[bp][:, 0:kd], AF.Copy, [("ps", bp)], [("PIN", pslot)])
                          if kd < n:
                              act(PX[:, mch, :, PB:PB + DS], PS[bp][:, kd:n].rearrange("p (s j) -> p s j", j=DS), AF.Copy, [("ps", bp)], [("PX", mch)])
                              act(PSN[:, mch, :], PS[bp][:, kd:n], AF.Copy, [("ps", bp)], [("PSN", mch)])
                      if not last_prompt_seg:
                          cp('dve', PTL[l][:, mch, :], PIN[pslot][:, Tp:Tp + PB], [("PIN", pslot)], [("PTL", l, mch)])
                      else:
                          cp('dve', PL15[:, mch, :], PIN[pslot][:, Tp:Tp + PB], [("PIN", pslot)], [("PL15", mch)])

                      def pool_region(xap_fn, E, rd_reg, outs):
                          cur = xap_fn
                          sh = 1
                          lvl = 0
                          cur_reg = rd_reg
                          while sh < wwin:
                              dst = TA if lvl % 2 == 0 else TB
                              dreg = "MGALL"
                              lo = 2 * sh - 1
                              wr_ = [dreg]
                              if not pool_first[0]:
                                  pool_first[0] = True
                                  wr_ = [dreg] + [("MG", k_, t_) for k_ in range(KC) for t_ in range(len(tiles))]
                              tt('dve', dst[:, lo:E], cur(lo, E), cur(lo - sh, E - sh), ALU.add, [cur_reg], wr_)
                              cur = (lambda d: (lambda lo_, hi_: d[:, lo_:hi_]))(dst)
                              cur_reg = dreg
                              sh *= 2
                              lvl += 1
                          outs(cur, cur_reg)

                      E = PB + Tp

                      def outs_p(cur, cur_reg, pslot=pslot, mch=mch, wwin=wwin):
                          for ti, (a, b_, kd) in enumerate(tiles):
                              if kd == 0:
                                  continue
                              stt(QF[:, mch, a:a + kd], cur(PB + a, PB + a + kd), 1.0 / wwin, PIN[pslot][:, PB + a:PB + a + kd], ALU.mult, ALU.subtract,
                                  [cur_reg, ("PIN", pslot)], [("QF", mch, ti)])
                          if first_seg:
                              nf = wwin - 1
                              tt('dve', T1[0][:, 0:nf], cur(PB, PB + nf), INV[:, 0:nf], ALU.mult, [cur_reg, "INV"], [("T1", 0)])
                              tt('dve', QF[:, mch, 0:nf], T1[0][:, 0:nf], PIN[pslot][:, PB:PB + nf], ALU.subtract, [("T1", 0), ("PIN", pslot)], [("QF", mch, 0)])
                      pool_region(lambda lo, hi, pslot=pslot: PIN[pslot][:, lo:hi], E, ("PIN", pslot), outs_p)
                      if hs:
                          E2 = NS * (PB + DS)
                          flat = PX[:, mch, :, :].rearrange("p s j -> p (s j)")

                          def outs_s(cur, cur_reg, mch=mch, wwin=wwin):
                              ti = len(tiles) - 1
                              a, b_, kd = tiles[ti]
                              cur3 = cur(0, E2).rearrange("p (s j) -> p s j", j=PB + DS)
                              stt(QF[:, mch, a + kd:b_].rearrange("p (s j) -> p s j", j=DS), cur3[:, :, PB:PB + DS], 1.0 / wwin, PX[:, mch, :, PB:PB + DS],
                                  ALU.mult, ALU.subtract, [cur_reg, ("PX", mch)], [("QF", mch, ti)])
                          pool_region(lambda lo, hi, flat=flat: flat[:, lo:hi], E2, ("PX", mch), outs_s)

                  wv = wg = wp = None
                  for mch in range(KC):
                      if mch % MB == 0:
                          wv = w_take()
                          wg = w_take()
                          wp = w_take()
                      glu(mch, wv, wg, mch % MB)
                      if mch >= 1:
                          conv(mch - 1)
                      pin(mch, wp, mch % MB)
                      if mch % MB == MB - 1:
                          w_release(3)
                  conv(KC - 1)

                  chk(3)
                  def out_rows_from_fm(src_fn, nrows, dst_fn, rd):
                      s = nxt("stg", 2)
                      for h in range(0, KC, 4):
                          b = bank()
                          mm_ = list(range(h, min(KC, h + 4)))
                          transposes(b, [(src_fn(q), i * 128, 128, nrows) for i, q in enumerate(mm_)], [r for q in mm_ for r in rd(q)])
                          act(STG[s][0:nrows, h * 128:(h + len(mm_)) * 128], PS[b][0:nrows, 0:len(mm_) * 128], AF.Copy, [("ps", b)], [("STG", s)])
                      return dst_fn(s)

                  if last_prompt_seg:
                      out_toks.append(out_rows_from_fm(lambda q: UL[:, q, :], CB,
                                                       lambda s: P.dma('sp', ncp[l], STG[s][0:CB, :], stg_st[s], [("STG", s)], ()),
                                                       lambda q: [("UL", q)]))
                  if hs:
                      for r0 in range(0, TS, 128):
                          n = min(128, TS - r0)

                          def dst(s, r0=r0, n=n):
                              tk = None
                              for sq_ in range(r0 // DS, (r0 + n) // DS):
                                  tk = P.dma('sp', ncs[l, sq_, CB - DS:CB, :], STG[s][(sq_ * DS - r0):(sq_ * DS - r0) + DS, :], stg_st[s], [("STG", s)], ())
                              return tk
                          out_toks.append(out_rows_from_fm(lambda q, r0=r0, n=n: US[:, q, r0:r0 + n], n, dst, lambda q: [("US", q)]))

                  chk(30)
                  for ti, (a, b_, kd) in enumerate(tiles):
                      n = b_ - a
                      b1 = bank()
                      P.op('pe', lambda e, b1=b1, a=a, b_=b_, n=n: e.matmul(PS[b1][:, 0:n], lhsT=ONES32[:], rhs=S1[:, a:b_], start=True, stop=True),
                           [("A1", ti), "ONES32"], [("ps", b1)])
                      b2 = bank()
                      P.op('pe', lambda e, b2=b2, a=a, b_=b_, n=n: e.matmul(PS[b2][:, 0:n], lhsT=ONES32[:], rhs=S2[:, a:b_], start=True, stop=True),
                           [("A2", ti), "ONES32"], [("ps", b2)])
                      cp('dve', MU[:, a:b_], PS[b1][:, 0:n], [("ps", b1)], [("A1", ti)])
                      tt('dve', RS[:, a:b_], MU[:, a:b_], MU[:, a:b_], ALU.mult, [("A1", ti)], [("A2", ti)])
                      tt('dve', RS[:, a:b_], PS[b2][:, 0:n], RS[:, a:b_], ALU.subtract, [("ps", b2), ("A2", ti)], [("A2", ti)])
                      act(RS[:, a:b_], RS[:, a:b_], AF.Ln, [("A2", ti), "EPSL"], [("A2", ti)], bias=EPSL[:, 0:1])
                      act(RS[:, a:b_], RS[:, a:b_], AF.Exp, [("A2", ti)], [("A2", ti)], scale=-0.5)
                  pend_fin = [None]
                  for blk in range(NB):
                      ws = w_take()
                      wa = w_take()
                      chs = list(range(blk * MB, (blk + 1) * MB))
                      if blk == NB - 1:
                          order = [(m_, t_) for t_ in range(len(tiles)) for m_ in chs]
                      else:
                          order = [(m_, t_) for m_ in chs for t_ in range(len(tiles))]
                      for (mch, ti) in order:
                          mloc = mch % MB
                          a, b_, kd = tiles[ti]
                          n = b_ - a
                          bs = bank()
                          mm_group(bs, n, [(W[ws][:, k, mloc * 128:(mloc + 1) * 128], H[:, k, a:b_]) for k in range(KC)], [("W", ws)] + HRL(ti))
                          bg = bank()
                          mm_group(bg, n, [(W[wa][:, k, mloc * 128:(mloc + 1) * 128], H[:, k, a:b_]) for k in range(KC)], [("W", wa)] + HRL(ti))
                          sg = nxt("sg", 4)
                          act(SG[sg][:, 0:n], PS[bs][:, 0:n], AF.Silu, [("ps", bs)], [("SG", sg)])
                          act(MG[:, mch, a:b_], PS[bg][:, 0:n], AF.Tanh, [("ps", bg)], [("MG", mch, ti), "MGALL"], scale=0.5)
                          t1 = nxt("t1", 4)
                          tt('dve', T1[t1][:, 0:n], CC[:, mch, a:b_], MU[:, a:b_], ALU.subtract, [("CC", mch, ti), ("A1", ti)], [("T1", t1)])
                          tt('dve', T1[t1][:, 0:n], T1[t1][:, 0:n], RS[:, a:b_], ALU.mult, [("T1", t1), ("A2", ti)], [("T1", t1)])
                          t2 = nxt("t2", 4)
                          act(T2[t2][:, 0:n], T1[t1][:, 0:n], AF.Silu, [("T1", t1), "PV"], [("T2", t2)], bias=pv(l, CWID + 2, mch), scale=pv(l, CWID + 1, mch))
                          if pend_fin[0] is not None:
                              pend_fin[0]()

                          def fin_(mch=mch, a=a, b_=b_, n=n, t2=t2, sg=sg, ti=ti):
                              tt('dve', CC[:, mch, a:b_], T2[t2][:, 0:n], SG[sg][:, 0:n], ALU.mult, [("T2", t2), ("SG", sg)], [("CC", mch, ti)])
                          pend_fin[0] = fin_
                      w_release(2)
                  pend_fin[0]()

                  chk(6)
                  if last_prompt_seg:
                      out_toks.append(out_rows_from_fm(lambda q: PL15[:, q, :], PB,
                                                       lambda s: P.dma('sp', npp[l], STG[s][0:PB, :], stg_st[s], [("STG", s)], ()),
                                                       lambda q: [("PL15", q)]))
                  if hs:
                      for r0 in range(0, TS, 128):
                          n = min(128, TS - r0)

                          def dst2(s, r0=r0, n=n):
                              tk = None
                              for sq_ in range(r0 // DS, (r0 + n) // DS):
                                  tk = P.dma('sp', nps[l, sq_, PB - DS:PB, :], STG[s][(sq_ * DS - r0):(sq_ * DS - r0) + DS, :], stg_st[s], [("STG", s)], ())
                              return tk
                          out_toks.append(out_rows_from_fm(lambda q, r0=r0, n=n: PSN[:, q, r0:r0 + n], n, dst2, lambda q: [("PSN", q)]))

                  chk(4)
                  wcs = [w_take() for _ in range(NB)]
                  for ti, (a, b_, kd) in enumerate(tiles):
                      n = b_ - a
                      for mch in range(KC):
                          wc = wcs[mch // MB]
                          mloc = mch % MB
                          ba = bank()
                          mm_group(ba, n, [(W[wc][:, k, mloc * 128:(mloc + 1) * 128], CC[:, k, a:b_]) for k in range(KC)],
                                   [("W", wc)] + [("CC", k, ti) for k in range(KC)])
                          stt(MG[:, mch, a:b_], MG[:, mch, a:b_], 1.0, PS[ba][:, 0:n], ALU.add, ALU.mult, [("ps", ba)], [("MG", mch, ti)])
                          if ti == len(tiles) - 1 and mch % MB == MB - 1:
                              w_release(1)

                  chk(60)
                  wms = None
                  wps = None
                  for mch in range(KC):
                      if mch % MB == 0:
                          wps = w_take()
                          wms = [w_take() for _ in range(MPB)]
                      mloc = mch % MB
                      g = mch // GC
                      wm = wms[mloc // GC]
                      for ti, (a, b_, kd) in enumerate(tiles):
                          n = b_ - a
                          bq = bank()
                          mm_group(bq, n, [(W[wm][:, kk, (mch % GC) * 128:(mch % GC + 1) * 128], QF[:, g * GC + kk, a:b_]) for kk in range(GC)],
                                   [("W", wm)] + [("QF", g * GC + kk, ti) for kk in range(GC)])
                          bs = bank()
                          mm_group(bs, n, [(W[wps][:, k, mloc * 128:(mloc + 1) * 128], H[:, k, a:b_]) for k in range(KC)], [("W", wps)] + HRL(ti))
                          sg = nxt("sg", 4)
                          act(SG[sg][:, 0:n], PS[bs][:, 0:n], AF.Silu, [("ps", bs)], [("SG", sg)])
                          stt(CC[:, mch, a:b_], PS[bq][:, 0:n], pv(l, CWID + 4, mch), SG[sg][:, 0:n], ALU.mult, ALU.mult,
                              [("ps", bq), ("SG", sg), "PV"], [("CC", mch, ti)])
                      if mch % MB == MB - 1:
                          w_release(1 + MPB)

                  chk(7)
                  wo_ = wb = None
                  for mch in range(KC):
                      if mch % MB == 0:
                          wb = w_take()
                      mloc = mch % MB
                      for ti, (a, b_, kd) in enumerate(tiles):
                          n = b_ - a
                          bg = bank()
                          mm_group(bg, n, [(W[wb][:, k, mloc * 128:(mloc + 1) * 128], H[:, k, a:b_]) for k in range(KC)], [("W", wb)] + HRL(ti))
                          act(QF[:, mch, a:b_], PS[bg][:, 0:n], AF.Sigmoid, [("ps", bg)], [("QF", mch, ti)])
                      if mch % MB == MB - 1:
                          w_release(1)
                  for mch in range(KC):
                      if mch % MB == 0:
                          wo_ = w_take()
                      mloc = mch % MB
                      for ti, (a, b_, kd) in enumerate(tiles):
                          n = b_ - a
                          bb = bank()
                          mm_group(bb, n, [(W[wo_][:, k, mloc * 128:(mloc + 1) * 128], CC[:, k, a:b_]) for k in range(KC)],
                                   [("W", wo_)] + [("CC", k, ti) for k in range(KC)])
                          t1 = nxt("t1", 4)
                          tt('dve', T1[t1][:, 0:n], PS[bb][:, 0:n], QF[:, mch, a:b_], ALU.mult, [("ps", bb), ("QF", mch, ti)], [("T1", t1)])
                          stt(MG[:, mch, a:b_], T1[t1][:, 0:n], 2.0, MG[:, mch, a:b_], ALU.mult, ALU.add, [("T1", t1)], [("MG", mch, ti)])
                      if mch % MB == MB - 1:
                          w_release(1)

                  chk(8)
                  wws = [w_take() for _ in range(NB)]
                  for ti, (a, b_, kd) in enumerate(tiles):
                      n = b_ - a
                      for mch in range(KC):
                          ww = wws[mch // MB]
                          mloc = mch % MB
                          bo = bank()
                          mm_group(bo, n, [(W[ww][:, k, mloc * 128:(mloc + 1) * 128], MG[:, k, a:b_]) for k in range(KC)],
                                   [("W", ww)] + [("MG", k, ti) for k in range(KC)])
                          stt(X[:, mch, a:b_], PS[bo][:, 0:n], 0.5, X[:, mch, a:b_], ALU.mult, ALU.add, [("ps", bo)], [XR(ti)])
                          if ti == len(tiles) - 1 and mch % MB == MB - 1:
                              w_release(1)

              chk(9)
              for ti, (a, b_, kd) in enumerate(tiles):
                  rms_stats(ti, a, b_)
                  for mch in range(KC):
                      stt(X[:, mch, a:b_], X[:, mch, a:b_], PV[:, mch, R - 1:R], RR[:, a:b_], ALU.mult, ALU.mult,
                          [XR(ti), ("A1", ti), "PV"], [XR(ti)])

              def xregs(c0, n):
                  return [("X", ti) for ti, (a, b2, kd) in enumerate(tiles) if not (b2 <= c0 or a >= c0 + n)]

              def store_rows(c0, n, dst_fn):
                  bst = BST[bst_next()]
                  st_ap, st_reg = bst[0], bst[1]
                  for h in range(0, KC, 4):
                      b = bank()
                      mm_ = list(range(h, min(KC, h + 4)))
                      transposes(b, [(X[:, q, c0:c0 + n], i * 128, 128, n) for i, q in enumerate(mm_)], xregs(c0, n))
                      act(st_ap[0:n, h * 128:(h + len(mm_)) * 128], PS[b][0:n, 0:len(mm_) * 128], AF.Copy, [("ps", b)], [st_reg])
                  return dst_fn(bst)

              if si + 1 < nseg and NBST > NPF + 1:
                  for (srcs_n, n_n, c0_n) in seg_blocks(*cfg.segs[si + 1])[:NPF]:
                      bi_n = issue_row_dmas(srcs_n)
                      bst_state['reserved'].add(bi_n)
                      bst_state['pre'].append(bi_n)
              pos = p0
              while pos < p1:
                  n = min(128, p1 - pos)
                  lo = max(pos, NMETA)
                  if lo < pos + n:
                      def dsty(bst, pos=pos, n=n, lo=lo):
                          return P.dma('sp', yp[lo - NMETA:pos + n - NMETA, :], bst[0][lo - pos:n, :], bst[3], [bst[1]], ())
                      out_toks.append(store_rows(pos - p0, n, dsty))
                  pos += n
              chk(10)
              if hs:
                  for r0 in range(0, TS, 128):
                      n = min(128, TS - r0)

                      def dsts(bst, r0=r0, n=n):
                          return P.dma('sp', ys[r0:r0 + n, :], bst[0][0:n, :], bst[3], [bst[1]], ())
                      out_toks.append(store_rows(Tp + r0, n, dsts))
          except _Stop:
            break

        finals = {}
        for tk in out_toks:
            if tk is None:
                continue
            k = tk[0].name
            tot = P.dcnt[k]
            finals[k] = (tk[0], tot)
        P.wait_all('sp', list(finals.values()))

        @block.sync
        def _(e):
            P.replay('sp', e)

        @block.gpsimd
        def _(e):
            P.replay('pool', e)

        @block.tensor
        def _(e):
            P.replay('pe', e)

        @block.scalar
        def _(e):
            P.replay('act', e)

        @block.vector
        def _(e):
            P.replay('dve', e)
    return nc


def make_in_maps(cfg, ncores, x_prompt, x_sample, state_conv, state_pool, meta_tokens, norm_g, w_in, conv_w, conv_b,
                 ln_g, ln_b, w_conv_out, w_pool_mix, pool_scale, w_pool_out, w_out, final_g):
    f = lambda a: np.ascontiguousarray(np.asarray(a, dtype=np.float32))
    NS, D = cfg.NS, cfg.D
    KC, BW, GC = cfg.KC, cfg.BW, cfg.GC
    PG = D // 4

    def pack_blocks(w):
        w = f(w)
        L, _, C = w.shape
        return np.ascontiguousarray(w.reshape(L, KC, 128, C // BW, BW).transpose(0, 3, 2, 1, 4)).reshape(L, C // BW, 128, KC * BW)

    def pack_mix(w):
        w = f(w)
        L = w.shape[0]
        return np.ascontiguousarray(w.reshape(L, 4, GC, 128, PG).transpose(0, 1, 3, 2, 4)).reshape(L, 4, 128, GC * PG)

    shared = dict(meta=f(meta_tokens), norm_g=f(norm_g), w_in=pack_blocks(w_in), conv_w=f(conv_w), conv_b=f(conv_b), ln_g=f(ln_g),
                  ln_b=f(ln_b), w_co=pack_blocks(w_conv_out), w_mix=pack_mix(w_pool_mix), pscale=f(pool_scale),
                  w_po=pack_blocks(w_pool_out), w_o=pack_blocks(w_out), final_g=f(final_g).reshape(1, D))
    x_prompt = np.asarray(x_prompt); x_sample = np.asarray(x_sample)
    state_conv = np.asarray(state_conv); state_pool = np.asarray(state_pool)
    maps = []
    for c in range(ncores):
        m = dict(shared)
        m["xp"] = f(x_prompt[c])
        m["xs"] = f(x_sample[c * NS:(c + 1) * NS]).reshape(NS * DS, D)
        m["sc"] = f(state_conv[:, c * NS:(c + 1) * NS])
        m["spl"] = f(state_pool[:, c * NS:(c + 1) * NS])
        maps.append(m)
    return maps


def gather(cfg, results):
    NS, D = cfg.NS, cfg.D
    y_prompt = np.stack([np.asarray(r["yp"]) for r in results], axis=0).astype(np.float32)
    y_sample = np.concatenate([np.asarray(r["ys"]).reshape(NS, DS, D) for r in results], axis=0).astype(np.float32)
    ncp = np.stack([np.asarray(r["ncp"]) for r in results], axis=1).astype(np.float32)
    npp = np.stack([np.asarray(r["npp"]) for r in results], axis=1).astype(np.float32)
    ncs = np.concatenate([np.asarray(r["ncs"]) for r in results], axis=1).astype(np.float32)
    nps = np.concatenate([np.asarray(r["nps"]) for r in results], axis=1).astype(np.float32)
    return (y_prompt, y_sample, ncp, npp, ncs, nps)


def kernel(**inputs):
    cfg = REAL
    nc = build_program(cfg)
    maps = make_in_maps(cfg, 8, **inputs)
    res = run_bass_kernel_spmd(nc, maps, core_ids=list(range(8)))
    return gather(cfg, res.results)
```

```python
import numpy as np
from contextlib import ExitStack
import concourse.bass as bass
import concourse.mybir as mybir
from concourse.bass_utils import run_bass_kernel_spmd

F32 = mybir.dt.float32
BF16 = mybir.dt.bfloat16
AF = mybir.ActivationFunctionType
ALU = mybir.AluOpType

ENGS = ['pe', 'act', 'dve', 'pool', 'sp']
CWID = 31
CB = 30
PB = 15
WINS = (2, 4, 8, 16)
NMETA = 16
DS = 8
RMS_EPS = 1e-6
LN_EPS = 1e-5
DEPTH = 2


class Cfg:
    def __init__(self, D=1024, SEQ=2048, NS=16, segs=None, ntile=2, BW=256, WR=6, merge_sample=True, wscratch=True):
        self.D = D
        self.KC = D // 128
        self.SEQ = SEQ
        self.PT = SEQ + NMETA
        self.NS = NS
        self.TS = NS * DS
        self.GC = (D // 4) // 128
        assert self.GC >= 1
        self.segs = segs
        self.ntile = ntile
        self.BW = BW
        self.merge_sample = merge_sample
        self.wscratch = wscratch
        self.WR = WR
        self.MB = BW // 128
        self.NB = D // BW
        self.R = 36 * DEPTH + 1


REAL = Cfg(segs=[(0, 736, False), (736, 1472, False), (1472, 2064, True)], WR=8)


class _Stop(Exception):
    pass


_cur = {'si': 0}


def chk(n):
    return


class Prog:
    def __init__(self, sems):
        self.ops = {e: [] for e in ENGS}
        self.cnt = {e: 0 for e in ENGS}
        self.sem = sems
        self.waited = {e: {} for e in ENGS}
        self.dcnt = {}
        self.lastw = {}
        self.readers = {}

    def _waits(self, eng, deps):
        waits = {}
        for d in deps:
            if d is None:
                continue
            sem, val = d
            key = sem.name
            if eng == 'pe' and key == self.sem['pe'].name:
                continue
            if self.waited[eng].get(key, 0) >= val:
                continue
            if key in waits and waits[key][1] >= val:
                continue
            waits[key] = (sem, val)
        for key, (sem, val) in waits.items():
            self.waited[eng][key] = val
        return list(waits.values())

    def _deps(self, reads, writes):
        deps = []
        for r in reads:
            deps.append(self.lastw.get(r))
        for w in writes:
            deps.append(self.lastw.get(w))
            deps.extend(self.readers.get(w, {}).values())
        return deps

    def _mark(self, tok, reads, writes):
        key = tok[0].name
        for r in reads:
            d = self.readers.setdefault(r, {})
            if key not in d or d[key][1] < tok[1]:
                d[key] = tok
        for w in writes:
            self.lastw[w] = tok
            self.readers[w] = {}

    def op(self, eng, fn, reads=(), writes=()):
        return self.group(eng, [fn], reads, writes)

    def group(self, eng, fns, reads=(), writes=()):
        waits = self._waits(eng, self._deps(reads, writes))
        self.cnt[eng] += 1
        tok = (self.sem[eng], self.cnt[eng])
        n = len(fns)
        for i, fn in enumerate(fns):
            self.ops[eng].append((waits if i == 0 else [], fn, self.sem[eng] if i == n - 1 else None, 1))
        self._mark(tok, reads, writes)
        return tok

    def chain(self, eng, items, writes=()):
        self.cnt[eng] += 1
        tok = (self.sem[eng], self.cnt[eng])
        n = len(items)
        allreads = []
        for i, (fn, reads) in enumerate(items):
            deps = [self.lastw.get(r) for r in reads]
            if i == 0:
                for w in writes:
                    deps.append(self.lastw.get(w))
                    deps.extend(self.readers.get(w, {}).values())
            waits = self._waits(eng, deps)
            self.ops[eng].append((waits, fn, self.sem[eng] if i == n - 1 else None, 1))
            allreads.extend(reads)
        self._mark(tok, allreads, writes)
        return tok

    def dma(self, eng, out, in_, sem, reads=(), writes=()):
        waits = self._waits(eng, self._deps(reads, writes))
        key = sem.name
        self.dcnt[key] = self.dcnt.get(key, 0) + 16
        tok = (sem, self.dcnt[key])
        self.ops[eng].append((waits, lambda e: e.dma_start(out=out, in_=in_), sem, 16))
        self._mark(tok, reads, writes)
        return tok

    def wait_all(self, eng, toks):
        waits = self._waits(eng, toks)
        self.ops[eng].append((waits, None, None, 0))

    def replay(self, eng, e):
        for waits, fn, sem, inc in self.ops[eng]:
            for (s, v) in waits:
                e.wait_ge(s, v)
            if fn is not None:
                inst = fn(e)
                if sem is not None:
                    inst.then_inc(sem, inc)


def build_program(cfg):
    D, KC, PT, NS, TS, GC = cfg.D, cfg.KC, cfg.PT, cfg.NS, cfg.TS, cfg.GC
    BW, MB, NB, R = cfg.BW, cfg.MB, cfg.NB, cfg.R
    PG = D // 4
    nc = bass.Bass("TRN2", target_bir_lowering=False)

    def din(name, shape):
        return nc.dram_tensor(name, shape, F32, kind="ExternalInput").ap()

    def dout(name, shape):
        return nc.dram_tensor(name, shape, F32, kind="ExternalOutput").ap()

    xp = din("xp", [cfg.SEQ, D])
    xs = din("xs", [TS, D])
    sc = din("sc", [DEPTH, NS, CB, D])
    spl = din("spl", [DEPTH, NS, PB, D])
    meta = din("meta", [NMETA, D])
    norm_g = din("norm_g", [DEPTH, D])
    w_in = din("w_in", [DEPTH, 7 * D // BW, 128, KC * BW])
    conv_w = din("conv_w", [DEPTH, CWID, D])
    conv_b = din("conv_b", [DEPTH, D])
    ln_g = din("ln_g", [DEPTH, D])
    ln_b = din("ln_b", [DEPTH, D])
    w_co = din("w_co", [DEPTH, D // BW, 128, KC * BW])
    w_mix = din("w_mix", [DEPTH, 4, 128, GC * PG])
    pscale = din("pscale", [DEPTH, D])
    w_po = din("w_po", [DEPTH, D // BW, 128, KC * BW])
    w_o = din("w_o", [DEPTH, D // BW, 128, KC * BW])
    final_g = din("final_g", [1, D])
    yp = dout("yp", [cfg.SEQ, D])
    ys = dout("ys", [TS, D])
    ncp = dout("ncp", [DEPTH, CB, D])
    npp = dout("npp", [DEPTH, PB, D])
    ncs = dout("ncs", [DEPTH, NS, CB, D])
    nps = dout("nps", [DEPTH, NS, PB, D])

    def seg_tiles(p0, p1, hs):
        Tp_ = p1 - p0
        tl = []
        nt_ = cfg.ntile
        base_ = (Tp_ // nt_) // 2 * 2
        c_ = 0
        for i_ in range(nt_):
            c1_ = Tp_ if i_ == nt_ - 1 else c_ + base_
            tl.append((c_, c1_, c1_ - c_))
            c_ = c1_
        if hs:
            la, lb, lnp = tl[-1]
            if (lb - la) + TS <= 512 and cfg.merge_sample:
                tl[-1] = (la, lb + TS, lnp)
            else:
                tl.append((Tp_, Tp_ + TS, 0))
        return tl

    TW = max(b_ - a_ for sg_ in cfg.segs for (a_, b_, _) in seg_tiles(*sg_))
    TW = (TW + 7) // 8 * 8
    TPmax = max(p1 - p0 for (p0, p1, _) in cfg.segs)
    Tmax = max((p1 - p0) + (TS if hs else 0) for (p0, p1, hs) in cfg.segs)
    NMAX = 512
    EU = CB + TPmax
    EP = PB + TPmax
    EPS_ = max(EP, NS * (PB + DS))

    with ExitStack() as st:
        def sbuf(name, shape, dt):
            return st.enter_context(nc.sbuf_tensor(name, shape, dt))

        def sem(name):
            return st.enter_context(nc.semaphore(name))

        X = sbuf("X", [128, KC, Tmax], F32)
        H = sbuf("H", [128, KC, Tmax], BF16)
        CC = sbuf("CC", [128, KC, Tmax], BF16)
        QF = sbuf("QF", [128, KC, Tmax], BF16)
        MG = sbuf("MG", [128, KC, Tmax], BF16)
        U = [sbuf(f"U{i}", [128, EU], BF16) for i in range(2)]
        UX = sbuf("UX", [128, KC, NS, CB + DS], BF16)
        PX = sbuf("PX", [128, KC, NS, PB + DS], F32)
        PIN = [sbuf(f"PIN{i}", [128, EP], F32) for i in range(2)]
        if KC * Tmax // 2 >= 2 * EPS_:
            MGf = MG[:].rearrange("p k t -> p (k t)").bitcast(F32)
            TA = MGf[:, 0:EPS_]
            TB = MGf[:, EPS_:2 * EPS_]
        else:
            TA = sbuf("TA", [128, EPS_], F32)[:]
            TB = sbuf("TB", [128, EPS_], F32)[:]
        WR = cfg.WR
        W = [sbuf(f"W{i}", [128, KC, BW], BF16) for i in range(WR)]
        DG = [sbuf(f"DG{i}", [128, CWID, 128], BF16) for i in range(3)]
        A1 = sbuf("A1", [128, Tmax], F32)
        A2 = sbuf("A2", [128, Tmax], F32)
        S1 = MU = RR = A1
        S2 = RS = A2
        SG = [sbuf(f"SG{i}", [128, TW], F32) for i in range(4)]
        T1 = [sbuf(f"T1_{i}", [128, TW], F32) for i in range(4)]
        T2 = [sbuf(f"T2_{i}", [128, TW], F32) for i in range(4)]
        STG = [sbuf(f"STG{i}", [128, D], F32) for i in range(2)]
        PV = sbuf("PV", [128, KC, R], F32)
        ID32 = sbuf("ID32", [128, 128], F32)
        IDB = sbuf("IDB", [128, 128], BF16)
        ONESB = sbuf("ONESB", [128, 128], BF16)
        ONES32 = sbuf("ONES32", [128, 128], F32)
        EPSR = sbuf("EPSR", [128, 1], F32)
        EPSL = sbuf("EPSL", [128, 1], F32)
        INV = sbuf("INV", [128, PB], F32)
        UL = sbuf("UL", [128, KC, CB], F32)
        PL15 = sbuf("PL15", [128, KC, PB], F32)
        US = sbuf("US", [128, KC, TS], F32)
        PSN = sbuf("PSN", [128, KC, TS], F32)
        DUMMY = sbuf("DUMMY", [128, 2], F32)
        UT = [sbuf(f"UT{l}", [128, KC, CB], BF16) for l in range(DEPTH)]
        PTL = [sbuf(f"PTL{l}", [128, KC, PB], F32) for l in range(DEPTH)]
        PS = [st.enter_context(nc.psum_tensor(f"ps{i}", [128, NMAX], F32)) for i in range(8)]

        sems = {e: sem("s_" + e) for e in ENGS}
        wsem = [sem(f"w{i}") for i in range(WR)]
        stg_ld = [sem(f"stl{i}") for i in range(2)]
        stg_st = [sem(f"sts{i}") for i in range(2)]
        misc_sem = sem("misc")
        BST = [(STG[i][:], ("STG", i), stg_ld[i], stg_st[i]) for i in range(2)]
        if CWID * 128 * 2 >= D * 4:
            for i in range(3):
                dgf = DG[i][:].rearrange("p k c -> p (k c)").bitcast(F32)
                BST.append((dgf[:, 0:D], ("DG", i), sem(f"bld{i}"), sem(f"bst{i}")))
        NBST = len(BST)
        d2d_sem = sem("d2d")
        block = st.enter_context(nc.Block())
        P = Prog(sems)

        psi = [0]

        def bank():
            b = psi[0] % 8
            psi[0] += 1
            return b

        rot = {}

        def nxt(name, n):
            v = rot.get(name, 0)
            rot[name] = v + 1
            return v % n

        def act(out, in_, func, reads, writes, bias=None, scale=None):
            kw = {}
            if bias is not None:
                kw['bias'] = bias
            if scale is not None:
                kw['scale'] = scale
            return P.op('act', lambda e: e.activation(out=out, in_=in_, func=func, **kw), reads, writes)

        def tt(eng, out, in0, in1, op, reads, writes):
            return P.op(eng, lambda e: e.tensor_tensor(out=out, in0=in0, in1=in1, op=op), reads, writes)

        def stt(out, in0, scalar, in1, op0, op1, reads, writes):
            return P.op('dve', lambda e: e.scalar_tensor_tensor(out=out, in0=in0, scalar=scalar, in1=in1, op0=op0, op1=op1), reads, writes)

        def cp(eng, out, in_, reads, writes):
            return P.op(eng, lambda e: e.tensor_copy(out=out, in_=in_), reads, writes)

        def mset(eng, ap, val, writes):
            return P.op(eng, lambda e: e.memset(ap, val), (), writes)

        def mm_group(b, n, pairs, reads, c0=0):
            out = PS[b][:, c0:c0 + n]
            fns = []
            last = len(pairs) - 1
            for i, (l, r) in enumerate(pairs):
                fns.append(lambda e, l=l, r=r, i=i: e.matmul(out, lhsT=l, rhs=r, start=(i == 0), stop=(i == last)))
            return P.group('pe', fns, reads, [("ps", b)])

        def mm_chain(b, n, triples, c0=0):
            out = PS[b][:, c0:c0 + n]
            last = len(triples) - 1
            items = []
            for i, (l_, r_, rd_) in enumerate(triples):
                items.append((lambda e, l_=l_, r_=r_, i=i: e.matmul(out, lhsT=l_, rhs=r_, start=(i == 0), stop=(i == last)), rd_))
            return P.chain('pe', items, [("ps", b)])

        def transposes(b, items, reads):
            fns = []
            for (in_ap, off, r, c) in items:
                fns.append(lambda e, in_ap=in_ap, off=off, r=r, c=c: e.transpose(out=PS[b][0:c, off:off + r], in_=in_ap, identity=ID32[0:r, 0:r]))
            return P.group('pe', fns, reads + ["ID32"], [("ps", b)])

        mset('pool', ID32[:], 0.0, ["ID32"])
        P.op('pool', lambda e: e.affine_select(out=ID32[:], in_=ID32[:], pattern=[[-1, 128]], compare_op=ALU.not_equal,
                                               fill=1.0, base=0, channel_multiplier=1), ["ID32"], ["ID32"])
        cp('dve', IDB[:], ID32[:], ["ID32"], ["IDB"])
        mset('pool', ONESB[:], 1.0 / D, ["ONESB"])
        mset('pool', ONES32[:], 1.0 / D, ["ONES32"])
        mset('pool', EPSR[:], RMS_EPS, ["EPSR"])
        mset('pool', EPSL[:], LN_EPS, ["EPSL"])
        for t in range(PB):
            mset('pool', INV[:, t:t + 1], 1.0 / (t + 1), ["INV"])
        d2d = []
        VEC, vreg, vsem, _ = BST[-1]
        vregs = []

        def vload(dst, src):
            reg = ("VECR", len(vregs))
            vregs.append(reg)
            P.dma('sp', dst, src, vsem, (), [reg])
        for l in range(DEPTH):
            b0 = 36 * l
            vload(VEC[b0:b0 + CWID, :], conv_w[l])
            for j, src in enumerate([conv_b, ln_g, ln_b, norm_g, pscale]):
                vload(VEC[b0 + CWID + j:b0 + CWID + j + 1, :], src[l:l + 1, :])
        vload(VEC[R - 1:R, :], final_g)
        per_bank = max(1, NMAX // R)
        m = 0
        while m < KC:
            b = bank()
            ms = list(range(m, min(KC, m + per_bank)))
            transposes(b, [(VEC[0:R, mm * 128:(mm + 1) * 128], i * R, R, 128) for i, mm in enumerate(ms)], vregs + [vreg])
            for i, mm in enumerate(ms):
                cp('dve', PV[:, mm, :], PS[b][:, i * R:(i + 1) * R], [("ps", b)], ["PV"])
            m += per_bank

        def pv(l, row, mchunk):
            r = 36 * l + row
            return PV[:, mchunk, r:r + 1]

        KH = max(1, KC // 2)
        assert MB % GC == 0 and PG <= BW
        MPB = MB // GC

        def wsrc_in(l, c0):
            v = w_in[l, c0 // BW].rearrange("p (k e) -> p k e", k=KC)
            return lambda slot: [(W[slot][:, k0:k0 + KH, 0:BW], v[:, k0:k0 + KH, :]) for k0 in range(0, KC, KH)]

        def wsrc_sq(wt, l, c0):
            v = wt[l, c0 // BW].rearrange("p (k e) -> p k e", k=KC)
            return lambda slot: [(W[slot][:, k0:k0 + KH, 0:BW], v[:, k0:k0 + KH, :]) for k0 in range(0, KC, KH)]

        def wsrc_mixg(l, g):
            v = w_mix[l, g].rearrange("p (kk e) -> p kk e", kk=GC)
            return lambda slot: [(W[slot][:, 0:GC, 0:PG], v)]

        wlist = []
        for (p0, p1, hs) in cfg.segs:
            for l in range(DEPTH):
                for j in range(NB):
                    wlist.append(wsrc_in(l, 0 * D + j * BW))
                    wlist.append(wsrc_in(l, 1 * D + j * BW))
                    wlist.append(wsrc_in(l, 3 * D + j * BW))
                for j in range(NB):
                    wlist.append(wsrc_in(l, 2 * D + j * BW))
                    wlist.append(wsrc_in(l, 5 * D + j * BW))
                for j in range(NB):
                    wlist.append(wsrc_sq(w_co, l, j * BW))
                for j in range(NB):
                    wlist.append(wsrc_in(l, 4 * D + j * BW))
                    for gg in range(MPB):
                        wlist.append(wsrc_mixg(l, j * MPB + gg))
                for j in range(NB):
                    wlist.append(wsrc_in(l, 6 * D + j * BW))
                for j in range(NB):
                    wlist.append(wsrc_sq(w_po, l, j * BW))
                for j in range(NB):
                    wlist.append(wsrc_sq(w_o, l, j * BW))
        wstate = {'issued': 0, 'used': 0}
        nseg_ = len(cfg.segs)
        NBLK = len(wlist) // nseg_
        use_wsc = cfg.wscratch and nseg_ > 1
        if use_wsc:
            wsc = nc.dram_tensor("wsc", [NBLK, 128, KC * BW], BF16, kind="Internal").ap()
            wst = [sem(f"wst{i}") for i in range(WR)]
        def w_issue():
            i = wstate['issued']
            if i >= len(wlist):
                return
            slot = i % WR
            if use_wsc and i >= NBLK:
                P.dma('pool', W[slot][:].rearrange("p k e -> p (k e)"), wsc[i % NBLK], wsem[slot], [("WSC", i % NBLK)], [("W", slot)])
            else:
                for (dst_, src_) in wlist[i](slot):
                    P.dma('pool', dst_, src_, wsem[slot], (), [("W", slot)])
                if use_wsc:
                    P.dma('sp', wsc[i], W[slot][:].rearrange("p k e -> p (k e)"), wst[slot], [("W", slot)], [("WSC", i)])
            wstate['issued'] = i + 1

        def w_take():
            i = wstate['used']
            wstate['used'] = i + 1
            assert i < wstate['issued']
            return i % WR

        def w_release(n=1):
            for _ in range(n):
                w_issue()

        out_toks = []
        nseg = len(cfg.segs)
        NPF = 2
        bst_state = {'i': 0, 'reserved': set(), 'pre': []}

        def bst_next():
            while True:
                i_ = bst_state['i'] % NBST
                bst_state['i'] += 1
                if i_ not in bst_state['reserved']:
                    return i_

        def seg_blocks(p0_, p1_, hs_):
            out_ = []
            pos_ = p0_
            while pos_ < p1_:
                n_ = min(128, p1_ - pos_)
                srcs_ = []
                q_ = pos_
                while q_ < pos_ + n_:
                    if q_ < NMETA:
                        k_ = min(NMETA, pos_ + n_) - q_
                        srcs_.append((meta[q_:q_ + k_, :], k_))
                    else:
                        k_ = pos_ + n_ - q_
                        srcs_.append((xp[q_ - NMETA:q_ - NMETA + k_, :], k_))
                    q_ += k_
                out_.append((srcs_, n_, pos_ - p0_))
                pos_ += n_
            if hs_:
                for r0_ in range(0, TS, 128):
                    n_ = min(128, TS - r0_)
                    out_.append(([(xs[r0_:r0_ + n_, :], n_)], n_, (p1_ - p0_) + r0_))
            return out_

        def issue_row_dmas(rows_src):
            i_ = bst_next()
            st_ap, st_reg, st_ld, _ = BST[i_]
            r_ = 0
            for (ap_, k_) in rows_src:
                P.dma('sp', st_ap[r_:r_ + k_, :], ap_, st_ld, (), [st_reg])
                r_ += k_
            return i_
        try:
            chk(0)
            _emit_all = True
        except _Stop:
            _emit_all = False
        for si, (p0, p1, hs) in enumerate(cfg.segs if _emit_all else []):
          try:
              _cur['si'] = si
              Tp = p1 - p0
              T = Tp + (TS if hs else 0)
              tiles = seg_tiles(p0, p1, hs)
              nt = cfg.ntile
              for (a, b_, kd) in tiles:
                  assert b_ - a <= NMAX
              first_seg = (p0 == 0)
              last_prompt_seg = (p1 == PT)
              if last_prompt_seg:
                  assert tiles[nt - 1][2] >= CB

              def XR(ti):
                  return ("X", ti)

              def HRL(ti):
                  return [("H", k_, ti) for k_ in range(KC)]

              def load_rows(rows_src, n, c0, pre=None):
                  bi = issue_row_dmas(rows_src) if pre is None else pre
                  st_ap, st_reg, st_ld, _ = BST[bi]
                  bst_state['reserved'].discard(bi)
                  for h in range(0, KC, 4):
                      b = bank()
                      mm_ = list(range(h, min(KC, h + 4)))
                      transposes(b, [(st_ap[0:n, q * 128:(q + 1) * 128], i * 128, n, 128) for i, q in enumerate(mm_)], [st_reg])
                      wr = [("X", ti) for ti, (a, b2, kd) in enumerate(tiles) if not (b2 <= c0 or a >= c0 + n)]
                      act(X[:, h:h + len(mm_), c0:c0 + n], PS[b][:, 0:len(mm_) * 128].rearrange("p (a t) -> p a t", t=128)[:, :, 0:n],
                          AF.Copy, [("ps", b)], wr)

              pre_list = bst_state['pre']
              bst_state['pre'] = []
              for bi_, (srcs, n, c0_) in enumerate(seg_blocks(p0, p1, hs)):
                  load_rows(srcs, n, c0_, pre=(pre_list[bi_] if bi_ < len(pre_list) else None))

              chk(1)
              if si == 0:
                  for _ in range(WR):
                      w_issue()
                  for l_ in range(DEPTH):
                      out_toks.append(P.dma('sp', ncs[l_, :, 0:CB - DS, :], sc[l_, :, DS:CB, :], d2d_sem))
                      out_toks.append(P.dma('sp', nps[l_, :, 0:PB - DS, :], spl[l_, :, DS:PB, :], d2d_sem))

              def rms_stats(ti, a, b_):
                  n = b_ - a
                  bk = bank()
                  for mch in range(KC):
                      act(CC[:, mch, a:b_], X[:, mch, a:b_], AF.Square, [XR(ti)], [("CC", mch, ti)])
                  for mch in range(KC):
                      P.op('pe', lambda e, mch=mch, bk=bk, n=n: e.matmul(PS[bk][:, 0:n], lhsT=ONESB[:], rhs=CC[:, mch, a:b_],
                                                                       start=(mch == 0), stop=(mch == KC - 1)),
                           [("CC", mch, ti), "ONESB"], [("ps", bk)])
                  act(RR[:, a:b_], PS[bk][:, 0:n], AF.Ln, [("ps", bk), "EPSR"], [("A1", ti)], bias=EPSR[:, 0:1])
                  act(RR[:, a:b_], RR[:, a:b_], AF.Exp, [("A1", ti)], [("A1", ti)], scale=-0.5)

              for l in range(DEPTH):
                  for ti, (a, b_, kd) in enumerate(tiles):
                      rms_stats(ti, a, b_)
                      for mch in range(KC):
                          stt(H[:, mch, a:b_], X[:, mch, a:b_], pv(l, CWID + 3, mch), RR[:, a:b_], ALU.mult, ALU.mult,
                              [XR(ti), ("A1", ti), "PV"], [("H", mch, ti)])

                  chk(2)
                  if hs:
                      SPG = 128 // CB
                      for g0 in range(0, NS, SPG):
                          ng = min(SPG, NS - g0)
                          s = nxt("stg", 2)
                          P.dma('sp', STG[s][0:ng * CB, :], sc[l, g0:g0 + ng].rearrange("s j d -> (s j) d"), stg_ld[s], (), [("STG", s)])
                          for h in range(0, KC, 4):
                              b = bank()
                              mm_ = list(range(h, min(KC, h + 4)))
                              transposes(b, [(STG[s][0:ng * CB, q * 128:(q + 1) * 128], i * 128, ng * CB, 128) for i, q in enumerate(mm_)], [("STG", s)])
                              for i, q in enumerate(mm_):
                                  cp('dve', UX[:, q, g0:g0 + ng, 0:CB], PS[b][:, i * 128:i * 128 + ng * CB].rearrange("p (s j) -> p s j", j=CB),
                                     [("ps", b)], [("UX", q)])
                      SPG2 = 128 // PB
                      for g0 in range(0, NS, SPG2):
                          ng = min(SPG2, NS - g0)
                          s = nxt("stg", 2)
                          P.dma('sp', STG[s][0:ng * PB, :], spl[l, g0:g0 + ng].rearrange("s j d -> (s j) d"), stg_ld[s], (), [("STG", s)])
                          for h in range(0, KC, 4):
                              b = bank()
                              mm_ = list(range(h, min(KC, h + 4)))
                              transposes(b, [(STG[s][0:ng * PB, q * 128:(q + 1) * 128], i * 128, ng * PB, 128) for i, q in enumerate(mm_)], [("STG", s)])
                              for i, q in enumerate(mm_):
                                  cp('dve', PX[:, q, g0:g0 + ng, 0:PB], PS[b][:, i * 128:i * 128 + ng * PB].rearrange("p (s j) -> p s j", j=PB),
                                     [("ps", b)], [("PX", q)])

                  chk(20)
                  mset('dve', S1[:, 0:T], 0.0, [("A1", ti) for ti in range(len(tiles))])
                  mset('dve', S2[:, 0:T], 0.0, [("A2", ti) for ti in range(len(tiles))])
                  uslot_of = {}

                  def glu(mch, wv, wg, mloc):
                      us = mch % 2
                      uslot_of[mch] = us
                      ds_ = mch % 3
                      base_r = 36 * l
                      def dg_build(k0, k1):
                          nk = k1 - k0
                          P.op('dve', lambda e: e.tensor_tensor(out=DG[ds_][:, k0:k1, :], in0=IDB[:].unsqueeze(1).broadcast_to([128, nk, 128]),
                                                                in1=PV[:, mch, base_r + k0:base_r + k1].unsqueeze(2).broadcast_to([128, nk, 128]),
                                                                op=ALU.mult), ["IDB", "PV"], [("DG", ds_)])
                      dg_parts = [(0, 11), (11, 21), (21, CWID)]
                      dg_build(*dg_parts[0])
                      if first_seg:
                          mset('dve', U[us][:, 0:CB], 0.0, [("U", us)])
                      else:
                          cp('dve', U[us][:, 0:CB], UT[l][:, mch, :], [("UT", l, mch)], [("U", us)])
                      for ti, (a, b_, kd) in enumerate(tiles):
                          n = b_ - a
                          bg = bank()
                          mm_chain(bg, n, [(W[wg][:, k, mloc * 128:(mloc + 1) * 128], H[:, k, a:b_], [("W", wg), ("H", k, ti)]) for k in range(KC)])
                          bv = bank()
                          mm_chain(bv, n, [(W[wv][:, k, mloc * 128:(mloc + 1) * 128], H[:, k, a:b_], [("W", wv), ("H", k, ti)]) for k in range(KC)])
                          sg = nxt("sg", 4)
                          act(SG[sg][:, 0:n], PS[bg][:, 0:n], AF.Sigmoid, [("ps", bg)], [("SG", sg)])
                          if kd > 0:
                              tt('dve', U[us][:, CB + a:CB + a + kd], PS[bv][:, 0:kd], SG[sg][:, 0:kd], ALU.mult, [("ps", bv), ("SG", sg)], [("U", us)])
                              if last_prompt_seg and a + kd == Tp:
                                  tt('dve', UL[:, mch, :], PS[bv][:, kd - CB:kd], SG[sg][:, kd - CB:kd], ALU.mult, [("ps", bv), ("SG", sg)], [("UL", mch)])
                          if kd < n:
                              tt('dve', UX[:, mch, :, CB:CB + DS], PS[bv][:, kd:n].rearrange("p (s j) -> p s j", j=DS),
                                 SG[sg][:, kd:n].rearrange("p (s j) -> p s j", j=DS), ALU.mult, [("ps", bv), ("SG", sg)], [("UX", mch)])
                              tt('dve', US[:, mch, :], PS[bv][:, kd:n], SG[sg][:, kd:n], ALU.mult, [("ps", bv), ("SG", sg)], [("US", mch)])
                          if ti + 1 < len(dg_parts):
                              dg_build(*dg_parts[ti + 1])
                      for pi_ in range(len(tiles) + 1, len(dg_parts)):
                          dg_build(*dg_parts[pi_])
                      if not last_prompt_seg:
                          cp('dve', UT[l][:, mch, :], U[us][:, Tp:Tp + CB], [("U", us)], [("UT", l, mch)])

                  def conv(mch):
                      us = uslot_of[mch]
                      ds_ = mch % 3
                      for ti, (a, b_, kd) in enumerate(tiles):
                          n = b_ - a
                          bc = bank()
                          if kd > 0:
                              mm_group(bc, kd, [(DG[ds_][:, k, :], U[us][:, a + k:a + kd + k]) for k in range(CWID)], [("DG", ds_), ("U", us)])
                          if kd < n:
                              mm_group(bc, n - kd, [(DG[ds_][:, k, :], UX[:, mch, :, k:k + DS]) for k in range(CWID)], [("DG", ds_), ("UX", mch)], c0=kd)
                          cb_ap = pv(l, CWID + 0, mch)
                          act(CC[:, mch, a:b_], PS[bc][:, 0:n], AF.Identity, [("ps", bc), "PV"], [("CC", mch, ti)], bias=cb_ap)
                          cs = nxt("t2", 4)
                          act(T2[cs][:, 0:n], PS[bc][:, 0:n], AF.Square, [("ps", bc), "PV"], [("T2", cs)], bias=cb_ap)
                          tt('dve', S1[:, a:b_], S1[:, a:b_], CC[:, mch, a:b_], ALU.add, [("CC", mch, ti)], [("A1", ti)])
                          tt('dve', S2[:, a:b_], S2[:, a:b_], T2[cs][:, 0:n], ALU.add, [("T2", cs)], [("A2", ti)])

                  pool_first = [False]

                  def pin(mch, wp, mloc):
                      pslot = mch % 2
                      wwin = WINS[mch // GC]
                      if first_seg:
                          mset('dve', PIN[pslot][:, 0:PB], 0.0, [("PIN", pslot)])
                      else:
                          cp('dve', PIN[pslot][:, 0:PB], PTL[l][:, mch, :], [("PTL", l, mch)], [("PIN", pslot)])
                      for ti, (a, b_, kd) in enumerate(tiles):
                          n = b_ - a
                          bp = bank()
                          mm_group(bp, n, [(W[wp][:, k, mloc * 128:(mloc + 1) * 128], H[:, k, a:b_]) for k in range(KC)], [("W", wp)] + HRL(ti))
                          if kd > 0:
                              act(PIN[pslot][:, PB + a:PB + a + kd], PS[bp][:, 0:kd], AF.Copy, [("ps", bp)], [("PIN", pslot)])
                          if kd < n:
                              act(PX[:, mch, :, PB:PB + DS], PS[bp][:, kd:n].rearrange("p (s j) -> p s j", j=DS), AF.Copy, [("ps", bp)], [("PX", mch)])
                              act(PSN[:, mch, :], PS[bp][:, kd:n], AF.Copy, [("ps", bp)], [("PSN", mch)])
                      if not last_prompt_seg:
                          cp('dve', PTL[l][:, mch, :], PIN[pslot][:, Tp:Tp + PB], [("PIN", pslot)], [("PTL", l, mch)])
                      else:
                          cp('dve', PL15[:, mch, :], PIN[pslot][:, Tp:Tp + PB], [("PIN", pslot)], [("PL15", mch)])

                      def pool_region(xap_fn, E, rd_reg, outs):
                          cur = xap_fn
                          sh = 1
                          lvl = 0
                          cur_reg = rd_reg
                          while sh < wwin:
                              dst = TA if lvl % 2 == 0 else TB
                              dreg = "MGALL"
                              lo = 2 * sh - 1
                              wr_ = [dreg]
                              if not pool_first[0]:
                                  pool_first[0] = True
                                  wr_ = [dreg] + [("MG", k_, t_) for k_ in range(KC) for t_ in range(len(tiles))]
                              tt('dve', dst[:, lo:E], cur(lo, E), cur(lo - sh, E - sh), ALU.add, [cur_reg], wr_)
                              cur = (lambda d: (lambda lo_, hi_: d[:, lo_:hi_]))(dst)
                              cur_reg = dreg
                              sh *= 2
                              lvl += 1
                          outs(cur, cur_reg)

                      E = PB + Tp

                      def outs_p(cur, cur_reg, pslot=pslot, mch=mch, wwin=wwin):
                          for ti, (a, b_, kd) in enumerate(tiles):
                              if kd == 0:
                                  continue
                              stt(QF[:, mch, a:a + kd], cur(PB + a, PB + a + kd), 1.0 / wwin, PIN[pslot][:, PB + a:PB + a + kd], ALU.mult, ALU.subtract,
                                  [cur_reg, ("PIN", pslot)], [("QF", mch, ti)])
                          if first_seg:
                              nf = wwin - 1
                              tt('dve', T1[0][:, 0:nf], cur(PB, PB + nf), INV[:, 0:nf], ALU.mult, [cur_reg, "INV"], [("T1", 0)])
                              tt('dve', QF[:, mch, 0:nf], T1[0][:, 0:nf], PIN[pslot][:, PB:PB + nf], ALU.subtract, [("T1", 0), ("PIN", pslot)], [("QF", mch, 0)])
                      pool_region(lambda lo, hi, pslot=pslot: PIN[pslot][:, lo:hi], E, ("PIN", pslot), outs_p)
                      if hs:
                          E2 = NS * (PB + DS)
                          flat = PX[:, mch, :, :].rearrange("p s j -> p (s j)")

                          def outs_s(cur, cur_reg, mch=mch, wwin=wwin):
                              ti = len(tiles) - 1
                              a, b_, kd = tiles[ti]
                              cur3 = cur(0, E2).rearrange("p (s j) -> p s j", j=PB + DS)
                              stt(QF[:, mch, a + kd:b_].rearrange("p (s j) -> p s j", j=DS), cur3[:, :, PB:PB + DS], 1.0 / wwin, PX[:, mch, :, PB:PB + DS],
                                  ALU.mult, ALU.subtract, [cur_reg, ("PX", mch)], [("QF", mch, ti)])
                          pool_region(lambda lo, hi, flat=flat: flat[:, lo:hi], E2, ("PX", mch), outs_s)

                  wv = wg = wp = None
                  for mch in range(KC):
                      if mch % MB == 0:
                          wv = w_take()
                          wg = w_take()
                          wp = w_take()
                      glu(mch, wv, wg, mch % MB)
                      if mch >= 1:
                          conv(mch - 1)
                      pin(mch, wp, mch % MB)
                      if mch % MB == MB - 1:
                          w_release(3)
                  conv(KC - 1)

                  chk(3)
                  def out_rows_from_fm(src_fn, nrows, dst_fn, rd):
                      s = nxt("stg", 2)
                      for h in range(0, KC, 4):
                          b = bank()
                          mm_ = list(range(h, min(KC, h + 4)))
                          transposes(b, [(src_fn(q), i * 128, 128, nrows) for i, q in enumerate(mm_)], [r for q in mm_ for r in rd(q)])
                          act(STG[s][0:nrows, h * 128:(h + len(mm_)) * 128], PS[b][0:nrows, 0:len(mm_) * 128], AF.Copy, [("ps", b)], [("STG", s)])
                      return dst_fn(s)

                  if last_prompt_seg:
                      out_toks.append(out_rows_from_fm(lambda q: UL[:, q, :], CB,
                                                       lambda s: P.dma('sp', ncp[l], STG[s][0:CB, :], stg_st[s], [("STG", s)], ()),
                                                       lambda q: [("UL", q)]))
                  if hs:
                      for r0 in range(0, TS, 128):
                          n = min(128, TS - r0)

                          def dst(s, r0=r0, n=n):
                              tk = None
                              for sq_ in range(r0 // DS, (r0 + n) // DS):
                                  tk = P.dma('sp', ncs[l, sq_, CB - DS:CB, :], STG[s][(sq_ * DS - r0):(sq_ * DS - r0) + DS, :], stg_st[s], [("STG", s)], ())
                              return tk
                          out_toks.append(out_rows_from_fm(lambda q, r0=r0, n=n: US[:, q, r0:r0 + n], n, dst, lambda q: [("US", q)]))

                  chk(30)
                  for ti, (a, b_, kd) in enumerate(tiles):
                      n = b_ - a
                      b1 = bank()
                      P.op('pe', lambda e, b1=b1, a=a, b_=b_, n=n: e.matmul(PS[b1][:, 0:n], lhsT=ONES32[:], rhs=S1[:, a:b_], start=True, stop=True),
                           [("A1", ti), "ONES32"], [("ps", b1)])
                      b2 = bank()
                      P.op('pe', lambda e, b2=b2, a=a, b_=b_, n=n: e.matmul(PS[b2][:, 0:n], lhsT=ONES32[:], rhs=S2[:, a:b_], start=True, stop=True),
                           [("A2", ti), "ONES32"], [("ps", b2)])
                      cp('dve', MU[:, a:b_], PS[b1][:, 0:n], [("ps", b1)], [("A1", ti)])
                      tt('dve', RS[:, a:b_], MU[:, a:b_], MU[:, a:b_], ALU.mult, [("A1", ti)], [("A2", ti)])
                      tt('dve', RS[:, a:b_], PS[b2][:, 0:n], RS[:, a:b_], ALU.subtract, [("ps", b2), ("A2", ti)], [("A2", ti)])
                      act(RS[:, a:b_], RS[:, a:b_], AF.Ln, [("A2", ti), "EPSL"], [("A2", ti)], bias=EPSL[:, 0:1])
                      act(RS[:, a:b_], RS[:, a:b_], AF.Exp, [("A2", ti)], [("A2", ti)], scale=-0.5)
                  pend_fin = [None]
                  for blk in range(NB):
                      ws = w_take()
                      wa = w_take()
                      chs = list(range(blk * MB, (blk + 1) * MB))
                      if blk == NB - 1:
                          order = [(m_, t_) for t_ in range(len(tiles)) for m_ in chs]
                      else:
                          order = [(m_, t_) for m_ in chs for t_ in range(len(tiles))]
                      for (mch, ti) in order:
                          mloc = mch % MB
                          a, b_, kd = tiles[ti]
                          n = b_ - a
                          bs = bank()
                          mm_group(bs, n, [(W[ws][:, k, mloc * 128:(mloc + 1) * 128], H[:, k, a:b_]) for k in range(KC)], [("W", ws)] + HRL(ti))
                          bg = bank()
                          mm_group(bg, n, [(W[wa][:, k, mloc * 128:(mloc + 1) * 128], H[:, k, a:b_]) for k in range(KC)], [("W", wa)] + HRL(ti))
                          sg = nxt("sg", 4)
                          act(SG[sg][:, 0:n], PS[bs][:, 0:n], AF.Silu, [("ps", bs)], [("SG", sg)])
                          act(MG[:, mch, a:b_], PS[bg][:, 0:n], AF.Tanh, [("ps", bg)], [("MG", mch, ti), "MGALL"], scale=0.5)
                          t1 = nxt("t1", 4)
                          tt('dve', T1[t1][:, 0:n], CC[:, mch, a:b_], MU[:, a:b_], ALU.subtract, [("CC", mch, ti), ("A1", ti)], [("T1", t1)])
                          tt('dve', T1[t1][:, 0:n], T1[t1][:, 0:n], RS[:, a:b_], ALU.mult, [("T1", t1), ("A2", ti)], [("T1", t1)])
                          t2 = nxt("t2", 4)
                          act(T2[t2][:, 0:n], T1[t1][:, 0:n], AF.Silu, [("T1", t1), "PV"], [("T2", t2)], bias=pv(l, CWID + 2, mch), scale=pv(l, CWID + 1, mch))
                          if pend_fin[0] is not None:
                              pend_fin[0]()

                          def fin_(mch=mch, a=a, b_=b_, n=n, t2=t2, sg=sg, ti=ti):
                              tt('dve', CC[:, mch, a:b_], T2[t2][:, 0:n], SG[sg][:, 0:n], ALU.mult, [("T2", t2), ("SG", sg)], [("CC", mch, ti)])
                          pend_fin[0] = fin_
                      w_release(2)
                  pend_fin[0]()

                  chk(6)
                  if last_prompt_seg:
                      out_toks.append(out_rows_from_fm(lambda q: PL15[:, q, :], PB,
                                                       lambda s: P.dma('sp', npp[l], STG[s][0:PB, :], stg_st[s], [("STG", s)], ()),
                                                       lambda q: [("PL15", q)]))
                  if hs:
                      for r0 in range(0, TS, 128):
                          n = min(128, TS - r0)

                          def dst2(s, r0=r0, n=n):
                              tk = None
                              for sq_ in range(r0 // DS, (r0 + n) // DS):
                                  tk = P.dma('sp', nps[l, sq_, PB - DS:PB, :], STG[s][(sq_ * DS - r0):(sq_ * DS - r0) + DS, :], stg_st[s], [("STG", s)], ())
                              return tk
                          out_toks.append(out_rows_from_fm(lambda q, r0=r0, n=n: PSN[:, q, r0:r0 + n], n, dst2, lambda q: [("PSN", q)]))

                  chk(4)
                  wcs = [w_take() for _ in range(NB)]
                  for ti, (a, b_, kd) in enumerate(tiles):
                      n = b_ - a
                      for mch in range(KC):
                          wc = wcs[mch // MB]
                          mloc = mch % MB
                          ba = bank()
                          mm_group(ba, n, [(W[wc][:, k, mloc * 128:(mloc + 1) * 128], CC[:, k, a:b_]) for k in range(KC)],
                                   [("W", wc)] + [("CC", k, ti) for k in range(KC)])
                          stt(MG[:, mch, a:b_], MG[:, mch, a:b_], 1.0, PS[ba][:, 0:n], ALU.add, ALU.mult, [("ps", ba)], [("MG", mch, ti)])
                          if ti == len(tiles) - 1 and mch % MB == MB - 1:
                              w_release(1)

                  chk(60)
                  wms = None
                  wps = None
                  for mch in range(KC):
                      if mch % MB == 0:
                          wps = w_take()
                          wms = [w_take() for _ in range(MPB)]
                      mloc = mch % MB
                      g = mch // GC
                      wm = wms[mloc // GC]
                      for ti, (a, b_, kd) in enumerate(tiles):
                          n = b_ - a
                          bq = bank()
                          mm_group(bq, n, [(W[wm][:, kk, (mch % GC) * 128:(mch % GC + 1) * 128], QF[:, g * GC + kk, a:b_]) for kk in range(GC)],
                                   [("W", wm)] + [("QF", g * GC + kk, ti) for kk in range(GC)])
                          bs = bank()
                          mm_group(bs, n, [(W[wps][:, k, mloc * 128:(mloc + 1) * 128], H[:, k, a:b_]) for k in range(KC)], [("W", wps)] + HRL(ti))
                          sg = nxt("sg", 4)
                          act(SG[sg][:, 0:n], PS[bs][:, 0:n], AF.Silu, [("ps", bs)], [("SG", sg)])
                          stt(CC[:, mch, a:b_], PS[bq][:, 0:n], pv(l, CWID + 4, mch), SG[sg][:, 0:n], ALU.mult, ALU.mult,
                              [("ps", bq), ("SG", sg), "PV"], [("CC", mch, ti)])
                      if mch % MB == MB - 1:
                          w_release(1 + MPB)

                  chk(7)
                  wo_ = wb = None
                  for mch in range(KC):
                      if mch % MB == 0:
                          wb = w_take()
                      mloc = mch % MB
                      for ti, (a, b_, kd) in enumerate(tiles):
                          n = b_ - a
                          bg = bank()
                          mm_group(bg, n, [(W[wb][:, k, mloc * 128:(mloc + 1) * 128], H[:, k, a:b_]) for k in range(KC)], [("W", wb)] + HRL(ti))
                          act(QF[:, mch, a:b_], PS[bg][:, 0:n], AF.Sigmoid, [("ps", bg)], [("QF", mch, ti)])
                      if mch % MB == MB - 1:
                          w_release(1)
                  for mch in range(KC):
                      if mch % MB == 0:
                          wo_ = w_take()
                      mloc = mch % MB
                      for ti, (a, b_, kd) in enumerate(tiles):
                          n = b_ - a
                          bb = bank()
                          mm_group(bb, n, [(W[wo_][:, k, mloc * 128:(mloc + 1) * 128], CC[:, k, a:b_]) for k in range(KC)],
                                   [("W", wo_)] + [("CC", k, ti) for k in range(KC)])
                          t1 = nxt("t1", 4)
                          tt('dve', T1[t1][:, 0:n], PS[bb][:, 0:n], QF[:, mch, a:b_], ALU.mult, [("ps", bb), ("QF", mch, ti)], [("T1", t1)])
                          stt(MG[:, mch, a:b_], T1[t1][:, 0:n], 2.0, MG[:, mch, a:b_], ALU.mult, ALU.add, [("T1", t1)], [("MG", mch, ti)])
                      if mch % MB == MB - 1:
                          w_release(1)

                  chk(8)
                  wws = [w_take() for _ in range(NB)]
                  act(DUMMY[:, 0:1], EPSR[:, 0:1], AF.Ln, ["EPSR"], ["DUMMY"])
                  for ti, (a, b_, kd) in enumerate(tiles):
                      n = b_ - a
                      for mch in range(KC):
                          ww = wws[mch // MB]
                          mloc = mch % MB
                          bo = bank()
                          mm_group(bo, n, [(W[ww][:, k, mloc * 128:(mloc + 1) * 128], MG[:, k, a:b_]) for k in range(KC)],
                                   [("W", ww)] + [("MG", k, ti) for k in range(KC)])
                          stt(X[:, mch, a:b_], PS[bo][:, 0:n], 0.5, X[:, mch, a:b_], ALU.mult, ALU.add, [("ps", bo)], [XR(ti)])
                          if ti == len(tiles) - 1 and mch % MB == MB - 1:
                              w_release(1)

              chk(9)
              for ti, (a, b_, kd) in enumerate(tiles):
                  rms_stats(ti, a, b_)
                  for mch in range(KC):
                      stt(X[:, mch, a:b_], X[:, mch, a:b_], PV[:, mch, R - 1:R], RR[:, a:b_], ALU.mult, ALU.mult,
                          [XR(ti), ("A1", ti), "PV"], [XR(ti)])

              def xregs(c0, n):
                  return [("X", ti) for ti, (a, b2, kd) in enumerate(tiles) if not (b2 <= c0 or a >= c0 + n)]

              def store_rows(c0, n, dst_fn):
                  bst = BST[bst_next()]
                  st_ap, st_reg = bst[0], bst[1]
                  for h in range(0, KC, 4):
                      b = bank()
                      mm_ = list(range(h, min(KC, h + 4)))
                      transposes(b, [(X[:, q, c0:c0 + n], i * 128, 128, n) for i, q in enumerate(mm_)], xregs(c0, n))
                      act(st_ap[0:n, h * 128:(h + len(mm_)) * 128], PS[b][0:n, 0:len(mm_) * 128], AF.Copy, [("ps", b)], [st_reg])
                  return dst_fn(bst)

              if si + 1 < nseg and NBST > NPF + 1:
                  for (srcs_n, n_n, c0_n) in seg_blocks(*cfg.segs[si + 1])[:NPF]:
                      bi_n = issue_row_dmas(srcs_n)
                      bst_state['reserved'].add(bi_n)
                      bst_state['pre'].append(bi_n)
              pos = p0
              while pos < p1:
                  n = min(128, p1 - pos)
                  lo = max(pos, NMETA)
                  if lo < pos + n:
                      def dsty(bst, pos=pos, n=n, lo=lo):
                          return P.dma('sp', yp[lo - NMETA:pos + n - NMETA, :], bst[0][lo - pos:n, :], bst[3], [bst[1]], ())
                      out_toks.append(store_rows(pos - p0, n, dsty))
                  pos += n
              chk(10)
              if hs:
                  for r0 in range(0, TS, 128):
                      n = min(128, TS - r0)

                      def dsts(bst, r0=r0, n=n):
                          return P.dma('sp', ys[r0:r0 + n, :], bst[0][0:n, :], bst[3], [bst[1]], ())
                      out_toks.append(store_rows(Tp + r0, n, dsts))
          except _Stop:
            break

        finals = {}
        for tk in out_toks:
            if tk is None:
                continue
            k = tk[0].name
            tot = P.dcnt[k]
            finals[k] = (tk[0], tot)
        P.wait_all('sp', list(finals.values()))

        @block.sync
        def _(e):
            P.replay('sp', e)

        @block.gpsimd
        def _(e):
            P.replay('pool', e)

        @block.tensor
        def _(e):
            P.replay('pe', e)

        @block.scalar
        def _(e):
            P.replay('act', e)

        @block.vector
        def _(e):
            P.replay('dve', e)
    return nc


def make_in_maps(cfg, ncores, x_prompt, x_sample, state_conv, state_pool, meta_tokens, norm_g, w_in, conv_w, conv_b,
                 ln_g, ln_b, w_conv_out, w_pool_mix, pool_scale, w_pool_out, w_out, final_g):
    f = lambda a: np.ascontiguousarray(np.asarray(a, dtype=np.float32))
    NS, D = cfg.NS, cfg.D
    KC, BW, GC = cfg.KC, cfg.BW, cfg.GC
    PG = D // 4

    def pack_blocks(w):
        w = f(w)
        L, _, C = w.shape
        return np.ascontiguousarray(w.reshape(L, KC, 128, C // BW, BW).transpose(0, 3, 2, 1, 4)).reshape(L, C // BW, 128, KC * BW)

    def pack_mix(w):
        w = f(w)
        L = w.shape[0]
        return np.ascontiguousarray(w.reshape(L, 4, GC, 128, PG).transpose(0, 1, 3, 2, 4)).reshape(L, 4, 128, GC * PG)

    shared = dict(meta=f(meta_tokens), norm_g=f(norm_g), w_in=pack_blocks(w_in), conv_w=f(conv_w), conv_b=f(conv_b), ln_g=f(ln_g),
                  ln_b=f(ln_b), w_co=pack_blocks(w_conv_out), w_mix=pack_mix(w_pool_mix), pscale=f(pool_scale),
                  w_po=pack_blocks(w_pool_out), w_o=pack_blocks(w_out), final_g=f(final_g).reshape(1, D))
    x_prompt = np.asarray(x_prompt); x_sample = np.asarray(x_sample)
    state_conv = np.asarray(state_conv); state_pool = np.asarray(state_pool)
    maps = []
    for c in range(ncores):
        m = dict(shared)
        m["xp"] = f(x_prompt[c])
        m["xs"] = f(x_sample[c * NS:(c + 1) * NS]).reshape(NS * DS, D)
        m["sc"] = f(state_conv[:, c * NS:(c + 1) * NS])
        m["spl"] = f(state_pool[:, c * NS:(c + 1) * NS])
        maps.append(m)
    return maps


def gather(cfg, results):
    NS, D = cfg.NS, cfg.D
    y_prompt = np.stack([np.asarray(r["yp"]) for r in results], axis=0).astype(np.float32)
    y_sample = np.concatenate([np.asarray(r["ys"]).reshape(NS, DS, D) for r in results], axis=0).astype(np.float32)
    ncp = np.stack([np.asarray(r["ncp"]) for r in results], axis=1).astype(np.float32)
    npp = np.stack([np.asarray(r["npp"]) for r in results], axis=1).astype(np.float32)
    ncs = np.concatenate([np.asarray(r["ncs"]) for r in results], axis=1).astype(np.float32)
    nps = np.concatenate([np.asarray(r["nps"]) for r in results], axis=1).astype(np.float32)
    return (y_prompt, y_sample, ncp, npp, ncs, nps)


def kernel(**inputs):
    cfg = REAL
    nc = build_program(cfg)
    maps = make_in_maps(cfg, 8, **inputs)
    res = run_bass_kernel_spmd(nc, maps, core_ids=list(range(8)))
    return gather(cfg, res.results)
```

```python
import numpy as np
from contextlib import ExitStack
import concourse.bass as bass
import concourse.mybir as mybir
from concourse.bass_utils import run_bass_kernel_spmd

F32 = mybir.dt.float32
BF16 = mybir.dt.bfloat16
AF = mybir.ActivationFunctionType
ALU = mybir.AluOpType

ENGS = ['pe', 'act', 'dve', 'pool', 'sp']
CWID = 31
CB = 30
PB = 15
WINS = (2, 4, 8, 16)
NMETA = 16
DS = 8
RMS_EPS = 1e-6
LN_EPS = 1e-5
DEPTH = 2


class Cfg:
    def __init__(self, D=1024, SEQ=2048, NS=16, segs=None, ntile=2, BW=256, WR=6, merge_sample=True, wscratch=True):
        self.D = D
        self.KC = D // 128
        self.SEQ = SEQ
        self.PT = SEQ + NMETA
        self.NS = NS
        self.TS = NS * DS
        self.GC = (D // 4) // 128
        assert self.GC >= 1
        self.segs = segs
        self.ntile = ntile
        self.BW = BW
        self.merge_sample = merge_sample
        self.wscratch = wscratch
        self.WR = WR
        self.MB = BW // 128
        self.NB = D // BW
        self.R = 36 * DEPTH + 1


REAL = Cfg(segs=[(0, 736, False), (736, 1472, False), (1472, 2064, True)], WR=8)


class _Stop(Exception):
    pass


_cur = {'si': 0}


def chk(n):
    return


class Prog:
    def __init__(self, sems):
        self.ops = {e: [] for e in ENGS}
        self.cnt = {e: 0 for e in ENGS}
        self.sem = sems
        self.waited = {e: {} for e in ENGS}
        self.dcnt = {}
        self.lastw = {}
        self.readers = {}

    def _waits(self, eng, deps):
        waits = {}
        for d in deps:
            if d is None:
                continue
            sem, val = d
            key = sem.name
            if eng == 'pe' and key == self.sem['pe'].name:
                continue
            if self.waited[eng].get(key, 0) >= val:
                continue
            if key in waits and waits[key][1] >= val:
                continue
            waits[key] = (sem, val)
        for key, (sem, val) in waits.items():
            self.waited[eng][key] = val
        return list(waits.values())

    def _deps(self, reads, writes):
        deps = []
        for r in reads:
            deps.append(self.lastw.get(r))
        for w in writes:
            deps.append(self.lastw.get(w))
            deps.extend(self.readers.get(w, {}).values())
        return deps

    def _mark(self, tok, reads, writes):
        key = tok[0].name
        for r in reads:
            d = self.readers.setdefault(r, {})
            if key not in d or d[key][1] < tok[1]:
                d[key] = tok
        for w in writes:
            self.lastw[w] = tok
            self.readers[w] = {}

    def op(self, eng, fn, reads=(), writes=()):
        return self.group(eng, [fn], reads, writes)

    def group(self, eng, fns, reads=(), writes=()):
        waits = self._waits(eng, self._deps(reads, writes))
        self.cnt[eng] += 1
        tok = (self.sem[eng], self.cnt[eng])
        n = len(fns)
        for i, fn in enumerate(fns):
            self.ops[eng].append((waits if i == 0 else [], fn, self.sem[eng] if i == n - 1 else None, 1))
        self._mark(tok, reads, writes)
        return tok

    def chain(self, eng, items, writes=()):
        self.cnt[eng] += 1
        tok = (self.sem[eng], self.cnt[eng])
        n = len(items)
        allreads = []
        for i, (fn, reads) in enumerate(items):
            deps = [self.lastw.get(r) for r in reads]
            if i == 0:
                for w in writes:
                    deps.append(self.lastw.get(w))
                    deps.extend(self.readers.get(w, {}).values())
            waits = self._waits(eng, deps)
            self.ops[eng].append((waits, fn, self.sem[eng] if i == n - 1 else None, 1))
            allreads.extend(reads)
        self._mark(tok, allreads, writes)
        return tok

    def dma(self, eng, out, in_, sem, reads=(), writes=()):
        waits = self._waits(eng, self._deps(reads, writes))
        key = sem.name
        self.dcnt[key] = self.dcnt.get(key, 0) + 16
        tok = (sem, self.dcnt[key])
        self.ops[eng].append((waits, lambda e: e.dma_start(out=out, in_=in_), sem, 16))
        self._mark(tok, reads, writes)
        return tok

    def wait_all(self, eng, toks):
        waits = self._waits(eng, toks)
        self.ops[eng].append((waits, None, None, 0))

    def replay(self, eng, e):
        for waits, fn, sem, inc in self.ops[eng]:
            for (s, v) in waits:
                e.wait_ge(s, v)
            if fn is not None:
                inst = fn(e)
                if sem is not None:
                    inst.then_inc(sem, inc)


def build_program(cfg):
    D, KC, PT, NS, TS, GC = cfg.D, cfg.KC, cfg.PT, cfg.NS, cfg.TS, cfg.GC
    BW, MB, NB, R = cfg.BW, cfg.MB, cfg.NB, cfg.R
    PG = D // 4
    nc = bass.Bass("TRN2", target_bir_lowering=False)

    def din(name, shape):
        return nc.dram_tensor(name, shape, F32, kind="ExternalInput").ap()

    def dout(name, shape):
        return nc.dram_tensor(name, shape, F32, kind="ExternalOutput").ap()

    xp = din("xp", [cfg.SEQ, D])
    xs = din("xs", [TS, D])
    sc = din("sc", [DEPTH, NS, CB, D])
    spl = din("spl", [DEPTH, NS, PB, D])
    meta = din("meta", [NMETA, D])
    norm_g = din("norm_g", [DEPTH, D])
    w_in = din("w_in", [DEPTH, 7 * D // BW, 128, KC * BW])
    conv_w = din("conv_w", [DEPTH, CWID, D])
    conv_b = din("conv_b", [DEPTH, D])
    ln_g = din("ln_g", [DEPTH, D])
    ln_b = din("ln_b", [DEPTH, D])
    w_co = din("w_co", [DEPTH, D // BW, 128, KC * BW])
    w_mix = din("w_mix", [DEPTH, 4, 128, GC * PG])
    pscale = din("pscale", [DEPTH, D])
    w_po = din("w_po", [DEPTH, D // BW, 128, KC * BW])
    w_o = din("w_o", [DEPTH, D // BW, 128, KC * BW])
    final_g = din("final_g", [1, D])
    yp = dout("yp", [cfg.SEQ, D])
    ys = dout("ys", [TS, D])
    ncp = dout("ncp", [DEPTH, CB, D])
    npp = dout("npp", [DEPTH, PB, D])
    ncs = dout("ncs", [DEPTH, NS, CB, D])
    nps = dout("nps", [DEPTH, NS, PB, D])

    def seg_tiles(p0, p1, hs):
        Tp_ = p1 - p0
        tl = []
        nt_ = cfg.ntile
        base_ = (Tp_ // nt_) // 2 * 2
        c_ = 0
        for i_ in range(nt_):
            c1_ = Tp_ if i_ == nt_ - 1 else c_ + base_
            tl.append((c_, c1_, c1_ - c_))
            c_ = c1_
        if hs:
            la, lb, lnp = tl[-1]
            if (lb - la) + TS <= 512 and cfg.merge_sample:
                tl[-1] = (la, lb + TS, lnp)
            else:
                tl.append((Tp_, Tp_ + TS, 0))
        return tl

    TW = max(b_ - a_ for sg_ in cfg.segs for (a_, b_, _) in seg_tiles(*sg_))
    TW = (TW + 7) // 8 * 8
    TPmax = max(p1 - p0 for (p0, p1, _) in cfg.segs)
    Tmax = max((p1 - p0) + (TS if hs else 0) for (p0, p1, hs) in cfg.segs)
    NMAX = 512
    EU = CB + TPmax
    EP = PB + TPmax
    EPS_ = max(EP, NS * (PB + DS))

    with ExitStack() as st:
        def sbuf(name, shape, dt):
            return st.enter_context(nc.sbuf_tensor(name, shape, dt))

        def sem(name):
            return st.enter_context(nc.semaphore(name))

        X = sbuf("X", [128, KC, Tmax], F32)
        H = sbuf("H", [128, KC, Tmax], BF16)
        CC = sbuf("CC", [128, KC, Tmax], BF16)
        QF = sbuf("QF", [128, KC, Tmax], BF16)
        MG = sbuf("MG", [128, KC, Tmax], BF16)
        U = [sbuf(f"U{i}", [128, EU], BF16) for i in range(2)]
        UX = sbuf("UX", [128, KC, NS, CB + DS], BF16)
        PX = sbuf("PX", [128, KC, NS, PB + DS], F32)
        PIN = [sbuf(f"PIN{i}", [128, EP], F32) for i in range(2)]
        if KC * Tmax // 2 >= 2 * EPS_:
            MGf = MG[:].rearrange("p k t -> p (k t)").bitcast(F32)
            TA = MGf[:, 0:EPS_]
            TB = MGf[:, EPS_:2 * EPS_]
        else:
            TA = sbuf("TA", [128, EPS_], F32)[:]
            TB = sbuf("TB", [128, EPS_], F32)[:]
        WR = cfg.WR
        W = [sbuf(f"W{i}", [128, KC, BW], BF16) for i in range(WR)]
        DG = [sbuf(f"DG{i}", [128, CWID, 128], BF16) for i in range(3)]
        A1 = sbuf("A1", [128, Tmax], F32)
        A2 = sbuf("A2", [128, Tmax], F32)
        S1 = MU = RR = A1
        S2 = RS = A2
        SG = [sbuf(f"SG{i}", [128, TW], F32) for i in range(4)]
        T1 = [sbuf(f"T1_{i}", [128, TW], F32) for i in range(4)]
        T2 = [sbuf(f"T2_{i}", [128, TW], F32) for i in range(4)]
        STG = [sbuf(f"STG{i}", [128, D], F32) for i in range(2)]
        PV = sbuf("PV", [128, KC, R], F32)
        ID32 = sbuf("ID32", [128, 128], F32)
        IDB = sbuf("IDB", [128, 128], BF16)
        ONESB = sbuf("ONESB", [128, 128], BF16)
        ONES32 = sbuf("ONES32", [128, 128], F32)
        EPSR = sbuf("EPSR", [128, 1], F32)
        EPSL = sbuf("EPSL", [128, 1], F32)
        INV = sbuf("INV", [128, PB], F32)
        UL = sbuf("UL", [128, KC, CB], F32)
        PL15 = sbuf("PL15", [128, KC, PB], F32)
        US = sbuf("US", [128, KC, TS], F32)
        PSN = sbuf("PSN", [128, KC, TS], F32)
        DUMMY = sbuf("DUMMY", [128, 2], F32)
        UT = [sbuf(f"UT{l}", [128, KC, CB], BF16) for l in range(DEPTH)]
        PTL = [sbuf(f"PTL{l}", [128, KC, PB], F32) for l in range(DEPTH)]
        PS = [st.enter_context(nc.psum_tensor(f"ps{i}", [128, NMAX], F32)) for i in range(8)]

        sems = {e: sem("s_" + e) for e in ENGS}
        wsem = [sem(f"w{i}") for i in range(WR)]
        stg_ld = [sem(f"stl{i}") for i in range(2)]
        stg_st = [sem(f"sts{i}") for i in range(2)]
        misc_sem = sem("misc")
        BST = [(STG[i][:], ("STG", i), stg_ld[i], stg_st[i]) for i in range(2)]
        if CWID * 128 * 2 >= D * 4:
            for i in range(3):
                dgf = DG[i][:].rearrange("p k c -> p (k c)").bitcast(F32)
                BST.append((dgf[:, 0:D], ("DG", i), sem(f"bld{i}"), sem(f"bst{i}")))
        NBST = len(BST)
        d2d_sem = sem("d2d")
        block = st.enter_context(nc.Block())
        P = Prog(sems)

        psi = [0]

        def bank():
            b = psi[0] % 8
            psi[0] += 1
            return b

        rot = {}

        def nxt(name, n):
            v = rot.get(name, 0)
            rot[name] = v + 1
            return v % n

        def act(out, in_, func, reads, writes, bias=None, scale=None):
            kw = {}
            if bias is not None:
                kw['bias'] = bias
            if scale is not None:
                kw['scale'] = scale
            return P.op('act', lambda e: e.activation(out=out, in_=in_, func=func, **kw), reads, writes)

        def tt(eng, out, in0, in1, op, reads, writes):
            return P.op(eng, lambda e: e.tensor_tensor(out=out, in0=in0, in1=in1, op=op), reads, writes)

        def stt(out, in0, scalar, in1, op0, op1, reads, writes):
            return P.op('dve', lambda e: e.scalar_tensor_tensor(out=out, in0=in0, scalar=scalar, in1=in1, op0=op0, op1=op1), reads, writes)

        def cp(eng, out, in_, reads, writes):
            return P.op(eng, lambda e: e.tensor_copy(out=out, in_=in_), reads, writes)

        def mset(eng, ap, val, writes):
            return P.op(eng, lambda e: e.memset(ap, val), (), writes)

        def mm_group(b, n, pairs, reads, c0=0):
            out = PS[b][:, c0:c0 + n]
            fns = []
            last = len(pairs) - 1
            for i, (l, r) in enumerate(pairs):
                fns.append(lambda e, l=l, r=r, i=i: e.matmul(out, lhsT=l, rhs=r, start=(i == 0), stop=(i == last)))
            return P.group('pe', fns, reads, [("ps", b)])

        def mm_chain(b, n, triples, c0=0):
            out = PS[b][:, c0:c0 + n]
            last = len(triples) - 1
            items = []
            for i, (l_, r_, rd_) in enumerate(triples):
                items.append((lambda e, l_=l_, r_=r_, i=i: e.matmul(out, lhsT=l_, rhs=r_, start=(i == 0), stop=(i == last)), rd_))
            return P.chain('pe', items, [("ps", b)])

        def transposes(b, items, reads):
            fns = []
            for (in_ap, off, r, c) in items:
                fns.append(lambda e, in_ap=in_ap, off=off, r=r, c=c: e.transpose(out=PS[b][0:c, off:off + r], in_=in_ap, identity=ID32[0:r, 0:r]))
            return P.group('pe', fns, reads + ["ID32"], [("ps", b)])

        mset('pool', ID32[:], 0.0, ["ID32"])
        P.op('pool', lambda e: e.affine_select(out=ID32[:], in_=ID32[:], pattern=[[-1, 128]], compare_op=ALU.not_equal,
                                               fill=1.0, base=0, channel_multiplier=1), ["ID32"], ["ID32"])
        cp('dve', IDB[:], ID32[:], ["ID32"], ["IDB"])
        mset('pool', ONESB[:], 1.0 / D, ["ONESB"])
        mset('pool', ONES32[:], 1.0 / D, ["ONES32"])
        mset('pool', EPSR[:], RMS_EPS, ["EPSR"])
        mset('pool', EPSL[:], LN_EPS, ["EPSL"])
        for t in range(PB):
            mset('pool', INV[:, t:t + 1], 1.0 / (t + 1), ["INV"])
        d2d = []
        VEC, vreg, vsem, _ = BST[-1]
        vregs = []

        def vload(dst, src):
            reg = ("VECR", len(vregs))
            vregs.append(reg)
            P.dma('sp', dst, src, vsem, (), [reg])
        for l in range(DEPTH):
            b0 = 36 * l
            vload(VEC[b0:b0 + CWID, :], conv_w[l])
            for j, src in enumerate([conv_b, ln_g, ln_b, norm_g, pscale]):
                vload(VEC[b0 + CWID + j:b0 + CWID + j + 1, :], src[l:l + 1, :])
        vload(VEC[R - 1:R, :], final_g)
        per_bank = max(1, NMAX // R)
        m = 0
        while m < KC:
            b = bank()
            ms = list(range(m, min(KC, m + per_bank)))
            transposes(b, [(VEC[0:R, mm * 128:(mm + 1) * 128], i * R, R, 128) for i, mm in enumerate(ms)], vregs + [vreg])
            for i, mm in enumerate(ms):
                cp('dve', PV[:, mm, :], PS[b][:, i * R:(i + 1) * R], [("ps", b)], ["PV"])
            m += per_bank

        def pv(l, row, mchunk):
            r = 36 * l + row
            return PV[:, mchunk, r:r + 1]

        KH = max(1, KC // 2)
        assert MB % GC == 0 and PG <= BW
        MPB = MB // GC

        def wsrc_in(l, c0):
            v = w_in[l, c0 // BW].rearrange("p (k e) -> p k e", k=KC)
            return lambda slot: [(W[slot][:, k0:k0 + KH, 0:BW], v[:, k0:k0 + KH, :]) for k0 in range(0, KC, KH)]

        def wsrc_sq(wt, l, c0):
            v = wt[l, c0 // BW].rearrange("p (k e) -> p k e", k=KC)
            return lambda slot: [(W[slot][:, k0:k0 + KH, 0:BW], v[:, k0:k0 + KH, :]) for k0 in range(0, KC, KH)]

        def wsrc_mixg(l, g):
            v = w_mix[l, g].rearrange("p (kk e) -> p kk e", kk=GC)
            return lambda slot: [(W[slot][:, 0:GC, 0:PG], v)]

        wlist = []
        for (p0, p1, hs) in cfg.segs:
            for l in range(DEPTH):
                for j in range(NB):
                    wlist.append(wsrc_in(l, 0 * D + j * BW))
                    wlist.append(wsrc_in(l, 1 * D + j * BW))
                    wlist.append(wsrc_in(l, 3 * D + j * BW))
                for j in range(NB):
                    wlist.append(wsrc_in(l, 2 * D + j * BW))
                    wlist.append(wsrc_in(l, 5 * D + j * BW))
                for j in range(NB):
                    wlist.append(wsrc_sq(w_co, l, j * BW))
                for j in range(NB):
                    wlist.append(wsrc_in(l, 4 * D + j * BW))
                    for gg in range(MPB):
                        wlist.append(wsrc_mixg(l, j * MPB + gg))
                for j in range(NB):
                    wlist.append(wsrc_in(l, 6 * D + j * BW))
                for j in range(NB):
                    wlist.append(wsrc_sq(w_po, l, j * BW))
                for j in range(NB):
                    wlist.append(wsrc_sq(w_o, l, j * BW))
        wstate = {'issued': 0, 'used': 0}
        nseg_ = len(cfg.segs)
        NBLK = len(wlist) // nseg_
        use_wsc = cfg.wscratch and nseg_ > 1
        if use_wsc:
            wsc = nc.dram_tensor("wsc", [NBLK, 128, KC * BW], BF16, kind="Internal").ap()
            wst = [sem(f"wst{i}") for i in range(WR)]
        def w_issue():
            i = wstate['issued']
            if i >= len(wlist):
                return
            slot = i % WR
            if use_wsc and i >= NBLK:
                P.dma('pool', W[slot][:].rearrange("p k e -> p (k e)"), wsc[i % NBLK], wsem[slot], [("WSC", i % NBLK)], [("W", slot)])
            else:
                for (dst_, src_) in wlist[i](slot):
                    P.dma('pool', dst_, src_, wsem[slot], (), [("W", slot)])
                if use_wsc:
                    P.dma('sp', wsc[i], W[slot][:].rearrange("p k e -> p (k e)"), wst[slot], [("W", slot)], [("WSC", i)])
            wstate['issued'] = i + 1

        def w_take():
            i = wstate['used']
            wstate['used'] = i + 1
            assert i < wstate['issued']
            return i % WR

        def w_release(n=1):
            for _ in range(n):
                w_issue()

        out_toks = []
        nseg = len(cfg.segs)
        NPF = 2
        bst_state = {'i': 0, 'reserved': set(), 'pre': []}

        def bst_next():
            while True:
                i_ = bst_state['i'] % NBST
                bst_state['i'] += 1
                if i_ not in bst_state['reserved']:
                    return i_

        def seg_blocks(p0_, p1_, hs_):
            out_ = []
            pos_ = p0_
            while pos_ < p1_:
                n_ = min(128, p1_ - pos_)
                srcs_ = []
                q_ = pos_
                while q_ < pos_ + n_:
                    if q_ < NMETA:
                        k_ = min(NMETA, pos_ + n_) - q_
                        srcs_.append((meta[q_:q_ + k_, :], k_))
                    else:
                        k_ = pos_ + n_ - q_
                        srcs_.append((xp[q_ - NMETA:q_ - NMETA + k_, :], k_))
                    q_ += k_
                out_.append((srcs_, n_, pos_ - p0_))
                pos_ += n_
            if hs_:
                for r0_ in range(0, TS, 128):
                    n_ = min(128, TS - r0_)
                    out_.append(([(xs[r0_:r0_ + n_, :], n_)], n_, (p1_ - p0_) + r0_))
            return out_

        def issue_row_dmas(rows_src):
            i_ = bst_next()
            st_ap, st_reg, st_ld, _ = BST[i_]
            r_ = 0
            for (ap_, k_) in rows_src:
                P.dma('sp', st_ap[r_:r_ + k_, :], ap_, st_ld, (), [st_reg])
                r_ += k_
            return i_
        try:
            chk(0)
            _emit_all = True
        except _Stop:
            _emit_all = False
        for si, (p0, p1, hs) in enumerate(cfg.segs if _emit_all else []):
          try:
              _cur['si'] = si
              Tp = p1 - p0
              T = Tp + (TS if hs else 0)
              tiles = seg_tiles(p0, p1, hs)
              nt = cfg.ntile
              for (a, b_, kd) in tiles:
                  assert b_ - a <= NMAX
              first_seg = (p0 == 0)
              last_prompt_seg = (p1 == PT)
              if last_prompt_seg:
                  assert tiles[nt - 1][2] >= CB

              def XR(ti):
                  return ("X", ti)

              def HRL(ti):
                  return [("H", k_, ti) for k_ in range(KC)]

              def load_rows(rows_src, n, c0, pre=None):
                  bi = issue_row_dmas(rows_src) if pre is None else pre
                  st_ap, st_reg, st_ld, _ = BST[bi]
                  bst_state['reserved'].discard(bi)
                  for h in range(0, KC, 4):
                      b = bank()
                      mm_ = list(range(h, min(KC, h + 4)))
                      transposes(b, [(st_ap[0:n, q * 128:(q + 1) * 128], i * 128, n, 128) for i, q in enumerate(mm_)], [st_reg])
                      wr = [("X", ti) for ti, (a, b2, kd) in enumerate(tiles) if not (b2 <= c0 or a >= c0 + n)]
                      act(X[:, h:h + len(mm_), c0:c0 + n], PS[b][:, 0:len(mm_) * 128].rearrange("p (a t) -> p a t", t=128)[:, :, 0:n],
                          AF.Copy, [("ps", b)], wr)

              pre_list = bst_state['pre']
              bst_state['pre'] = []
              for bi_, (srcs, n, c0_) in enumerate(seg_blocks(p0, p1, hs)):
                  load_rows(srcs, n, c0_, pre=(pre_list[bi_] if bi_ < len(pre_list) else None))

              chk(1)
              if si == 0:
                  for _ in range(WR):
                      w_issue()
                  for l_ in range(DEPTH):
                      out_toks.append(P.dma('sp', ncs[l_, :, 0:CB - DS, :], sc[l_, :, DS:CB, :], d2d_sem))
                      out_toks.append(P.dma('sp', nps[l_, :, 0:PB - DS, :], spl[l_, :, DS:PB, :], d2d_sem))

              def rms_stats(ti, a, b_):
                  n = b_ - a
                  bk = bank()
                  for mch in range(KC):
                      act(CC[:, mch, a:b_], X[:, mch, a:b_], AF.Square, [XR(ti)], [("CC", mch, ti)])
                  for mch in range(KC):
                      P.op('pe', lambda e, mch=mch, bk=bk, n=n: e.matmul(PS[bk][:, 0:n], lhsT=ONESB[:], rhs=CC[:, mch, a:b_],
                                                                       start=(mch == 0), stop=(mch == KC - 1)),
                           [("CC", mch, ti), "ONESB"], [("ps", bk)])
                  act(RR[:, a:b_], PS[bk][:, 0:n], AF.Ln, [("ps", bk), "EPSR"], [("A1", ti)], bias=EPSR[:, 0:1])
                  act(RR[:, a:b_], RR[:, a:b_], AF.Exp, [("A1", ti)], [("A1", ti)], scale=-0.5)

              for l in range(DEPTH):
                  for ti, (a, b_, kd) in enumerate(tiles):
                      rms_stats(ti, a, b_)
                      for mch in range(KC):
                          stt(H[:, mch, a:b_], X[:, mch, a:b_], pv(l, CWID + 3, mch), RR[:, a:b_], ALU.mult, ALU.mult,
                              [XR(ti), ("A1", ti), "PV"], [("H", mch, ti)])

                  chk(2)
                  if hs:
                      SPG = 128 // CB
                      for g0 in range(0, NS, SPG):
                          ng = min(SPG, NS - g0)
                          s = nxt("stg", 2)
                          P.dma('sp', STG[s][0:ng * CB, :], sc[l, g0:g0 + ng].rearrange("s j d -> (s j) d"), stg_ld[s], (), [("STG", s)])
                          for h in range(0, KC, 4):
                              b = bank()
                              mm_ = list(range(h, min(KC, h + 4)))
                              transposes(b, [(STG[s][0:ng * CB, q * 128:(q + 1) * 128], i * 128, ng * CB, 128) for i, q in enumerate(mm_)], [("STG", s)])
                              for i, q in enumerate(mm_):
                                  cp('dve', UX[:, q, g0:g0 + ng, 0:CB], PS[b][:, i * 128:i * 128 + ng * CB].rearrange("p (s j) -> p s j", j=CB),
                                     [("ps", b)], [("UX", q)])
                      SPG2 = 128 // PB
                      for g0 in range(0, NS, SPG2):
                          ng = min(SPG2, NS - g0)
                          s = nxt("stg", 2)
                          P.dma('sp', STG[s][0:ng * PB, :], spl[l, g0:g0 + ng].rearrange("s j d -> (s j) d"), stg_ld[s], (), [("STG", s)])
                          for h in range(0, KC, 4):
                              b = bank()
                              mm_ = list(range(h, min(KC, h + 4)))
                              transposes(b, [(STG[s][0:ng * PB, q * 128:(q + 1) * 128], i * 128, ng * PB, 128) for i, q in enumerate(mm_)], [("STG", s)])
                              for i, q in enumerate(mm_):
                                  cp('dve', PX[:, q, g0:g0 + ng, 0:PB], PS[b][:, i * 128:i * 128 + ng * PB].rearrange("p (s j) -> p s j", j=PB),
                                     [("ps", b)], [("PX", q)])

                  chk(20)
                  mset('dve', S1[:, 0:T], 0.0, [("A1", ti) for ti in range(len(tiles))])
                  mset('dve', S2[:, 0:T], 0.0, [("A2", ti) for ti in range(len(tiles))])
                  uslot_of = {}

                  def glu(mch, wv, wg, mloc):
                      us = mch % 2
                      uslot_of[mch] = us
                      ds_ = mch % 3
                      base_r = 36 * l
                      def dg_build(k0, k1):
                          nk = k1 - k0
                          P.op('dve', lambda e: e.tensor_tensor(out=DG[ds_][:, k0:k1, :], in0=IDB[:].unsqueeze(1).broadcast_to([128, nk, 128]),
                                                                in1=PV[:, mch, base_r + k0:base_r + k1].unsqueeze(2).broadcast_to([128, nk, 128]),
                                                                op=ALU.mult), ["IDB", "PV"], [("DG", ds_)])
                      dg_parts = [(0, 11), (11, 21), (21, CWID)]
                      dg_build(*dg_parts[0])
                      if first_seg:
                          mset('dve', U[us][:, 0:CB], 0.0, [("U", us)])
                      else:
                          cp('dve', U[us][:, 0:CB], UT[l][:, mch, :], [("UT", l, mch)], [("U", us)])
                      for ti, (a, b_, kd) in enumerate(tiles):
                          n = b_ - a
                          bg = bank()
                          mm_chain(bg, n, [(W[wg][:, k, mloc * 128:(mloc + 1) * 128], H[:, k, a:b_], [("W", wg), ("H", k, ti)]) for k in range(KC)])
                          bv = bank()
                          mm_chain(bv, n, [(W[wv][:, k, mloc * 128:(mloc + 1) * 128], H[:, k, a:b_], [("W", wv), ("H", k, ti)]) for k in range(KC)])
                          sg = nxt("sg", 4)
                          act(SG[sg][:, 0:n], PS[bg][:, 0:n], AF.Sigmoid, [("ps", bg)], [("SG", sg)])
                          if kd > 0:
                              tt('dve', U[us][:, CB + a:CB + a + kd], PS[bv][:, 0:kd], SG[sg][:, 0:kd], ALU.mult, [("ps", bv), ("SG", sg)], [("U", us)])
                              if last_prompt_seg and a + kd == Tp:
                                  tt('dve', UL[:, mch, :], PS[bv][:, kd - CB:kd], SG[sg][:, kd - CB:kd], ALU.mult, [("ps", bv), ("SG", sg)], [("UL", mch)])
                          if kd < n:
                              tt('dve', UX[:, mch, :, CB:CB + DS], PS[bv][:, kd:n].rearrange("p (s j) -> p s j", j=DS),
                                 SG[sg][:, kd:n].rearrange("p (s j) -> p s j", j=DS), ALU.mult, [("ps", bv), ("SG", sg)], [("UX", mch)])
                              tt('dve', US[:, mch, :], PS[bv][:, kd:n], SG[sg][:, kd:n], ALU.mult, [("ps", bv), ("SG", sg)], [("US", mch)])
                          if ti + 1 < len(dg_parts):
                              dg_build(*dg_parts[ti + 1])
                      for pi_ in range(len(tiles) + 1, len(dg_parts)):
                          dg_build(*dg_parts[pi_])
                      if not last_prompt_seg:
                          cp('dve', UT[l][:, mch, :], U[us][:, Tp:Tp + CB], [("U", us)], [("UT", l, mch)])
                      if mch == KC - 1:
                          act(DUMMY[:, 1:2], EPSR[:, 0:1], AF.Ln, ["EPSR"], ["DUMMY2"])

                  def conv(mch):
                      us = uslot_of[mch]
                      ds_ = mch % 3
                      for ti, (a, b_, kd) in enumerate(tiles):
                          n = b_ - a
                          bc = bank()
                          if kd > 0:
                              mm_group(bc, kd, [(DG[ds_][:, k, :], U[us][:, a + k:a + kd + k]) for k in range(CWID)], [("DG", ds_), ("U", us)])
                          if kd < n:
                              mm_group(bc, n - kd, [(DG[ds_][:, k, :], UX[:, mch, :, k:k + DS]) for k in range(CWID)], [("DG", ds_), ("UX", mch)], c0=kd)
                          cb_ap = pv(l, CWID + 0, mch)
                          act(CC[:, mch, a:b_], PS[bc][:, 0:n], AF.Identity, [("ps", bc), "PV"], [("CC", mch, ti)], bias=cb_ap)
                          cs = nxt("t2", 4)
                          act(T2[cs][:, 0:n], PS[bc][:, 0:n], AF.Square, [("ps", bc), "PV"], [("T2", cs)], bias=cb_ap)
                          tt('dve', S1[:, a:b_], S1[:, a:b_], CC[:, mch, a:b_], ALU.add, [("CC", mch, ti)], [("A1", ti)])
                          tt('dve', S2[:, a:b_], S2[:, a:b_], T2[cs][:, 0:n], ALU.add, [("T2", cs)], [("A2", ti)])

                  pool_first = [False]

                  def pin(mch, wp, mloc):
                      pslot = mch % 2
                      wwin = WINS[mch // GC]
                      if first_seg:
                          mset('dve', PIN[pslot][:, 0:PB], 0.0, [("PIN", pslot)])
                      else:
                          cp('dve', PIN[pslot][:, 0:PB], PTL[l][:, mch, :], [("PTL", l, mch)], [("PIN", pslot)])
                      for ti, (a, b_, kd) in enumerate(tiles):
                          n = b_ - a
                          bp = bank()
                          mm_group(bp, n, [(W[wp][:, k, mloc * 128:(mloc + 1) * 128], H[:, k, a:b_]) for k in range(KC)], [("W", wp)] + HRL(ti))
                          if kd > 0:
                              act(PIN[pslot][:, PB + a:PB + a + kd], PS[bp][:, 0:kd], AF.Copy, [("ps", bp)], [("PIN", pslot)])
                          if kd < n:
                              act(PX[:, mch, :, PB:PB + DS], PS[bp][:, kd:n].rearrange("p (s j) -> p s j", j=DS), AF.Copy, [("ps", bp)], [("PX", mch)])
                              act(PSN[:, mch, :], PS[bp][:, kd:n], AF.Copy, [("ps", bp)], [("PSN", mch)])
                      if not last_prompt_seg:
                          cp('dve', PTL[l][:, mch, :], PIN[pslot][:, Tp:Tp + PB], [("PIN", pslot)], [("PTL", l, mch)])
                      else:
                          cp('dve', PL15[:, mch, :], PIN[pslot][:, Tp:Tp + PB], [("PIN", pslot)], [("PL15", mch)])

                      def pool_region(xap_fn, E, rd_reg, outs):
                          cur = xap_fn
                          sh = 1
                          lvl = 0
                          cur_reg = rd_reg
                          while sh < wwin:
                              dst = TA if lvl % 2 == 0 else TB
                              dreg = "MGALL"
                              lo = 2 * sh - 1
                              wr_ = [dreg]
                              if not pool_first[0]:
                                  pool_first[0] = True
                                  wr_ = [dreg] + [("MG", k_, t_) for k_ in range(KC) for t_ in range(len(tiles))]
                              tt('dve', dst[:, lo:E], cur(lo, E), cur(lo - sh, E - sh), ALU.add, [cur_reg], wr_)
                              cur = (lambda d: (lambda lo_, hi_: d[:, lo_:hi_]))(dst)
                              cur_reg = dreg
                              sh *= 2
                              lvl += 1
                          outs(cur, cur_reg)

                      E = PB + Tp

                      def outs_p(cur, cur_reg, pslot=pslot, mch=mch, wwin=wwin):
                          for ti, (a, b_, kd) in enumerate(tiles):
                              if kd == 0:
                                  continue
                              stt(QF[:, mch, a:a + kd], cur(PB + a, PB + a + kd), 1.0 / wwin, PIN[pslot][:, PB + a:PB + a + kd], ALU.mult, ALU.subtract,
                                  [cur_reg, ("PIN", pslot)], [("QF", mch, ti)])
                          if first_seg:
                              nf = wwin - 1
                              tt('dve', T1[0][:, 0:nf], cur(PB, PB + nf), INV[:, 0:nf], ALU.mult, [cur_reg, "INV"], [("T1", 0)])
                              tt('dve', QF[:, mch, 0:nf], T1[0][:, 0:nf], PIN[pslot][:, PB:PB + nf], ALU.subtract, [("T1", 0), ("PIN", pslot)], [("QF", mch, 0)])
                      pool_region(lambda lo, hi, pslot=pslot: PIN[pslot][:, lo:hi], E, ("PIN", pslot), outs_p)
                      if hs:
                          E2 = NS * (PB + DS)
                          flat = PX[:, mch, :, :].rearrange("p s j -> p (s j)")

                          def outs_s(cur, cur_reg, mch=mch, wwin=wwin):
                              ti = len(tiles) - 1
                              a, b_, kd = tiles[ti]
                              cur3 = cur(0, E2).rearrange("p (s j) -> p s j", j=PB + DS)
                              stt(QF[:, mch, a + kd:b_].rearrange("p (s j) -> p s j", j=DS), cur3[:, :, PB:PB + DS], 1.0 / wwin, PX[:, mch, :, PB:PB + DS],
                                  ALU.mult, ALU.subtract, [cur_reg, ("PX", mch)], [("QF", mch, ti)])
                          pool_region(lambda lo, hi, flat=flat: flat[:, lo:hi], E2, ("PX", mch), outs_s)

                  wv = wg = wp = None
                  for mch in range(KC):
                      if mch % MB == 0:
                          wv = w_take()
                          wg = w_take()
                          wp = w_take()
                      glu(mch, wv, wg, mch % MB)
                      if mch >= 1:
                          conv(mch - 1)
                      pin(mch, wp, mch % MB)
                      if mch % MB == MB - 1:
                          w_release(3)
                  conv(KC - 1)

                  chk(3)
                  def out_rows_from_fm(src_fn, nrows, dst_fn, rd):
                      s = nxt("stg", 2)
                      for h in range(0, KC, 4):
                          b = bank()
                          mm_ = list(range(h, min(KC, h + 4)))
                          transposes(b, [(src_fn(q), i * 128, 128, nrows) for i, q in enumerate(mm_)], [r for q in mm_ for r in rd(q)])
                          act(STG[s][0:nrows, h * 128:(h + len(mm_)) * 128], PS[b][0:nrows, 0:len(mm_) * 128], AF.Copy, [("ps", b)], [("STG", s)])
                      return dst_fn(s)

                  if last_prompt_seg:
                      out_toks.append(out_rows_from_fm(lambda q: UL[:, q, :], CB,
                                                       lambda s: P.dma('sp', ncp[l], STG[s][0:CB, :], stg_st[s], [("STG", s)], ()),
                                                       lambda q: [("UL", q)]))
                  if hs:
                      for r0 in range(0, TS, 128):
                          n = min(128, TS - r0)

                          def dst(s, r0=r0, n=n):
                              tk = None
                              for sq_ in range(r0 // DS, (r0 + n) // DS):
                                  tk = P.dma('sp', ncs[l, sq_, CB - DS:CB, :], STG[s][(sq_ * DS - r0):(sq_ * DS - r0) + DS, :], stg_st[s], [("STG", s)], ())
                              return tk
                          out_toks.append(out_rows_from_fm(lambda q, r0=r0, n=n: US[:, q, r0:r0 + n], n, dst, lambda q: [("US", q)]))

                  chk(30)
                  for ti, (a, b_, kd) in enumerate(tiles):
                      n = b_ - a
                      b1 = bank()
                      P.op('pe', lambda e, b1=b1, a=a, b_=b_, n=n: e.matmul(PS[b1][:, 0:n], lhsT=ONES32[:], rhs=S1[:, a:b_], start=True, stop=True),
                           [("A1", ti), "ONES32"], [("ps", b1)])
                      b2 = bank()
                      P.op('pe', lambda e, b2=b2, a=a, b_=b_, n=n: e.matmul(PS[b2][:, 0:n], lhsT=ONES32[:], rhs=S2[:, a:b_], start=True, stop=True),
                           [("A2", ti), "ONES32"], [("ps", b2)])
                      cp('dve', MU[:, a:b_], PS[b1][:, 0:n], [("ps", b1)], [("A1", ti)])
                      tt('dve', RS[:, a:b_], MU[:, a:b_], MU[:, a:b_], ALU.mult, [("A1", ti)], [("A2", ti)])
                      tt('dve', RS[:, a:b_], PS[b2][:, 0:n], RS[:, a:b_], ALU.subtract, [("ps", b2), ("A2", ti)], [("A2", ti)])
                      act(RS[:, a:b_], RS[:, a:b_], AF.Ln, [("A2", ti), "EPSL"], [("A2", ti)], bias=EPSL[:, 0:1])
                      act(RS[:, a:b_], RS[:, a:b_], AF.Exp, [("A2", ti)], [("A2", ti)], scale=-0.5)
                  pend_fin = [None]
                  for blk in range(NB):
                      ws = w_take()
                      wa = w_take()
                      chs = list(range(blk * MB, (blk + 1) * MB))
                      if blk == NB - 1:
                          order = [(m_, t_) for t_ in range(len(tiles)) for m_ in chs]
                      else:
                          order = [(m_, t_) for m_ in chs for t_ in range(len(tiles))]
                      for (mch, ti) in order:
                          mloc = mch % MB
                          a, b_, kd = tiles[ti]
                          n = b_ - a
                          bs = bank()
                          mm_group(bs, n, [(W[ws][:, k, mloc * 128:(mloc + 1) * 128], H[:, k, a:b_]) for k in range(KC)], [("W", ws)] + HRL(ti))
                          bg = bank()
                          mm_group(bg, n, [(W[wa][:, k, mloc * 128:(mloc + 1) * 128], H[:, k, a:b_]) for k in range(KC)], [("W", wa)] + HRL(ti))
                          sg = nxt("sg", 4)
                          act(SG[sg][:, 0:n], PS[bs][:, 0:n], AF.Silu, [("ps", bs)], [("SG", sg)])
                          act(MG[:, mch, a:b_], PS[bg][:, 0:n], AF.Tanh, [("ps", bg)], [("MG", mch, ti), "MGALL"], scale=0.5)
                          t1 = nxt("t1", 4)
                          tt('dve', T1[t1][:, 0:n], CC[:, mch, a:b_], MU[:, a:b_], ALU.subtract, [("CC", mch, ti), ("A1", ti)], [("T1", t1)])
                          tt('dve', T1[t1][:, 0:n], T1[t1][:, 0:n], RS[:, a:b_], ALU.mult, [("T1", t1), ("A2", ti)], [("T1", t1)])
                          t2 = nxt("t2", 4)
                          act(T2[t2][:, 0:n], T1[t1][:, 0:n], AF.Silu, [("T1", t1), "PV"], [("T2", t2)], bias=pv(l, CWID + 2, mch), scale=pv(l, CWID + 1, mch))
                          if pend_fin[0] is not None:
                              pend_fin[0]()

                          def fin_(mch=mch, a=a, b_=b_, n=n, t2=t2, sg=sg, ti=ti):
                              tt('dve', CC[:, mch, a:b_], T2[t2][:, 0:n], SG[sg][:, 0:n], ALU.mult, [("T2", t2), ("SG", sg)], [("CC", mch, ti)])
                          pend_fin[0] = fin_
                      w_release(2)
                  pend_fin[0]()

                  chk(6)
                  if last_prompt_seg:
                      out_toks.append(out_rows_from_fm(lambda q: PL15[:, q, :], PB,
                                                       lambda s: P.dma('sp', npp[l], STG[s][0:PB, :], stg_st[s], [("STG", s)], ()),
                                                       lambda q: [("PL15", q)]))
                  if hs:
                      for r0 in range(0, TS, 128):
                          n = min(128, TS - r0)

                          def dst2(s, r0=r0, n=n):
                              tk = None
                              for sq_ in range(r0 // DS, (r0 + n) // DS):
                                  tk = P.dma('sp', nps[l, sq_, PB - DS:PB, :], STG[s][(sq_ * DS - r0):(sq_ * DS - r0) + DS, :], stg_st[s], [("STG", s)], ())
                              return tk
                          out_toks.append(out_rows_from_fm(lambda q, r0=r0, n=n: PSN[:, q, r0:r0 + n], n, dst2, lambda q: [("PSN", q)]))

                  chk(4)
                  wcs = [w_take() for _ in range(NB)]
                  for ti, (a, b_, kd) in enumerate(tiles):
                      n = b_ - a
                      for mch in range(KC):
                          wc = wcs[mch // MB]
                          mloc = mch % MB
                          ba = bank()
                          mm_group(ba, n, [(W[wc][:, k, mloc * 128:(mloc + 1) * 128], CC[:, k, a:b_]) for k in range(KC)],
                                   [("W", wc)] + [("CC", k, ti) for k in range(KC)])
                          stt(MG[:, mch, a:b_], MG[:, mch, a:b_], 1.0, PS[ba][:, 0:n], ALU.add, ALU.mult, [("ps", ba)], [("MG", mch, ti)])
                          if ti == len(tiles) - 1 and mch % MB == MB - 1:
                              w_release(1)

                  chk(60)
                  wms = None
                  wps = None
                  for mch in range(KC):
                      if mch % MB == 0:
                          wps = w_take()
                          wms = [w_take() for _ in range(MPB)]
                      mloc = mch % MB
                      g = mch // GC
                      wm = wms[mloc // GC]
                      for ti, (a, b_, kd) in enumerate(tiles):
                          n = b_ - a
                          bq = bank()
                          mm_group(bq, n, [(W[wm][:, kk, (mch % GC) * 128:(mch % GC + 1) * 128], QF[:, g * GC + kk, a:b_]) for kk in range(GC)],
                                   [("W", wm)] + [("QF", g * GC + kk, ti) for kk in range(GC)])
                          bs = bank()
                          mm_group(bs, n, [(W[wps][:, k, mloc * 128:(mloc + 1) * 128], H[:, k, a:b_]) for k in range(KC)], [("W", wps)] + HRL(ti))
                          sg = nxt("sg", 4)
                          act(SG[sg][:, 0:n], PS[bs][:, 0:n], AF.Silu, [("ps", bs)], [("SG", sg)])
                          stt(CC[:, mch, a:b_], PS[bq][:, 0:n], pv(l, CWID + 4, mch), SG[sg][:, 0:n], ALU.mult, ALU.mult,
                              [("ps", bq), ("SG", sg), "PV"], [("CC", mch, ti)])
                      if mch % MB == MB - 1:
                          w_release(1 + MPB)

                  chk(7)
                  wo_ = wb = None
                  for mch in range(KC):
                      if mch % MB == 0:
                          wb = w_take()
                      mloc = mch % MB
                      for ti, (a, b_, kd) in enumerate(tiles):
                          n = b_ - a
                          bg = bank()
                          mm_group(bg, n, [(W[wb][:, k, mloc * 128:(mloc + 1) * 128], H[:, k, a:b_]) for k in range(KC)], [("W", wb)] + HRL(ti))
                          act(QF[:, mch, a:b_], PS[bg][:, 0:n], AF.Sigmoid, [("ps", bg)], [("QF", mch, ti)])
                      if mch % MB == MB - 1:
                          w_release(1)
                  for mch in range(KC):
                      if mch % MB == 0:
                          wo_ = w_take()
                      mloc = mch % MB
                      for ti, (a, b_, kd) in enumerate(tiles):
                          n = b_ - a
                          bb = bank()
                          mm_group(bb, n, [(W[wo_][:, k, mloc * 128:(mloc + 1) * 128], CC[:, k, a:b_]) for k in range(KC)],
                                   [("W", wo_)] + [("CC", k, ti) for k in range(KC)])
                          t1 = nxt("t1", 4)
                          tt('dve', T1[t1][:, 0:n], PS[bb][:, 0:n], QF[:, mch, a:b_], ALU.mult, [("ps", bb), ("QF", mch, ti)], [("T1", t1)])
                          stt(MG[:, mch, a:b_], T1[t1][:, 0:n], 2.0, MG[:, mch, a:b_], ALU.mult, ALU.add, [("T1", t1)], [("MG", mch, ti)])
                      if mch % MB == MB - 1:
                          w_release(1)

                  chk(8)
                  wws = [w_take() for _ in range(NB)]
                  act(DUMMY[:, 0:1], EPSR[:, 0:1], AF.Ln, ["EPSR"], ["DUMMY"])
                  for ti, (a, b_, kd) in enumerate(tiles):
                      n = b_ - a
                      for mch in range(KC):
                          ww = wws[mch // MB]
                          mloc = mch % MB
                          bo = bank()
                          mm_group(bo, n, [(W[ww][:, k, mloc * 128:(mloc + 1) * 128], MG[:, k, a:b_]) for k in range(KC)],
                                   [("W", ww)] + [("MG", k, ti) for k in range(KC)])
                          stt(X[:, mch, a:b_], PS[bo][:, 0:n], 0.5, X[:, mch, a:b_], ALU.mult, ALU.add, [("ps", bo)], [XR(ti)])
                          if ti == len(tiles) - 1 and mch % MB == MB - 1:
                              w_release(1)

              chk(9)
              for ti, (a, b_, kd) in enumerate(tiles):
                  rms_stats(ti, a, b_)
                  for mch in range(KC):
                      stt(X[:, mch, a:b_], X[:, mch, a:b_], PV[:, mch, R - 1:R], RR[:, a:b_], ALU.mult, ALU.mult,
                          [XR(ti), ("A1", ti), "PV"], [XR(ti)])

              def xregs(c0, n):
                  return [("X", ti) for ti, (a, b2, kd) in enumerate(tiles) if not (b2 <= c0 or a >= c0 + n)]

              def store_rows(c0, n, dst_fn):
                  bst = BST[bst_next()]
                  st_ap, st_reg = bst[0], bst[1]
                  for h in range(0, KC, 4):
                      b = bank()
                      mm_ = list(range(h, min(KC, h + 4)))
                      transposes(b, [(X[:, q, c0:c0 + n], i * 128, 128, n) for i, q in enumerate(mm_)], xregs(c0, n))
                      act(st_ap[0:n, h * 128:(h + len(mm_)) * 128], PS[b][0:n, 0:len(mm_) * 128], AF.Copy, [("ps", b)], [st_reg])
                  return dst_fn(bst)

              if si + 1 < nseg and NBST > NPF + 1:
                  for (srcs_n, n_n, c0_n) in seg_blocks(*cfg.segs[si + 1])[:NPF]:
                      bi_n = issue_row_dmas(srcs_n)
                      bst_state['reserved'].add(bi_n)
                      bst_state['pre'].append(bi_n)
              pos = p0
              while pos < p1:
                  n = min(128, p1 - pos)
                  lo = max(pos, NMETA)
                  if lo < pos + n:
                      def dsty(bst, pos=pos, n=n, lo=lo):
                          return P.dma('sp', yp[lo - NMETA:pos + n - NMETA, :], bst[0][lo - pos:n, :], bst[3], [bst[1]], ())
                      out_toks.append(store_rows(pos - p0, n, dsty))
                  pos += n
              chk(10)
              if hs:
                  for r0 in range(0, TS, 128):
                      n = min(128, TS - r0)

                      def dsts(bst, r0=r0, n=n):
                          return P.dma('sp', ys[r0:r0 + n, :], bst[0][0:n, :], bst[3], [bst[1]], ())
                      out_toks.append(store_rows(Tp + r0, n, dsts))
          except _Stop:
            break

        finals = {}
        for tk in out_toks:
            if tk is None:
                continue
            k = tk[0].name
            tot = P.dcnt[k]
            finals[k] = (tk[0], tot)
        P.wait_all('sp', list(finals.values()))

        @block.sync
        def _(e):
            P.replay('sp', e)

        @block.gpsimd
        def _(e):
            P.replay('pool', e)

        @block.tensor
        def _(e):
            P.replay('pe', e)

        @block.scalar
        def _(e):
            P.replay('act', e)

        @block.vector
        def _(e):
            P.replay('dve', e)
    return nc


def make_in_maps(cfg, ncores, x_prompt, x_sample, state_conv, state_pool, meta_tokens, norm_g, w_in, conv_w, conv_b,
                 ln_g, ln_b, w_conv_out, w_pool_mix, pool_scale, w_pool_out, w_out, final_g):
    f = lambda a: np.ascontiguousarray(np.asarray(a, dtype=np.float32))
    NS, D = cfg.NS, cfg.D
    KC, BW, GC = cfg.KC, cfg.BW, cfg.GC
    PG = D // 4

    def pack_blocks(w):
        w = f(w)
        L, _, C = w.shape
        return np.ascontiguousarray(w.reshape(L, KC, 128, C // BW, BW).transpose(0, 3, 2, 1, 4)).reshape(L, C // BW, 128, KC * BW)

    def pack_mix(w):
        w = f(w)
        L = w.shape[0]
        return np.ascontiguousarray(w.reshape(L, 4, GC, 128, PG).transpose(0, 1, 3, 2, 4)).reshape(L, 4, 128, GC * PG)

    shared = dict(meta=f(meta_tokens), norm_g=f(norm_g), w_in=pack_blocks(w_in), conv_w=f(conv_w), conv_b=f(conv_b), ln_g=f(ln_g),
                  ln_b=f(ln_b), w_co=pack_blocks(w_conv_out), w_mix=pack_mix(w_pool_mix), pscale=f(pool_scale),
                  w_po=pack_blocks(w_pool_out), w_o=pack_blocks(w_out), final_g=f(final_g).reshape(1, D))
    x_prompt = np.asarray(x_prompt); x_sample = np.asarray(x_sample)
    state_conv = np.asarray(state_conv); state_pool = np.asarray(state_pool)
    maps = []
    for c in range(ncores):
        m = dict(shared)
        m["xp"] = f(x_prompt[c])
        m["xs"] = f(x_sample[c * NS:(c + 1) * NS]).reshape(NS * DS, D)
        m["sc"] = f(state_conv[:, c * NS:(c + 1) * NS])
        m["spl"] = f(state_pool[:, c * NS:(c + 1) * NS])
        maps.append(m)
    return maps


def gather(cfg, results):
    NS, D = cfg.NS, cfg.D
    y_prompt = np.stack([np.asarray(r["yp"]) for r in results], axis=0).astype(np.float32)
    y_sample = np.concatenate([np.asarray(r["ys"]).reshape(NS, DS, D) for r in results], axis=0).astype(np.float32)
    ncp = np.stack([np.asarray(r["ncp"]) for r in results], axis=1).astype(np.float32)
    npp = np.stack([np.asarray(r["npp"]) for r in results], axis=1).astype(np.float32)
    ncs = np.concatenate([np.asarray(r["ncs"]) for r in results], axis=1).astype(np.float32)
    nps = np.concatenate([np.asarray(r["nps"]) for r in results], axis=1).astype(np.float32)
    return (y_prompt, y_sample, ncp, npp, ncs, nps)


def kernel(**inputs):
    cfg = REAL
    nc = build_program(cfg)
    maps = make_in_maps(cfg, 8, **inputs)
    res = run_bass_kernel_spmd(nc, maps, core_ids=list(range(8)))
    return gather(cfg, res.results)
```
